# Optimizing a Trainium2 kernel written in Bass

```python
import jax, jax.numpy as jnp
from jax import lax
import numpy as np

D_MODEL = 2048
BATCH = 8
SEQ = 2048
DEPTH = 1

RET_HEADS = 4
RET_HEAD_DIM = 256
RET_WIDTH = RET_HEADS * RET_HEAD_DIM
RET_CHUNK = 128
RET_ROPE_THETA = 10000.0
MOBA_HEADS = 8
MOBA_HEAD_DIM = 128
MOBA_WIDTH = MOBA_HEADS * MOBA_HEAD_DIM
MOBA_BLOCK = 256
MOBA_TOPK = 3
MOBA_Q_CHUNK = 64
ROPE_THETA = 500000.0
ROT_DIM = MOBA_HEAD_DIM // 4
D_FF = 5632
CONV_WIDTH = 3
EPS = 1e-6
IN_SIZES = (RET_WIDTH, RET_WIDTH, RET_WIDTH, RET_WIDTH,
            MOBA_WIDTH, MOBA_WIDTH, MOBA_WIDTH,
            D_MODEL, D_MODEL)
IN_TOTAL = sum(IN_SIZES)

kernel_name = "hybrid_retention_moba_convffn"


def rmsnorm(x, w):
    xf = x.astype(jnp.float32)
    y = xf * lax.rsqrt(jnp.mean(xf * xf, axis=-1, keepdims=True) + EPS)
    return (y * w.astype(jnp.float32)).astype(x.dtype)


def rotary(x, rot_dim, theta):
    S = x.shape[1]
    pos = jnp.arange(S, dtype=jnp.float32)
    inv_freq = jnp.asarray(theta, jnp.float32) ** (-jnp.arange(0, rot_dim, 2, dtype=jnp.float32) / rot_dim)
    ang = pos[:, None] * inv_freq[None, :]
    cos = jnp.cos(ang)[None, :, None, :]
    sin = jnp.sin(ang)[None, :, None, :]
    xr = x[..., :rot_dim].astype(jnp.float32)
    x1, x2 = jnp.split(xr, 2, axis=-1)
    rot = jnp.concatenate([x1 * cos - x2 * sin, x2 * cos + x1 * sin], axis=-1).astype(x.dtype)
    return jnp.concatenate([rot, x[..., rot_dim:]], axis=-1)


def retention(q, k, v):
    B, S, H, dk = q.shape
    dv = v.shape[-1]
    C = RET_CHUNK
    N = S // C
    q = q.astype(jnp.float32)
    k = k.astype(jnp.float32) * (dk ** -0.5)
    v = v.astype(jnp.float32)
    log_g = jnp.log1p(-jnp.exp2(-5.0 - jnp.arange(H, dtype=jnp.float32)))
    i = jnp.arange(C, dtype=jnp.float32)
    diff = i[:, None] - i[None, :]
    causal = diff >= 0
    decay = jnp.where(causal[None], jnp.exp(log_g[:, None, None] * jnp.where(causal, diff, 0.0)[None]), 0.0)
    xi = jnp.exp(log_g[:, None] * (i[None, :] + 1.0))
    zeta = jnp.exp(log_g[:, None] * (C - 1.0 - i[None, :]))
    g_chunk = jnp.exp(log_g * C)

    def to_chunks(t):
        return t.reshape(B, N, C, H, t.shape[-1]).transpose(1, 0, 3, 2, 4)

    def step(state, xs):
        qc, kc, vc = xs
        scores = jnp.einsum('bhqd,bhkd->bhqk', qc, kc) * decay[None]
        inner = jnp.einsum('bhqk,bhkd->bhqd', scores, vc)
        cross = jnp.einsum('bhqd,bhde->bhqe', qc, state) * xi[None, :, :, None]
        new_state = state * g_chunk[None, :, None, None] + jnp.einsum(
            'bhkd,bhke->bhde', kc * zeta[None, :, :, None], vc)
        return new_state, inner + cross

    init = jnp.zeros((B, H, dk, dv), jnp.float32)
    _, out = lax.scan(step, init, (to_chunks(q), to_chunks(k), to_chunks(v)))
    return out.transpose(1, 0, 3, 2, 4).reshape(B, S, H, dv)


def moba_attention(q, k, v):
    B, S, H, d = q.shape
    dtype = q.dtype
    BS = MOBA_BLOCK
    Cq = MOBA_Q_CHUNK
    NB = -(-S // BS)
    S_pad = NB * BS
    NQ = S // Cq
    topk = min(MOBA_TOPK, NB)
    scale = d ** -0.5
    qh = q.transpose(0, 2, 1, 3)
    pad = ((0, 0), (0, 0), (0, S_pad - S), (0, 0))
    kb = jnp.pad(k.transpose(0, 2, 1, 3), pad).reshape(B, H, NB, BS, d)
    vb = jnp.pad(v.transpose(0, 2, 1, 3), pad).reshape(B, H, NB, BS, d)
    kmean = jnp.mean(kb.astype(jnp.float32), axis=3)
    q_all = qh.reshape(B, H, NQ, Cq, d).transpose(0, 2, 1, 3, 4).reshape(B * NQ, H, Cq, d)
    b_idx = jnp.repeat(jnp.arange(B, dtype=jnp.int32), NQ)
    n_idx = jnp.tile(jnp.arange(NQ, dtype=jnp.int32), B)
    hh = jnp.arange(H)[:, None, None]

    def step(args):
        qc, b, n = args
        kb_b, vb_b, km_b = kb[b], vb[b], kmean[b]
        q0 = n * Cq
        cb = q0 // BS
        qpos = q0 + jnp.arange(Cq)
        bscore = jnp.einsum('hqd,hnd->hqn', qc.astype(jnp.float32), km_b)
        past = jnp.arange(NB) < cb
        bscore = jnp.where(past[None, None, :], bscore, -jnp.inf)
        _, idx = lax.top_k(bscore, topk)
        valid = idx < cb
        k_sel = kb_b[hh, idx]
        v_sel = vb_b[hh, idx]
        s_sel = jnp.einsum('hqd,hqnkd->hqnk', qc, k_sel, preferred_element_type=jnp.float32) * scale
        s_sel = jnp.where(valid[..., None], s_sel, -jnp.inf).reshape(H, Cq, topk * BS)
        k_own = lax.dynamic_index_in_dim(kb_b, cb, axis=1, keepdims=False)
        v_own = lax.dynamic_index_in_dim(vb_b, cb, axis=1, keepdims=False)
        s_own = jnp.einsum('hqd,hkd->hqk', qc, k_own, preferred_element_type=jnp.float32) * scale
        kpos = cb * BS + jnp.arange(BS)
        s_own = jnp.where((kpos[None, :] <= qpos[:, None])[None], s_own, -jnp.inf)
        p = jax.nn.softmax(jnp.concatenate([s_sel, s_own], axis=-1), axis=-1).astype(dtype)
        p_sel = p[..., :topk * BS].reshape(H, Cq, topk, BS)
        p_own = p[..., topk * BS:]
        o = jnp.einsum('hqnk,hqnkd->hqd', p_sel, v_sel) + jnp.einsum('hqk,hkd->hqd', p_own, v_own)
        return o.astype(dtype)

    out = lax.map(step, (q_all, b_idx, n_idx))
    return out.reshape(B, NQ, H, Cq, d).transpose(0, 1, 3, 2, 4).reshape(B, S, H * d)


def causal_dwconv(u, w, b):
    S = u.shape[1]
    up = jnp.pad(u, ((0, 0), (CONV_WIDTH - 1, 0), (0, 0)))
    y = b
    for j in range(CONV_WIDTH):
        y = y + w[j] * up[:, j:j + S]
    return y


def setup_inputs(seed: int = 0) -> dict:
    key = jax.random.key(seed)
    ks = jax.random.split(key, 14)
    f32 = jnp.float32

    def normal(k, shape, scale):
        return jax.random.normal(k, shape, f32) * scale

    def gain(k, shape):
        return 1.0 + 0.05 * jax.random.normal(k, shape, f32)

    return {
        "x": jax.random.normal(ks[0], (BATCH, SEQ, D_MODEL), f32),
        "attn_norm_w": gain(ks[1], (DEPTH, D_MODEL)),
        "w_in": normal(ks[2], (DEPTH, D_MODEL, IN_TOTAL), D_MODEL ** -0.5),
        "ret_norm_w": gain(ks[3], (DEPTH, RET_WIDTH)),
        "w_ret_up": normal(ks[4], (DEPTH, RET_WIDTH, D_MODEL), RET_WIDTH ** -0.5),
        "w_moba_up": normal(ks[5], (DEPTH, MOBA_WIDTH, D_MODEL), MOBA_WIDTH ** -0.5),
        "w_out": normal(ks[6], (DEPTH, D_MODEL, D_MODEL), D_MODEL ** -0.5),
        "ffn_norm_w": gain(ks[7], (DEPTH, D_MODEL)),
        "w_ffn_up": normal(ks[8], (DEPTH, D_MODEL, 2 * D_FF), D_MODEL ** -0.5),
        "conv_w": normal(ks[9], (DEPTH, CONV_WIDTH, 2 * D_FF), CONV_WIDTH ** -0.5),
        "conv_b": normal(ks[10], (DEPTH, 2 * D_FF), 0.01),
        "w_ffn_down": normal(ks[11], (DEPTH, D_FF, D_MODEL), D_FF ** -0.5),
        "final_norm_w": gain(ks[12], (D_MODEL,)),
    }


def reference(x, attn_norm_w, w_in, ret_norm_w, w_ret_up, w_moba_up, w_out,
              ffn_norm_w, w_ffn_up, conv_w, conv_b, w_ffn_down, final_norm_w):
    B, S, _ = x.shape
    split_at = [int(s) for s in np.cumsum(IN_SIZES)[:-1]]
    for l in range(DEPTH):
        h = rmsnorm(x, attn_norm_w[l])
        proj = h @ w_in[l]
        rq, rk, rv, rg, mq, mk, mv, g_ret, g_moba = jnp.split(proj, split_at, axis=-1)
        rq = rotary(rq.reshape(B, S, RET_HEADS, RET_HEAD_DIM), RET_HEAD_DIM, RET_ROPE_THETA)
        rk = rotary(rk.reshape(B, S, RET_HEADS, RET_HEAD_DIM), RET_HEAD_DIM, RET_ROPE_THETA)
        rv = rv.reshape(B, S, RET_HEADS, RET_HEAD_DIM)
        o_ret = retention(rq, rk, rv)
        o_ret = o_ret * lax.rsqrt(jnp.mean(o_ret * o_ret, axis=-1, keepdims=True) + EPS)
        o_ret = o_ret.reshape(B, S, RET_WIDTH).astype(x.dtype) * ret_norm_w[l]
        o_ret = jax.nn.silu(rg) * o_ret
        mq = rotary(mq.reshape(B, S, MOBA_HEADS, MOBA_HEAD_DIM), ROT_DIM, ROPE_THETA)
        mk = rotary(mk.reshape(B, S, MOBA_HEADS, MOBA_HEAD_DIM), ROT_DIM, ROPE_THETA)
        mv = mv.reshape(B, S, MOBA_HEADS, MOBA_HEAD_DIM)
        o_moba = moba_attention(mq, mk, mv)
        merged = jax.nn.sigmoid(g_ret) * (o_ret @ w_ret_up[l]) + jax.nn.sigmoid(g_moba) * (o_moba @ w_moba_up[l])
        x = x + merged @ w_out[l]
        h2 = rmsnorm(x, ffn_norm_w[l])
        up = causal_dwconv(h2 @ w_ffn_up[l], conv_w[l], conv_b[l])
        a, b = jnp.split(up, 2, axis=-1)
        x = x + (jax.nn.silu(a) * b) @ w_ffn_down[l]
    return rmsnorm(x, final_norm_w)
```

```python
import math
from contextlib import ExitStack

import numpy as np
import ml_dtypes
import concourse.bass as bass
import concourse.mybir as mybir
from concourse.bass_utils import run_bass_kernel_spmd

F32 = mybir.dt.float32
BF16 = mybir.dt.bfloat16
AF = mybir.ActivationFunctionType
ALU = mybir.AluOpType
AX = mybir.AxisListType

S = 2048
D = 2048
NT = 16
KC = 16
DFF = 5632
NFF = 44
IN_TOTAL = 11264
OFF_RQ, OFF_RK, OFF_RV, OFF_RG, OFF_MQ, OFF_MK, OFF_MV, OFF_GR, OFF_GM = (
    0, 1024, 2048, 3072, 4096, 5120, 6144, 7168, 9216)
EPS = 1e-6
NEG = -30000.0


class Buf:
    __slots__ = ("name", "w", "r", "dsem", "dcount", "excl")

    def __init__(self, name):
        self.name = name
        self.excl = False
        self.w = {}
        self.r = {}
        self.dsem = None
        self.dcount = 0


class Eng:
    def __init__(self, name, eng, sem):
        self.name = name
        self.eng = eng
        self.sem = sem
        self.seq = 0
        self.waited = {}
        self.nwaits = 0
        self.ninst = 0


class FW:
    def __init__(self, nc, stack):
        self.nc = nc
        self.stack = stack
        self.E = {}
        for name, eng in (("pe", nc.tensor), ("act", nc.scalar), ("dve", nc.vector),
                          ("pool", nc.gpsimd), ("sp", nc.sync)):
            sem = stack.enter_context(nc.semaphore("sem_" + name))
            self.E[name] = Eng(name, eng, sem)
        self.bufs = []
        self.free_dsems = []
        self.nsem = 5

    def buf(self, name):
        b = Buf(name)
        self.bufs.append(b)
        return b

    def bufs_n(self, name, n):
        return [self.buf(f"{name}{i}") for i in range(n)]

    def sb(self, st, name, shape, dtype):
        return st.enter_context(self.nc.sbuf_tensor(name, list(shape), dtype))

    def _wait(self, e, tok):
        sem, val = tok
        k = id(sem)
        if e.waited.get(k, 0) >= val:
            return
        e.eng.wait_ge(sem, val)
        e.waited[k] = val
        e.nwaits += 1

    def _deps(self, e, reads, writes, partial):
        toks = []
        for b in reads:
            toks.extend(b.w.values())
            if b.excl:
                toks.extend(t for t in b.r.values() if t[0] is not e.sem)
        for b in writes:
            if not partial:
                toks.extend(b.w.values())
            toks.extend(b.r.values())
        for tok in toks:
            if e.name == "pe" and tok[0] is e.sem:
                continue
            self._wait(e, tok)

    def _record(self, tok, reads, writes, partial):
        k = id(tok[0])
        for b in reads:
            old = b.r.get(k)
            if old is None or old[1] < tok[1]:
                b.r[k] = tok
        for b in writes:
            if not partial:
                b.w = {}
                b.r = {}
            b.w[k] = tok

    def op(self, en, fn, reads=(), writes=(), signal=True, partial=False):
        e = self.E[en]
        self._deps(e, reads, writes, partial)
        inst = fn(e.eng)
        e.ninst += 1
        if signal:
            e.seq += 1
            inst.then_inc(e.sem, 1)
            tok = (e.sem, e.seq)
        else:
            tok = (e.sem, e.seq + 1)
        self._record(tok, reads, writes, partial)
        return inst

    def dma(self, qn, out, in_, reads=(), writes=(), partial=False, **kw):
        e = self.E[qn]
        (dst,) = writes
        self._deps(e, reads, writes, partial)
        if dst.dsem is None:
            dst.dsem = self.stack.enter_context(self.nc.semaphore("dsem_" + dst.name))
            self.nsem += 1
        inst = e.eng.dma_start(out=out, in_=in_, **kw)
        dst.dcount += 1
        inst.then_inc(dst.dsem, 16)
        tok = (dst.dsem, 16 * dst.dcount)
        e.ninst += 1
        self._record(tok, reads, writes, partial)
        return inst

    def wait_buf(self, en, b):
        for tok in list(b.w.values()):
            self._wait(self.E[en], tok)

    def barrier(self):
        toks = {}
        for e in self.E.values():
            if e.seq > 0:
                toks[id(e.sem)] = (e.sem, e.seq)
        for b in self.bufs:
            for d in (b.w, b.r):
                for k, t in d.items():
                    if k not in toks or toks[k][1] < t[1]:
                        toks[k] = t
        for e in self.E.values():
            for t in toks.values():
                if t[0] is e.sem:
                    continue
                self._wait(e, t)


def _consts():
    c = {}
    bf = ml_dtypes.bfloat16
    c["c_ident"] = np.eye(128, dtype=np.float32).astype(bf)
    c["c_ones"] = np.ones((128, 128), np.float32).astype(bf)
    ind = np.zeros((128, 8, 128), np.float32)
    for n in range(8):
        ind[n, n, :] = 1.0
    c["c_ind"] = ind.astype(bf)
    cm = np.zeros((128, 4, 512), np.float32)
    kk = np.arange(128)[:, None]
    qq = np.arange(512)[None, :]
    for r in range(4):
        cm[:, r, :] = np.where(qq >= r * 128 + kk, 0.0, NEG)
    c["c_cm"] = cm.astype(bf)
    pos = np.arange(S, dtype=np.float64)
    inv = 500000.0 ** (-np.arange(0, 32, 2, dtype=np.float64) / 32.0)
    ang = (pos[None, :].astype(np.float32) * inv[:, None].astype(np.float32)).astype(np.float32).astype(np.float64)
    c["c_mcos"] = np.concatenate([np.cos(ang), np.cos(ang)], 0).astype(np.float32)
    c["c_msin"] = np.concatenate([-np.sin(ang), np.sin(ang)], 0).astype(np.float32)
    pastneg = np.zeros((128, 16, 8), np.float32)
    past01 = np.zeros((128, 16, 8), np.float32)
    own01 = np.zeros((128, 16, 8), np.float32)
    for t in range(16):
        cb = t // 2
        for n in range(8):
            if n < cb:
                past01[:, t, n] = 1.0
            else:
                pastneg[:, t, n] = -1e30
            if n == cb:
                own01[:, t, n] = 1.0
    c["c_pastneg"] = pastneg
    c["c_past01"] = past01
    c["c_own01"] = own01
    inv_r0 = 10000.0 ** (-np.arange(0, 256, 2, dtype=np.float64) / 256.0)
    ang_r0 = (pos[None, :].astype(np.float32) * inv_r0[:, None].astype(np.float32)).astype(np.float32).astype(np.float64)
    c["c_rcos"] = np.cos(ang_r0).astype(np.float32)
    c["c_rsin"] = np.sin(ang_r0).astype(np.float32)
    rdt = np.zeros((128, 4, 128), np.float64)
    rzeta = np.zeros((128, 4), np.float64)
    repsq = np.zeros((128, 4, 512), np.float64)
    kk_ = np.arange(128, dtype=np.float64)
    for h in range(4):
        lg = np.log1p(-np.exp2(-5.0 - h))
        causal = (kk_[:, None] <= kk_[None, :])
        rdt[:, h, :] = np.where(causal, np.exp(-lg * (kk_[:, None] + 1.0)) / 16.0, 0.0)
        rzeta[:, h] = np.exp(lg * (127.0 - kk_)) / 16.0
        eq = EPS / np.exp(2.0 * lg * (kk_ + 1.0))
        repsq[:, h, :] = np.tile(eq, 4)[None, :]
    c["c_rdt"] = rdt.astype(np.float32)
    c["c_rzeta"] = rzeta.astype(np.float32)
    c["c_repsq"] = repsq.astype(np.float32)
    inv_r = 10000.0 ** (-np.arange(0, 256, 2, dtype=np.float64) / 256.0)
    ang_r = (pos[None, :].astype(np.float32) * inv_r[:, None].astype(np.float32)).astype(np.float32).astype(np.float64)
    cosr, sinr = np.cos(ang_r), np.sin(ang_r)
    tl = (np.arange(S) % 128).astype(np.float64)
    rt = np.zeros((4, 4, 128, S), np.float32)
    gch = np.zeros((4,), np.float64)
    for h in range(4):
        lg = np.log1p(-np.exp2(-5.0 - h))
        xi = np.exp(lg * (tl + 1.0))
        kz = np.exp(-lg * (tl + 1.0)) / 16.0
        rt[h, 0] = cosr * xi[None, :]
        rt[h, 1] = sinr * xi[None, :]
        rt[h, 2] = cosr * kz[None, :]
        rt[h, 3] = sinr * kz[None, :]
        gch[h] = np.exp(lg * 128.0)
    return c, gch


_CONST_CACHE = {}
LASTFW = [None]


def _get_consts():
    if "c" not in _CONST_CACHE:
        _CONST_CACHE["c"] = _consts()
    return _CONST_CACHE["c"]


def build_nc(stage=99, debug=False):
    consts, gch = _get_consts()
    nc = bass.Bass("TRN2", target_bir_lowering=False)
    dram = {}

    def din(name, shape, dt=F32):
        dram[name] = nc.dram_tensor(name, list(shape), dt, kind="ExternalInput").ap()
        return dram[name]

    x = din("x", [S, D])
    w_in = din("w_in", [D, IN_TOTAL])
    small = debug and stage <= 3
    w_ret_up = None if small else din("w_ret_up", [1024, D])
    w_moba_up = None if small else din("w_moba_up", [1024, D])
    w_out = None if small else din("w_out", [D, D])
    w_ffn_up = None if small else din("w_ffn_up", [D, 2 * DFF])
    w_ffn_down = None if small else din("w_ffn_down", [DFF, D])
    attn_nw = din("attn_norm_w", [1, D])
    ffn_nw = din("ffn_norm_w", [1, D])
    fin_nw = din("final_norm_w", [1, D])
    retw_t = din("retw_t", [128, 8])
    convw_t = din("convw_t", [128, 3 * 88])
    convb_t = din("convb_t", [128, 88])
    cd = {}
    for k, v in consts.items():
        cd[k] = din(k, v.shape, BF16 if v.dtype == ml_dtypes.bfloat16 else F32)
    out = nc.dram_tensor("out", [S, D], F32, kind="ExternalOutput").ap()
    dbg = {}

    def dout(name, shape, dt=F32):
        dbg[name] = nc.dram_tensor(name, list(shape), dt, kind="ExternalOutput").ap()
        return dbg[name]

    def dscr(name, shape, dt):
        return nc.dram_tensor(name, list(shape), dt, kind="Internal").ap()

    OM = dscr("scr_om", [1024, S], BF16)
    OR = dscr("scr_or", [1024, S], BF16)
    MG = dscr("scr_mg", [NT, 128, KC * 128], BF16)
    X2 = dscr("scr_x2", [S, D], F32)
    GT = dscr("scr_gt", [NT, 128, NFF * 128], BF16)
    X3 = dscr("scr_x3", [S, D], F32)

    with ExitStack() as top:
        fw = FW(nc, top)
        PS = [top.enter_context(nc.psum_tensor(f"ps{i}", [128, 512], F32)) for i in range(8)]
        bPS = fw.bufs_n("ps", 8)
        for b_ in bPS:
            b_.excl = True
        ident = fw.sb(top, "ident", [128, 128], BF16)
        ones = fw.sb(top, "ones", [128, 128], BF16)
        epst = fw.sb(top, "epst", [128, 1], F32)
        b_const = fw.buf("const")
        fw.dma("sp", ident[:], cd["c_ident"], writes=[b_const], partial=True)
        fw.dma("sp", ones[:], cd["c_ones"], writes=[b_const], partial=True)
        b_eps = fw.buf("eps")
        fw.op("dve", lambda e: e.memset(epst[:], EPS), writes=[b_eps])

        NW = 8
        wring = [fw.sb(top, f"wring{i}", [128, KC, 128], BF16) for i in range(NW)]
        bW = fw.bufs_n("wring", NW)
        wstate = {"n": 0}

        def wload(src_ap, col0, nk=KC):
            i = wstate["n"] % NW
            wstate["n"] += 1
            srcv = src_ap[0:nk * 128, col0:col0 + 128].rearrange("(kc p) n -> p kc n", p=128)
            fw.dma("pool", wring[i][:, 0:nk, :], srcv, writes=[bW[i]])
            return wring[i], bW[i]

        class WStream:
            def __init__(self, specs, group=1):
                self.specs = specs
                self.group = group
                self.issued = 0
                self.tiles = []

            def get(self, i, group=None, base=None):
                g = self.group if group is None else group
                b = i if base is None else base
                while self.issued < len(self.specs) and self.issued <= b + NW - g:
                    self.tiles.append(wload(*self.specs[self.issued]))
                    self.issued += 1
                return self.tiles[i]

        mspecs = []
        for h in range(8):
            mspecs += [(w_in, OFF_MQ + h * 128, KC), (w_in, OFF_MK + h * 128, KC), (w_in, OFF_MV + h * 128, KC)]
        rspecs = []
        for h in range(4):
            for off in (OFF_RQ, OFF_RK, OFF_RV, OFF_RG):
                for c in range(2):
                    rspecs.append((w_in, off + h * 256 + c * 128, KC))
        gspecs = []
        if w_ret_up is not None:
            for c in range(KC):
                gspecs += [(w_in, OFF_GR + c * 128, KC), (w_in, OFF_GM + c * 128, KC),
                           (w_ret_up, c * 128, 8), (w_moba_up, c * 128, 8)]
        nsp = [8 * 3 if stage >= 2 else 0, 32 if stage >= 3 else 0, 64 if stage >= 4 else 0]
        aspecs = mspecs[:nsp[0]] + rspecs[:nsp[1]] + gspecs[:nsp[2]]
        AWS = WStream(aspecs)
        RBASE = nsp[0]
        GBASE = nsp[0] + nsp[1]

        sh = ExitStack()
        top.callback(sh.close)
        hT = fw.sb(sh, "hT", [128, KC, S], BF16)
        bH = fw.bufs_n("hT", 4)

        psrr = {"i": 0}

        def next_bank(lo=0, n=8):
            i = lo + psrr["i"] % n
            psrr["i"] += 1
            return i

        def proj_chunk(wt, bw, evac, act_T=None, b_act=None, nk=KC, banks=(0, 4)):
            aT = hT if act_T is None else act_T
            bA = bH if b_act is None else b_act
            for j in range(4):
                bi = next_bank(*banks)
                for kc in range(nk):
                    fw.op("pe", lambda e, kc=kc, bi=bi, j=j: e.matmul(
                        PS[bi][:, :], wt[:, kc, :], aT[:, kc, j * 512:(j + 1) * 512],
                        start=(kc == 0), stop=(kc == nk - 1)),
                        reads=[bw, bA[j]], writes=[bPS[bi]], signal=(kc == nk - 1), partial=(kc > 0))
                evac(j, PS[bi], bPS[bi])

        sc2 = ExitStack()
        top.callback(sc2.close)
        ind = fw.sb(sc2, "m_ind", [128, 8, 128], BF16)
        cm = fw.sb(sc2, "m_cm", [128, 4, 512], BF16)
        mcos = fw.sb(sc2, "m_cos", [32, S], F32)
        msin = fw.sb(sc2, "m_sin", [32, S], F32)
        pastneg = fw.sb(sc2, "m_pastneg", [128, 128], F32)
        past01 = fw.sb(sc2, "m_past01", [128, 128], F32)
        own01 = fw.sb(sc2, "m_own01", [128, 128], F32)
        b_c2 = fw.buf("m_consts")
        for tile_, src in ((ind, cd["c_ind"]), (cm, cd["c_cm"]), (mcos, cd["c_mcos"]), (msin, cd["c_msin"]),
                           (pastneg, cd["c_pastneg"].rearrange("p t n -> p (t n)")),
                           (past01, cd["c_past01"].rearrange("p t n -> p (t n)")),
                           (own01, cd["c_own01"].rearrange("p t n -> p (t n)"))):
            fw.dma("sp", tile_[:], src, writes=[b_c2], partial=True)
        if stage >= 2:
            AWS.get(0)

        def rms_tile(st_name, xt_ap, b_xt, ss, sd, rstd, col, b_stat, junk, b_junk):
            fw.op("act", lambda e: e.activation(out=junk[:], in_=xt_ap, func=AF.Square,
                                                accum_out=ss[:, col:col + 1]),
                  reads=[b_xt], writes=[b_junk, b_stat])
            fw.op("act", lambda e: e.activation(out=sd[:, col:col + 1], in_=ss[:, col:col + 1], func=AF.Sqrt,
                                                scale=1.0 / D, bias=epst[:, 0:1]),
                  reads=[b_stat, b_eps], writes=[b_stat])
            fw.op("dve", lambda e: e.reciprocal(rstd[:, col:col + 1], sd[:, col:col + 1]),
                  reads=[b_stat], writes=[b_stat])

        with ExitStack() as p1:
            xt = [fw.sb(p1, f"p1_xt{i}", [128, D], F32) for i in range(3)]
            bXt = fw.bufs_n("p1_xt", 3)
            xn = [fw.sb(p1, f"p1_xn{i}", [128, D], BF16) for i in range(2)]
            bXn = fw.bufs_n("p1_xn", 2)
            junk = fw.sb(p1, "p1_junk", [128, D], BF16)
            b_junk = fw.buf("p1_junk")
            wbc = fw.sb(p1, "p1_wbc", [128, D], F32)
            b_wbc = fw.buf("p1_wbc")
            ss = fw.sb(p1, "p1_ss", [128, NT], F32)
            sd = fw.sb(p1, "p1_sd", [128, NT], F32)
            rstd = fw.sb(p1, "p1_rstd", [128, NT], F32)
            bSt = fw.bufs_n("p1_st", NT)
            fw.dma("sp", wbc[:], attn_nw.partition_broadcast(128), writes=[b_wbc])
            def stats1(t):
                s = t % 3
                fw.dma("sp", xt[s][:], x[t * 128:(t + 1) * 128, :], writes=[bXt[s]])
                rms_tile("p1", xt[s][:], bXt[s], ss, sd, rstd, t, bSt[t], junk, b_junk)

            def norm1(t):
                s = t % 2
                s3 = t % 3
                fw.op("dve", lambda e: e.scalar_tensor_tensor(
                    out=xn[s][:], in0=xt[s3][:], scalar=rstd[:, t:t + 1], in1=wbc[:],
                    op0=ALU.mult, op1=ALU.mult),
                    reads=[bXt[s3], bSt[t], b_wbc], writes=[bXn[s]])
                ba, bb = (0, 1) if s == 0 else (2, 3)
                for half, bi in ((0, ba), (1, bb)):
                    pv = PS[bi][:].bitcast(BF16)
                    for c8 in range(8):
                        c = half * 8 + c8
                        fw.op("pe", lambda e, c=c, c8=c8, pv=pv: e.transpose(
                            pv[:, c8 * 128:(c8 + 1) * 128], xn[s][:, c * 128:(c + 1) * 128], ident[:]),
                            reads=[bXn[s], b_const], writes=[bPS[bi]], signal=(c8 == 7), partial=(c8 > 0))
                    fw.op("act", lambda e, half=half, pv=pv: e.activation(
                        out=hT[:, half * 8:(half + 1) * 8, t * 128:(t + 1) * 128],
                        in_=pv.rearrange("p (c n) -> p c n", n=128), func=AF.Copy),
                        reads=[bPS[bi]], writes=[bH[t // 4]], partial=True)
            stats1(0)
            for t in range(NT):
                if t + 1 < NT:
                    stats1(t + 1)
                norm1(t)
            fw.barrier()

        if debug and stage == 1:
            d_hT = dout("d_hT", [D, S], BF16)
            b_d = fw.buf("d_hT")
            fw.dma("sp", d_hT.rearrange("(c p) s -> p c s", p=128), hT[:], reads=bH, writes=[b_d])
            fw.wait_buf("sp", b_d)
            return nc, dram, dbg


        def mm(out_ap, lhsT, rhs, start, stop, reads, wbuf, signal, partial):
            fw.op("pe", lambda e: e.matmul(out_ap, lhsT, rhs, start=start, stop=stop),
                  reads=reads, writes=[wbuf], signal=signal, partial=partial)

        if debug:
            d_om = dout("d_om", [1024, S], BF16)
            b_dom = fw.bufs_n("d_om", 2)
        with ExitStack() as p2:
            SCALE = 128.0 ** -0.5
            qT = [fw.sb(p2, f"m_qT{i}", [128, S], BF16) for i in range(2)]
            kT = [fw.sb(p2, f"m_kT{i}", [128, S], BF16) for i in range(2)]
            bQ = fw.bufs_n("m_qT", 2)
            bK = fw.bufs_n("m_kT", 2)
            vT = fw.sb(p2, "m_vT", [128, S], BF16)
            b_vT = fw.buf("m_vT")
            vtok = [fw.sb(p2, f"m_vtok{i}", [128, NT, 128], BF16) for i in range(2)]
            bV = fw.bufs_n("m_vtok", 2)
            rr = fw.sb(p2, "m_rr", [32, S], F32)
            rp = fw.sb(p2, "m_rp", [32, S], F32)
            b_rr = fw.buf("m_rr")
            b_rp = fw.buf("m_rp")
            biasfull = fw.sb(p2, "m_biasfull", [128, NT, 128], BF16)
            b_bf = fw.buf("m_biasfull")
            biasT = fw.sb(p2, "m_biasT", [128, S], BF16)
            b_bT = fw.buf("m_biasT")
            Es = [fw.sb(p2, f"m_E{i}", [128, 512], BF16) for i in range(3)]
            bE = fw.bufs_n("m_E", 3)
            rec = fw.sb(p2, "m_rec", [128, 512], F32)
            b_rec = fw.buf("m_rec")
            dsb = fw.sb(p2, "m_dsb", [128, 512], F32)
            b_dsb = fw.buf("m_dsb")
            osb = fw.sb(p2, "m_osb", [128, 512], F32)
            b_osb = fw.buf("m_osb")
            oout = [fw.sb(p2, f"m_oout{i}", [128, S], BF16) for i in range(2)]
            bO = fw.bufs_n("m_oout", 2)
            km32 = fw.sb(p2, "m_km32", [128, 8], F32)
            kmb = [fw.sb(p2, f"m_kmb{i}", [128, 8], BF16) for i in range(2)]
            b_km = fw.buf("m_km32")
            bKm = fw.bufs_n("m_kmb", 2)
            s1 = fw.sb(p2, "m_s1", [128, 128], F32)
            cnt = fw.sb(p2, "m_cnt", [128, 128], F32)
            cmpt = fw.sb(p2, "m_cmp", [128, 128], F32)
            b_sel = fw.buf("m_sel")
            fw.op("dve", lambda e: e.memset(biasfull[:], 0.0), writes=[b_bf])
            fw.op("dve", lambda e: e.memset(biasT[:], 0.0), writes=[b_bT])

            class _M:
                def get(self, i):
                    return AWS.get(i, group=1)
            mws = _M()

            def v3(ap2d):
                return ap2d.rearrange("p (t n) -> p t n", n=8)

            def prepA(h):
                s = h % 2
                for which, dstT, bD in ((0, qT[s], bQ[s]), (1, kT[s], bK[s])):
                    wt, bw = mws.get(3 * h + which)

                    def evac(j, ps, bps, dstT=dstT, bD=bD):
                        fw.op("act", lambda e: e.activation(out=dstT[:, j * 512:(j + 1) * 512],
                                                            in_=ps[:, :], func=AF.Copy),
                              reads=[bps], writes=[bD], partial=True)
                        fw.op("dve", lambda e: e.tensor_copy(out=rr[:, j * 512:(j + 1) * 512], in_=ps[0:32, :]),
                              reads=[bps], writes=[b_rr], partial=(j > 0))
                    proj_chunk(wt, bw, evac)
                    fw.dma("sp", rp[0:16, :], rr[16:32, :], reads=[b_rr], writes=[b_rp])
                    fw.dma("sp", rp[16:32, :], rr[0:16, :], reads=[b_rr], writes=[b_rp], partial=True)
                    fw.op("dve", lambda e: e.tensor_tensor(out=rr[:], in0=rr[:], in1=mcos[:], op=ALU.mult),
                          reads=[b_rr, b_rp, b_c2], writes=[b_rr])
                    fw.op("dve", lambda e: e.tensor_tensor(out=rp[:], in0=rp[:], in1=msin[:], op=ALU.mult),
                          reads=[b_rp, b_c2], writes=[b_rp])
                    fw.op("dve", lambda e, dstT=dstT: e.tensor_tensor(out=dstT[0:32, :], in0=rr[:], in1=rp[:], op=ALU.add),
                          reads=[b_rr, b_rp], writes=[bD], partial=False)
                fw.op("dve", lambda e: e.tensor_reduce(out=km32[:, 0:8],
                                                       in_=kT[s][:, :].rearrange("p (n s) -> p n s", s=256),
                                                       axis=AX.X, op=ALU.add),
                      reads=[bK[s]], writes=[b_km])
                fw.op("act", lambda e: e.activation(out=kmb[s][:], in_=km32[:], func=AF.Copy, scale=1.0 / 256.0),
                      reads=[b_km], writes=[bKm[s]])
                wt, bw = mws.get(3 * h + 2)

                def evac_v(j, ps, bps):
                    fw.op("act", lambda e: e.activation(out=vT[:, j * 512:(j + 1) * 512], in_=ps[:, :], func=AF.Copy),
                          reads=[bps], writes=[b_vT], partial=(j > 0))
                proj_chunk(wt, bw, evac_v)
                for half in range(2):
                    bi = next_bank(0, 4)
                    pv = PS[bi][:].bitcast(BF16)
                    for t8 in range(8):
                        t = half * 8 + t8
                        fw.op("pe", lambda e, t=t, t8=t8, pv=pv: e.transpose(
                            pv[:, t8 * 128:(t8 + 1) * 128], vT[:, t * 128:(t + 1) * 128], ident[:]),
                            reads=[b_vT, b_const], writes=[bPS[bi]], signal=(t8 == 7), partial=(t8 > 0))
                    fw.op("dve", lambda e, half=half, pv=pv: e.tensor_copy(
                        out=vtok[s][:, half * 8:(half + 1) * 8, :], in_=pv.rearrange("p (c n) -> p c n", n=128)),
                        reads=[bPS[bi]], writes=[bV[s]], partial=(half > 0))

            def bscore(h):
                s = h % 2
                bi = next_bank(0, 4)
                for t in range(NT):
                    mm(PS[bi][:, t * 8:(t + 1) * 8], qT[s][:, t * 128:(t + 1) * 128], kmb[s][:, 0:8], True, True,
                       [bQ[s], bKm[s]], bPS[bi], t == NT - 1, t > 0)
                D_ = "dve"
                fw.op(D_, lambda e: e.tensor_tensor(out=s1[:], in0=PS[bi][:, 0:128], in1=pastneg[:], op=ALU.add),
                      reads=[bPS[bi], b_c2], writes=[b_sel])
                for m in range(8):
                    dst = cnt if m == 0 else cmpt
                    fw.op(D_, lambda e, m=m, dst=dst: e.tensor_tensor(
                        out=v3(dst[:]), in0=v3(s1[:])[:, :, m:m + 1].to_broadcast([128, NT, 8]), in1=v3(s1[:]),
                        op=ALU.is_gt), reads=[b_sel], writes=[b_sel])
                    if m > 0:
                        fw.op(D_, lambda e: e.tensor_tensor(out=cnt[:], in0=cnt[:], in1=cmpt[:], op=ALU.add),
                              reads=[b_sel], writes=[b_sel])
                fw.op(D_, lambda e: e.tensor_scalar(out=cnt[:], in0=cnt[:], scalar1=3.0, scalar2=None, op0=ALU.is_lt),
                      reads=[b_sel], writes=[b_sel])
                fw.op(D_, lambda e: e.tensor_tensor(out=cnt[:], in0=cnt[:], in1=past01[:], op=ALU.mult),
                      reads=[b_sel, b_c2], writes=[b_sel])
                fw.op(D_, lambda e: e.tensor_tensor(out=cnt[:], in0=cnt[:], in1=own01[:], op=ALU.add),
                      reads=[b_sel, b_c2], writes=[b_sel])
                fw.op(D_, lambda e: e.tensor_scalar(out=biasfull[:, :, 0:8], in0=v3(cnt[:]), scalar1=-1.0, scalar2=-NEG,
                                                    op0=ALU.add, op1=ALU.mult),
                      reads=[b_sel], writes=[b_bf])

            def biasTr(h):
                for g in range(4):
                    bi = next_bank(0, 4)
                    for t4 in range(4):
                        t = g * 4 + t4
                        mm(PS[bi][:, t4 * 128:(t4 + 1) * 128], biasfull[:, t, :], ident[:], True, True,
                           [b_bf, b_const], bPS[bi], t4 == 3, t4 > 0)
                    fw.op("act", lambda e, g=g, bi=bi: e.activation(out=biasT[0:8, g * 512:(g + 1) * 512],
                                                                    in_=PS[bi][0:8, :], func=AF.Copy),
                          reads=[bPS[bi]], writes=[b_bT], partial=(g > 0))

            def att(h):
                s = h % 2
                pairs = [(j, i) for j in range(4) for i in range(4 * (j + 1))]

                def qcols(p):
                    j, i = pairs[p]
                    return (256, 512) if i - 4 * j >= 2 else (0, 512)

                def emitS(p):
                    j, i = pairs[p]
                    sbk = 4 + (p % 2)
                    diag = i >= 4 * j
                    c0, c1 = qcols(p)
                    mm(PS[sbk][:, c0:c1], kT[s][:, i * 128:(i + 1) * 128], qT[s][:, j * 512 + c0:j * 512 + c1], True, False,
                       [bK[s], bQ[s]], bPS[sbk], False, False)
                    mm(PS[sbk][:, c0:c1], ind[:, i // 2, :], biasT[:, j * 512 + c0:j * 512 + c1], False, not diag,
                       [b_c2, b_bT], bPS[sbk], not diag, True)
                    if diag:
                        mm(PS[sbk][:, c0:c1], ident[:], cm[:, i - 4 * j, c0:c1], False, True,
                           [b_c2, b_const], bPS[sbk], True, True)
                    fw.op("act", lambda e: e.activation(out=Es[p % 3][:, c0:c1], in_=PS[sbk][:, c0:c1], func=AF.Exp,
                                                        scale=SCALE),
                          reads=[bPS[sbk]], writes=[bE[p % 3]])

                def emitOD(p):
                    j, i = pairs[p]
                    ni = 4 * (j + 1)
                    bo, bd = 6, 7
                    c0, c1 = qcols(p)
                    mm(PS[bo][:, c0:c1], vtok[s][:, i, :], Es[p % 3][:, c0:c1], i == 0, i == ni - 1,
                       [bV[s], bE[p % 3]], bPS[bo], False, i > 0)
                    mm(PS[bd][:, c0:c1], ones[:], Es[p % 3][:, c0:c1], i == 0, i == ni - 1,
                       [b_const, bE[p % 3]], bPS[bd], True, i > 0)
                    if i == ni - 1:
                        fw.op("act", lambda e: e.activation(out=dsb[:], in_=PS[bd][:, :], func=AF.Copy),
                              reads=[bPS[bd]], writes=[b_dsb])
                        fw.op("dve", lambda e: e.tensor_copy(out=osb[:], in_=PS[bo][:, :]),
                              reads=[bPS[bo]], writes=[b_osb])
                        fw.op("dve", lambda e: e.reciprocal(rec[:], dsb[:]), reads=[b_dsb], writes=[b_rec])
                        fw.op("dve", lambda e: e.tensor_tensor(out=oout[s][:, j * 512:(j + 1) * 512], in0=osb[:],
                                                               in1=rec[:], op=ALU.mult),
                              reads=[b_osb, b_rec], writes=[bO[s]], partial=(j > 0))
                emitS(0)
                for p in range(len(pairs)):
                    if p + 1 < len(pairs):
                        emitS(p + 1)
                    emitOD(p)
                fw.dma("sp", OM[h * 128:(h + 1) * 128, :], oout[s][:], reads=[bO[s]], writes=[b_OM[s]], partial=True)
                if debug:
                    fw.dma("sp", d_om[h * 128:(h + 1) * 128, :], oout[s][:], reads=[bO[s]], writes=[b_dom[s]], partial=True)

            b_OM = fw.bufs_n("scr_om", 2)
            NH = 8 if stage >= 2 else 0
            import os as _os
            sub = int(_os.environ.get("K_SUB", "0")) if debug else 0
            if sub:
                NH = 0
                d_q = dout("d_q", [128, S], BF16)
                d_k = dout("d_k", [128, S], BF16)
                d_v = dout("d_v", [128, NT * 128], BF16)
                d_b = dout("d_b", [128, S], BF16)
                b_dq = fw.bufs_n("d_sub", 4)
                prepA(0)
                if sub >= 2:
                    bscore(0)
                if sub >= 3:
                    biasTr(0)
                if sub >= 4:
                    att(0)
                fw.dma("sp", d_q, qT[0][:], reads=[bQ[0]], writes=[b_dq[0]])
                fw.dma("sp", d_k, kT[0][:], reads=[bK[0]], writes=[b_dq[1]])
                fw.dma("sp", d_v, vtok[0][:].rearrange("p a b -> p (a b)"), reads=[bV[0]], writes=[b_dq[2]])
                fw.dma("sp", d_b, biasT[:], reads=[b_bT, b_bf], writes=[b_dq[3]])
                for b_ in b_dq:
                    fw.wait_buf("sp", b_)
                if sub >= 4:
                    fw.wait_buf("sp", b_dom[0])
                return nc, dram, dbg
            if NH:
                prepA(0)
                bscore(0)
                if NH > 1:
                    prepA(1)
                biasTr(0)
                for h in range(NH):
                    att(h)
                    if h + 1 < NH:
                        bscore(h + 1)
                        if h + 2 < NH:
                            prepA(h + 2)
                        biasTr(h + 1)
            fw.barrier()
        sc2.close()

        if debug and stage == 2:
            fw.wait_buf("sp", b_dom[0])
            fw.wait_buf("sp", b_dom[1])
            return nc, dram, dbg


        if debug:
            d_or = dout("d_or", [1024, S], BF16)
            b_dor = fw.buf("d_or")
        b_OR = fw.buf("scr_or")
        with ExitStack() as p3:
            rcos = fw.sb(p3, "r_cos", [128, S], F32)
            rsin = fw.sb(p3, "r_sin", [128, S], F32)
            rdt = fw.sb(p3, "r_dt", [128, 4, 128], F32)
            rzeta = fw.sb(p3, "r_zeta", [128, 4], F32)
            repsq = fw.sb(p3, "r_epsq", [128, 4, 512], F32)
            retw = fw.sb(p3, "r_retw", [128, 8], F32)
            b_c3 = fw.buf("r_consts")
            for tile_, src in ((rcos, cd["c_rcos"]), (rsin, cd["c_rsin"]), (rdt, cd["c_rdt"]), (rzeta, cd["c_rzeta"]),
                               (repsq, cd["c_repsq"]), (retw, dram["retw_t"])):
                fw.dma("sp", tile_[:], src, writes=[b_c3], partial=True)
            rqT = fw.sb(p3, "r_qT", [128, 2, S], BF16)
            rkT = fw.sb(p3, "r_kT", [128, 2, S], BF16)
            rvT = fw.sb(p3, "r_vT", [128, 2, S], BF16)
            b_rq, b_rk, b_rv = fw.buf("r_qT"), fw.buf("r_kT"), fw.buf("r_vT")
            kTok = fw.sb(p3, "r_kTok", [128, NT, 256], BF16)
            rvtok = fw.sb(p3, "r_vtok", [128, NT, 256], BF16)
            b_kTok, b_rvtok = fw.buf("r_kTok"), fw.buf("r_vtok")
            rgs = fw.sb(p3, "r_gs", [128, 2, S], BF16)
            b_rgs = fw.buf("r_gs")
            oraw = fw.sb(p3, "r_oraw", [128, 2, S], F32)
            b_oraw = fw.buf("r_oraw")
            sq, b_sq = rvT, b_rv
            orT, b_orT = rkT, b_rk
            tmps = [fw.sb(p3, f"r_tmp{i}", [128, 512], F32) for i in range(4)]
            bT = fw.bufs_n("r_tmp", 4)
            Wst = fw.sb(p3, "r_W", [128, 512], F32)
            b_W = fw.buf("r_W")
            Wb = [fw.sb(p3, f"r_Wb{i}", [128, 512], BF16) for i in range(2)]
            bWb = fw.bufs_n("r_Wb", 2)
            PT = [fw.sb(p3, f"r_PT{i}", [128, 128], BF16) for i in range(2)]
            bPT = fw.bufs_n("r_PT", 2)
            nrm = fw.sb(p3, "r_nrm", [128, 512], F32)
            b_nrm = fw.buf("r_nrm")

            class _R:
                def get(self, i):
                    return AWS.get(RBASE + i, group=2, base=RBASE + (i // 2) * 2)
            rws = _R()

            def proj_pair(i0, evac2):
                (wt0, bw0), (wt1, bw1) = rws.get(i0), rws.get(i0 + 1)
                for j in range(4):
                    ba = next_bank(0, 4)
                    bb = next_bank(0, 4)
                    for wt, bw, bi in ((wt0, bw0, ba), (wt1, bw1, bb)):
                        for kc in range(KC):
                            fw.op("pe", lambda e, kc=kc, bi=bi, j=j, wt=wt: e.matmul(
                                PS[bi][:, :], wt[:, kc, :], hT[:, kc, j * 512:(j + 1) * 512],
                                start=(kc == 0), stop=(kc == KC - 1)),
                                reads=[bw, bH[j]], writes=[bPS[bi]], signal=(kc == KC - 1), partial=(kc > 0))
                    evac2(j, ba, bb)

            def rot_evac(dstT, bD):
                def f(j, ba, bb):
                    sl = slice(j * 512, (j + 1) * 512)
                    TT = lambda o, a, b_, op_, rd, wr, part=False: fw.op(
                        "dve", lambda e: e.tensor_tensor(out=o, in0=a, in1=b_, op=op_), reads=rd, writes=wr, partial=part)
                    TT(tmps[0][:], PS[ba][:, :], rcos[:, sl], ALU.mult, [bPS[ba], b_c3], [bT[0]])
                    TT(tmps[1][:], PS[bb][:, :], rsin[:, sl], ALU.mult, [bPS[bb], b_c3], [bT[1]])
                    TT(tmps[2][:], PS[bb][:, :], rcos[:, sl], ALU.mult, [bPS[bb], b_c3], [bT[2]])
                    TT(tmps[3][:], PS[ba][:, :], rsin[:, sl], ALU.mult, [bPS[ba], b_c3], [bT[3]])
                    TT(dstT[:, 0, sl], tmps[0][:], tmps[1][:], ALU.subtract, [bT[0], bT[1]], [bD], part=True)
                    TT(dstT[:, 1, sl], tmps[2][:], tmps[3][:], ALU.add, [bT[2], bT[3]], [bD], part=True)
                return f

            def copy_evac(dstT, bD, func):
                def f(j, ba, bb):
                    sl = slice(j * 512, (j + 1) * 512)
                    for c, bi in ((0, ba), (1, bb)):
                        fw.op("act", lambda e, c=c, bi=bi: e.activation(out=dstT[:, c, sl], in_=PS[bi][:, :], func=func),
                              reads=[bPS[bi]], writes=[bD], partial=True)
                return f

            def to_tok(srcT, bS, dst, bDst, h, scaled):
                for g in range(4):
                    bi = next_bank(0, 4)
                    pv = PS[bi][:].bitcast(BF16)
                    for t4 in range(4):
                        t = g * 4 + t4
                        for c in range(2):
                            k8 = t4 * 2 + c
                            fw.op("pe", lambda e, t=t, c=c, k8=k8, pv=pv: e.transpose(
                                pv[:, k8 * 128:(k8 + 1) * 128], srcT[:, c, t * 128:(t + 1) * 128], ident[:]),
                                reads=[bS, b_const], writes=[bPS[bi]], signal=(k8 == 7), partial=(k8 > 0))
                    if scaled:
                        fw.op("act", lambda e, g=g, pv=pv: e.activation(
                            out=dst[:, g * 4:(g + 1) * 4, :], in_=pv.rearrange("p (t n) -> p t n", n=256),
                            func=AF.Copy, scale=rzeta[:, h:h + 1]),
                            reads=[bPS[bi], b_c3], writes=[bDst], partial=(g > 0))
                    else:
                        fw.op("dve", lambda e, g=g, pv=pv: e.tensor_copy(
                            out=dst[:, g * 4:(g + 1) * 4, :], in_=pv.rearrange("p (t n) -> p t n", n=256)),
                            reads=[bPS[bi]], writes=[bDst], partial=(g > 0))

            for h in range(4 if stage >= 3 else 0):
                gC = float(gch[h])
                proj_pair(8 * h + 0, rot_evac(rqT, b_rq))
                proj_pair(8 * h + 2, rot_evac(rkT, b_rk))
                proj_pair(8 * h + 4, copy_evac(rvT, b_rv, AF.Copy))
                proj_pair(8 * h + 6, copy_evac(rgs, b_rgs, AF.Silu))
                to_tok(rkT, b_rk, kTok, b_kTok, h, True)
                to_tok(rvT, b_rv, rvtok, b_rvtok, h, False)

                def emitA(n):
                    st = n % 2
                    cs = slice(n * 128, (n + 1) * 128)
                    for c in range(2):
                        mm(PS[st][:, 0:128], rkT[:, c, cs], rqT[:, c, cs], c == 0, c == 1,
                           [b_rk, b_rq], bPS[st], c == 1, c > 0)
                    fw.op("dve", lambda e: e.tensor_tensor(out=PT[st][:], in0=PS[st][:, 0:128], in1=rdt[:, h, :],
                                                           op=ALU.mult),
                          reads=[bPS[st], b_c3], writes=[bPT[st]])
                    if n < NT - 1:
                        for c in range(2):
                            mm(PS[2 + st][:, c * 256:(c + 1) * 256], kTok[:, n, c * 128:(c + 1) * 128], rvtok[:, n, :],
                               True, True, [b_kTok, b_rvtok], bPS[2 + st], c == 1, c > 0)

                def emitB(n):
                    st = n % 2
                    ob = 4 + st
                    cs = slice(n * 128, (n + 1) * 128)
                    if n < NT - 1:
                        if n == 0:
                            fw.op("dve", lambda e: e.tensor_copy(out=Wst[:], in_=PS[2 + st][:, :]),
                                  reads=[bPS[2 + st]], writes=[b_W])
                        else:
                            fw.op("dve", lambda e: e.scalar_tensor_tensor(
                                out=Wst[:], in0=Wst[:], scalar=gC, in1=PS[2 + st][:, :], op0=ALU.mult, op1=ALU.add),
                                reads=[b_W, bPS[2 + st]], writes=[b_W])
                        fw.op("act", lambda e: e.activation(out=Wb[(n + 1) % 2][:], in_=Wst[:], func=AF.Copy),
                              reads=[b_W], writes=[bWb[(n + 1) % 2]])
                    for ec in range(2):
                        reg = PS[ob][:, ec * 128:(ec + 1) * 128]
                        last_inner = (n == 0)
                        mm(reg, rvtok[:, n, ec * 128:(ec + 1) * 128], PT[st][:], True, last_inner,
                           [b_rvtok, bPT[st]], bPS[ob], last_inner and ec == 1, ec > 0)
                        if n > 0:
                            for c in range(2):
                                mm(reg, Wb[st][:, c * 256 + ec * 128: c * 256 + (ec + 1) * 128], rqT[:, c, cs],
                                   False, c == 1, [bWb[st], b_rq], bPS[ob], c == 1 and ec == 1, True)
                    fw.op("act", lambda e: e.activation(out=oraw[:, :, cs],
                                                        in_=PS[ob][:, 0:256].rearrange("p (c n) -> p c n", n=128),
                                                        func=AF.Copy),
                          reads=[bPS[ob]], writes=[b_oraw], partial=(n > 0))
                emitA(0)
                for n in range(NT):
                    if n + 1 < NT:
                        emitA(n + 1)
                    emitB(n)
                fw.op("act", lambda e: e.activation(out=sq[:], in_=oraw[:], func=AF.Square),
                      reads=[b_oraw], writes=[b_sq])
                for j in range(4):
                    sl = slice(j * 512, (j + 1) * 512)
                    bi = next_bank(0, 4)
                    for c in range(2):
                        mm(PS[bi][:, :], ones[:], sq[:, c, sl], c == 0, c == 1, [b_const, b_sq], bPS[bi], c == 1, c > 0)
                    fw.op("dve", lambda e: e.scalar_tensor_tensor(out=nrm[:], in0=PS[bi][:, :], scalar=1.0 / 256.0,
                                                                  in1=repsq[:, h, :], op0=ALU.mult, op1=ALU.add),
                          reads=[bPS[bi], b_c3], writes=[b_nrm])
                    fw.op("act", lambda e: e.activation(out=nrm[:], in_=nrm[:], func=AF.Sqrt),
                          reads=[b_nrm], writes=[b_nrm])
                    fw.op("dve", lambda e: e.reciprocal(nrm[:], nrm[:]), reads=[b_nrm], writes=[b_nrm])
                    for ec in range(2):
                        ch = h * 2 + ec
                        fw.op("dve", lambda e, ec=ec, ch=ch: e.scalar_tensor_tensor(
                            out=tmps[ec][:], in0=oraw[:, ec, sl], scalar=retw[:, ch:ch + 1], in1=nrm[:],
                            op0=ALU.mult, op1=ALU.mult),
                            reads=[b_oraw, b_c3, b_nrm], writes=[bT[ec]])
                        fw.op("dve", lambda e, ec=ec: e.tensor_tensor(out=orT[:, ec, sl], in0=tmps[ec][:],
                                                                      in1=rgs[:, ec, sl], op=ALU.mult),
                              reads=[bT[ec], b_rgs], writes=[b_orT], partial=(j > 0 or ec > 0))
                for ec in range(2):
                    ch = h * 2 + ec
                    fw.dma("sp", OR[ch * 128:(ch + 1) * 128, :], orT[:, ec, :], reads=[b_orT], writes=[b_OR], partial=True)
                    if debug:
                        fw.dma("sp", d_or[ch * 128:(ch + 1) * 128, :], orT[:, ec, :], reads=[b_orT], writes=[b_dor],
                               partial=True)
            fw.barrier()

        if debug and stage == 3:
            fw.wait_buf("sp", b_dor)
            return nc, dram, dbg


        if debug and stage == 4:
            b_dmg = fw.buf("d_mg")
        b_MG = fw.bufs_n("scr_mg", 2)
        with ExitStack() as p4:
            orA = fw.sb(p4, "g_orA", [128, 8, S], BF16)
            omA = fw.sb(p4, "g_omA", [128, 8, S], BF16)
            b_orA, b_omA = fw.buf("g_orA"), fw.buf("g_omA")
            fw.dma("sp", omA[:], OM.rearrange("(c p) s -> p c s", p=128), reads=b_OM, writes=[b_omA])
            fw.dma("sp", orA[:], OR.rearrange("(c p) s -> p c s", p=128), reads=[b_OR], writes=[b_orA])
            sg = [fw.sb(p4, f"g_sg{i}", [128, 512], F32) for i in range(4)]
            bSg = fw.bufs_n("g_sg", 4)
            tt_ = [fw.sb(p4, f"g_tt{i}", [128, 512], F32) for i in range(4)]
            bTt = fw.bufs_n("g_tt", 4)
            mgc = [fw.sb(p4, f"g_mgc{i}", [128, S], BF16) for i in range(2)]
            bMgc = fw.bufs_n("g_mgc", 2)
            class _G:
                def get(self, i):
                    return AWS.get(GBASE + i, group=4, base=GBASE + (i // 4) * 4)
            gws = _G()
            it = 0
            srcs = ((hT, None, KC), (hT, None, KC), (orA, b_orA, 8), (omA, b_omA, 8))

            def grp(ws_, i, j, bi):
                aT, bA, nk = srcs[i]
                bA = bH[j] if bA is None else bA
                wt, bw = ws_[i]
                sl = slice(j * 512, (j + 1) * 512)
                for kc in range(nk):
                    mm(PS[bi][:, :], wt[:, kc, :], aT[:, kc, sl], kc == 0, kc == nk - 1,
                       [bw, bA], bPS[bi], kc == nk - 1, kc > 0)

            def combine(mc, bmc, j, bg0, bg1, bu0, bu1, k2):
                sl = slice(j * 512, (j + 1) * 512)
                for i, bg in enumerate((bg0, bg1)):
                    fw.op("act", lambda e, i=i, bg=bg: e.activation(out=sg[k2 + i][:], in_=PS[bg][:, :], func=AF.Sigmoid),
                          reads=[bPS[bg]], writes=[bSg[k2 + i]])
                for i, bu in enumerate((bu0, bu1)):
                    fw.op("dve", lambda e, i=i, bu=bu: e.tensor_tensor(out=tt_[k2 + i][:], in0=PS[bu][:, :],
                                                                       in1=sg[k2 + i][:], op=ALU.mult),
                          reads=[bPS[bu], bSg[k2 + i]], writes=[bTt[k2 + i]])
                fw.op("dve", lambda e: e.tensor_tensor(out=mc[:, sl], in0=tt_[k2][:], in1=tt_[k2 + 1][:], op=ALU.add),
                      reads=[bTt[k2], bTt[k2 + 1]], writes=[bmc], partial=(j > 0))
            for c in range(KC if stage >= 4 else 0):
                ws_ = [gws.get(4 * c + i) for i in range(4)]
                mc = mgc[c % 2]
                if c == 0:
                    for j in range(4):
                        grp(ws_, 0, j, 2 * j)
                        grp(ws_, 1, j, 2 * j + 1)
                    sgx = [fw.sb(p4, f"g_sgx{i}", [128, 512], F32) for i in range(8)]
                    bSgx = fw.bufs_n("g_sgx", 8)
                    for b_ in range(8):
                        fw.op("act", lambda e, b_=b_: e.activation(out=sgx[b_][:], in_=PS[b_][:, :], func=AF.Sigmoid),
                              reads=[bPS[b_]], writes=[bSgx[b_]])
                    for j in range(4):
                        sl = slice(j * 512, (j + 1) * 512)
                        bu0, bu1 = 2 * (j % 2), 2 * (j % 2) + 1
                        grp(ws_, 2, j, bu0)
                        grp(ws_, 3, j, bu1)
                        k2 = 2 * (j % 2)
                        for i, bu in enumerate((bu0, bu1)):
                            fw.op("dve", lambda e, i=i, bu=bu: e.tensor_tensor(
                                out=tt_[k2 + i][:], in0=PS[bu][:, :], in1=sgx[2 * j + i][:], op=ALU.mult),
                                reads=[bPS[bu], bSgx[2 * j + i]], writes=[bTt[k2 + i]])
                        fw.op("dve", lambda e: e.tensor_tensor(out=mc[:, sl], in0=tt_[k2][:], in1=tt_[k2 + 1][:],
                                                               op=ALU.add),
                              reads=[bTt[k2], bTt[k2 + 1]], writes=[bMgc[c % 2]], partial=(j > 0))
                else:
                    for j in range(4):
                        base = 4 * (it % 2)
                        it += 1
                        for i in range(4):
                            grp(ws_, i, j, base + i)
                        combine(mc, bMgc[c % 2], j, base, base + 1, base + 2, base + 3, 2 * (it % 2))
                fw.dma("sp", MG[:, :, c * 128:(c + 1) * 128].rearrange("t p n -> p t n"),
                       mc[:, :].rearrange("p (t n) -> p t n", n=128), reads=[bMgc[c % 2]], writes=[b_MG[c % 2]], partial=True)
                if debug and stage == 4:
                    pass
            fw.barrier()
        sh.close()
        fspecs = []
        if w_ffn_up is not None and stage >= 6:
            for c in range(NFF):
                fspecs += [(w_ffn_up, c * 128, KC), (w_ffn_up, DFF + c * 128, KC)]
        fws = WStream(fspecs)

        if debug and stage == 4:
            with ExitStack() as pd:
                dt_ = fw.sb(pd, "dbg_t", [128, KC * 128], BF16)
                b_dt = fw.buf("dbg_t")
                d_mg2 = dout("d_mg2", [NT, 128, KC * 128], BF16)
                for t in range(NT):
                    fw.dma("sp", dt_[:], MG[t], reads=b_MG, writes=[b_dt])
                    fw.dma("sp", d_mg2[t], dt_[:], reads=[b_dt], writes=[b_dmg], partial=True)
                fw.wait_buf("sp", b_dmg)
            return nc, dram, dbg


        sh2 = ExitStack()
        top.callback(sh2.close)
        h2T = fw.sb(sh2, "h2T", [128, KC, S], BF16)
        bH2 = fw.bufs_n("h2T", 4)
        with ExitStack() as p5:
            wout = fw.sb(p5, "o_wout", [128, KC, D], BF16)
            bWo = fw.bufs_n("o_wout", 4)
            for cg in range(4):
                fw.dma("pool", wout[:, :, cg * 512:(cg + 1) * 512],
                       w_out[:, cg * 512:(cg + 1) * 512].rearrange("(kc p) n -> p kc n", p=128), writes=[bWo[cg]])
            if fspecs:
                fws.get(0)
            wbc2 = fw.sb(p5, "o_wbc", [128, D], F32)
            b_wbc2 = fw.buf("o_wbc")
            fw.dma("sp", wbc2[:], ffn_nw.partition_broadcast(128), writes=[b_wbc2])
            mgt = [fw.sb(p5, f"o_mgt{i}", [128, KC * 128], BF16) for i in range(2)]
            bMgt = fw.bufs_n("o_mgt", 2)
            xb5 = [fw.sb(p5, f"o_xb{i}", [128, D], F32) for i in range(2)]
            bXb5 = fw.bufs_n("o_xb", 2)
            xn5 = fw.sb(p5, "o_xn", [128, D], BF16)
            b_xn5 = fw.buf("o_xn")
            ss5 = fw.sb(p5, "o_ss", [128, NT], F32)
            sd5 = fw.sb(p5, "o_sd", [128, NT], F32)
            rstd5 = fw.sb(p5, "o_rstd", [128, NT], F32)
            bSt5 = fw.bufs_n("o_st", NT)
            b_X2p = fw.bufs_n("scr_x2_", 2)

            def loads5(t):
                fw.dma("sp", mgt[t % 2][:], MG[t], reads=b_MG, writes=[bMgt[t % 2]])
                fw.dma("sp", xb5[t % 2][:], x[t * 128:(t + 1) * 128, :], writes=[bXb5[t % 2]])
            NT5 = NT if stage >= 5 else 0

            def mm5(t):
                s_ = t % 2
                for cg in range(4):
                    bi = cg
                    for kc in range(KC):
                        mm(PS[bi][:, :], mgt[s_][:, kc * 128:(kc + 1) * 128], wout[:, kc, cg * 512:(cg + 1) * 512],
                           kc == 0, kc == KC - 1, [bMgt[s_], bWo[cg]], bPS[bi], kc == KC - 1, kc > 0)

            def add5(t):
                s_ = t % 2
                x2t, b_x2t = xb5[s_], bXb5[s_]
                for cg in range(4):
                    bi = cg
                    fw.op("dve", lambda e, cg=cg, bi=bi: e.tensor_tensor(
                        out=x2t[:, cg * 512:(cg + 1) * 512], in0=PS[bi][:, :], in1=x2t[:, cg * 512:(cg + 1) * 512],
                        op=ALU.add), reads=[bPS[bi], b_x2t], writes=[b_x2t])
                fw.dma("sp", X2[t * 128:(t + 1) * 128, :], x2t[:], reads=[b_x2t], writes=[b_X2p[s_]], partial=True)

            def norm5(t):
                s_ = t % 2
                x2t, b_x2t = xb5[s_], bXb5[s_]
                rms_tile("p5", x2t[:], b_x2t, ss5, sd5, rstd5, t, bSt5[t], xn5, b_xn5)
                fw.op("dve", lambda e: e.scalar_tensor_tensor(
                    out=xn5[:], in0=x2t[:], scalar=rstd5[:, t:t + 1], in1=wbc2[:], op0=ALU.mult, op1=ALU.mult),
                    reads=[b_x2t, bSt5[t], b_wbc2], writes=[b_xn5])
                ba, bb = (4, 5) if s_ == 0 else (6, 7)
                for half, bi in ((0, ba), (1, bb)):
                    pv = PS[bi][:].bitcast(BF16)
                    for c8 in range(8):
                        c = half * 8 + c8
                        fw.op("pe", lambda e, c=c, c8=c8, pv=pv: e.transpose(
                            pv[:, c8 * 128:(c8 + 1) * 128], xn5[:, c * 128:(c + 1) * 128], ident[:]),
                            reads=[b_xn5, b_const], writes=[bPS[bi]], signal=(c8 == 7), partial=(c8 > 0))
                    fw.op("act", lambda e, half=half, pv=pv: e.activation(
                        out=h2T[:, half * 8:(half + 1) * 8, t * 128:(t + 1) * 128],
                        in_=pv.rearrange("p (c n) -> p c n", n=128), func=AF.Copy),
                        reads=[bPS[bi]], writes=[bH2[t // 4]], partial=True)
            if NT5:
                loads5(0)
                loads5(1)
                mm5(0)
                add5(0)
            for t in range(NT5):
                if t + 1 < NT5:
                    mm5(t + 1)
                norm5(t)
                if t + 2 < NT5:
                    loads5(t + 2)
                if t + 1 < NT5:
                    add5(t + 1)
            fw.barrier()

        if debug and stage == 5:
            d_x2 = dout("d_x2", [S, D], F32)
            d_h2T = dout("d_h2T", [D, S], BF16)
            b_d5 = fw.buf("d_5")
            fw.dma("sp", d_h2T.rearrange("(c p) s -> p c s", p=128), h2T[:], reads=bH2, writes=[b_d5], partial=True)
            with ExitStack() as pd:
                dt_ = fw.sb(pd, "dbg_t5", [128, D], F32)
                b_dt = fw.buf("dbg_t5")
                for t in range(NT):
                    fw.dma("sp", dt_[:], X2[t * 128:(t + 1) * 128, :], reads=b_X2p, writes=[b_dt])
                    fw.dma("sp", d_x2[t * 128:(t + 1) * 128, :], dt_[:], reads=[b_dt], writes=[b_d5], partial=True)
            fw.wait_buf("sp", b_d5)
            return nc, dram, dbg

        b_GT = fw.bufs_n("scr_gt", 2)
        with ExitStack() as p6:
            convw = fw.sb(p6, "f_convw", [128, 3 * 88], F32)
            convb = fw.sb(p6, "f_convb", [128, 88], F32)
            b_c6 = fw.buf("f_consts")
            fw.dma("sp", convw[:], dram["convw_t"], writes=[b_c6], partial=True)
            fw.dma("sp", convb[:], dram["convb_t"], writes=[b_c6], partial=True)
            ua = [[fw.sb(p6, f"f_u{ab}{i}", [128, S + 2], F32) for i in range(2)] for ab in range(2)]
            ya = [[fw.sb(p6, f"f_y{ab}{i}", [128, S], F32) for i in range(2)] for ab in range(2)]
            bU = [fw.bufs_n(f"f_u{ab}", 2) for ab in range(2)]
            bY = [fw.bufs_n(f"f_y{ab}", 2) for ab in range(2)]
            gt = [fw.sb(p6, f"f_g{i}", [128, S], BF16) for i in range(2)]
            bG = fw.bufs_n("f_g", 2)
            for ab in range(2):
                for i in range(2):
                    fw.op("dve", lambda e, ab=ab, i=i: e.memset(ua[ab][i][:, 0:2], 0.0), writes=[bU[ab][i]])
            for c in range(NFF if stage >= 6 else 0):
                s_ = c % 2
                for ab in range(2):
                    wt, bw = fws.get(2 * c + ab)
                    chn = ab * NFF + c
                    u_, y_ = ua[ab][s_], ya[ab][s_]
                    bu_, by_ = bU[ab][s_], bY[ab][s_]

                    def evac(j, ps, bps, u_=u_, y_=y_, bu_=bu_, by_=by_, chn=chn):
                        fw.op("act", lambda e: e.activation(out=u_[:, 2 + j * 512: 2 + (j + 1) * 512], in_=ps[:, :],
                                                            func=AF.Copy),
                              reads=[bps], writes=[bu_], partial=True)
                        fw.op("act", lambda e: e.activation(out=y_[:, j * 512:(j + 1) * 512], in_=ps[:, :],
                                                            func=AF.Identity, scale=convw[:, 2 * 88 + chn: 2 * 88 + chn + 1],
                                                            bias=convb[:, chn:chn + 1]),
                              reads=[bps, b_c6], writes=[by_], partial=(j > 0))
                    proj_chunk(wt, bw, evac, act_T=h2T, b_act=bH2, banks=(0, 8))
                    for tap in (1, 0):
                        sh_ = 2 - tap
                        fw.op("dve", lambda e, u_=u_, y_=y_, tap=tap, sh_=sh_, chn=chn: e.scalar_tensor_tensor(
                            out=y_[:], in0=u_[:, 2 - sh_: 2 - sh_ + S], scalar=convw[:, tap * 88 + chn: tap * 88 + chn + 1],
                            in1=y_[:], op0=ALU.mult, op1=ALU.add),
                            reads=[bu_, by_, b_c6], writes=[by_])
                fw.op("act", lambda e: e.activation(out=ya[0][s_][:], in_=ya[0][s_][:], func=AF.Silu),
                      reads=[bY[0][s_]], writes=[bY[0][s_]])
                fw.op("dve", lambda e: e.tensor_tensor(out=gt[s_][:], in0=ya[0][s_][:], in1=ya[1][s_][:], op=ALU.mult),
                      reads=[bY[0][s_], bY[1][s_]], writes=[bG[s_]])
                fw.dma("sp", GT[:, :, c * 128:(c + 1) * 128].rearrange("t p n -> p t n"),
                       gt[s_][:, :].rearrange("p (t n) -> p t n", n=128), reads=[bG[s_]], writes=[b_GT[s_]], partial=True)
            fw.barrier()
        sh2.close()

        if debug and stage == 6:
            b_d6 = fw.buf("d_6")
            d_gt2 = dout("d_gt2", [NT, 128, NFF * 128], BF16)
            with ExitStack() as pd:
                dt_ = fw.sb(pd, "dbg_t6", [128, NFF * 128], BF16)
                b_dt = fw.buf("dbg_t6")
                for t in range(NT):
                    fw.dma("sp", dt_[:], GT[t], reads=b_GT, writes=[b_dt])
                    fw.dma("sp", d_gt2[t], dt_[:], reads=[b_dt], writes=[b_d6], partial=True)
            fw.wait_buf("sp", b_d6)
            return nc, dram, dbg

        b_X3 = fw.bufs_n("scr_x3", 4)
        b_out = fw.bufs_n("out", 2)
        with ExitStack() as p7:
            wd = [fw.sb(p7, f"d_wd{i}", [128, NFF, 512], BF16) for i in range(2)]
            bWd = [[fw.buf(f"d_wd{i}_")] * NFF for i in range(2)]
            gtt = [fw.sb(p7, f"d_gtt{i}", [128, NFF * 128], BF16) for i in range(3)]
            bGtt = fw.bufs_n("d_gtt", 3)
            x2q = [fw.sb(p7, f"d_x2q{i}", [128, 512], F32) for i in range(3)]
            bX2q = fw.bufs_n("d_x2q", 3)
            x3q = [fw.sb(p7, f"d_x3q{i}", [128, 512], F32) for i in range(2)]
            bX3q = fw.bufs_n("d_x3q", 2)
            wbc3 = fw.sb(p7, "e_wbc", [128, D], F32)
            b_wbc3 = fw.buf("e_wbc")
            x3t = [fw.sb(p7, f"e_x3t{i}", [128, D], F32) for i in range(2)]
            bX3t = fw.bufs_n("e_x3t", 2)
            junk8 = fw.sb(p7, "e_junk", [128, D], BF16)
            b_junk8 = fw.buf("e_junk")
            ss8 = fw.sb(p7, "e_ss", [128, NT], F32)
            sd8 = fw.sb(p7, "e_sd", [128, NT], F32)
            rstd8 = fw.sb(p7, "e_rstd", [128, NT], F32)
            bSt8 = fw.bufs_n("e_st", NT)

            def load_wd(q):
                for hk in range(2):
                    k0, k1 = hk * 22, (hk + 1) * 22
                    fw.dma("pool", wd[q % 2][:, k0:k1, :],
                           w_ffn_down[k0 * 128:k1 * 128, q * 512:(q + 1) * 512].rearrange("(kc p) n -> p kc n", p=128),
                           writes=[bWd[q % 2][k0]], partial=(hk > 0))
            seq7 = [(q, t) for q in range(4) for t in range(NT)] if stage >= 7 else []

            def loads7(k):
                q, t = seq7[k]
                fw.dma("sp", gtt[k % 3][:], GT[t], reads=b_GT, writes=[bGtt[k % 3]])
                fw.dma("sp", x2q[k % 3][:], X2[t * 128:(t + 1) * 128, q * 512:(q + 1) * 512], reads=b_X2p,
                       writes=[bX2q[k % 3]])

            def final8a(t):
                s_ = t % 2
                fw.dma("act", x3t[s_][:], X3[t * 128:(t + 1) * 128, :], reads=b_X3, writes=[bX3t[s_]])
                fw.op("act", lambda e: e.activation(out=junk8[:], in_=x3t[s_][:], func=AF.Square,
                                                    accum_out=ss8[:, t:t + 1]),
                      reads=[bX3t[s_]], writes=[b_junk8, bSt8[t]])
                fw.op("act", lambda e: e.activation(out=sd8[:, t:t + 1], in_=ss8[:, t:t + 1], func=AF.Sqrt,
                                                    scale=1.0 / D, bias=epst[:, 0:1]),
                      reads=[bSt8[t], b_eps], writes=[bSt8[t]])

            def final8b(t):
                s_ = t % 2
                fw.op("dve", lambda e: e.reciprocal(rstd8[:, t:t + 1], sd8[:, t:t + 1]),
                      reads=[bSt8[t]], writes=[bSt8[t]])
                fw.op("dve", lambda e: e.scalar_tensor_tensor(
                    out=x3t[s_][:], in0=x3t[s_][:], scalar=rstd8[:, t:t + 1], in1=wbc3[:], op0=ALU.mult, op1=ALU.mult),
                    reads=[bX3t[s_], bSt8[t], b_wbc3], writes=[bX3t[s_]])
                fw.dma("act", out[t * 128:(t + 1) * 128, :], x3t[s_][:], reads=[bX3t[s_]], writes=[b_out[s_]], partial=True)
            if seq7:
                fw.dma("sp", wbc3[:], fin_nw.partition_broadcast(128), writes=[b_wbc3])
                load_wd(0)
                loads7(0)
                loads7(1)
            for k, (q, t) in enumerate(seq7):
                if t == 0 and q + 1 < 4:
                    load_wd(q + 1)
                if k + 2 < len(seq7):
                    loads7(k + 2)
                s3, s_ = k % 3, k % 2
                bi = next_bank(0, 8)
                for kc in range(NFF):
                    mm(PS[bi][:, :], gtt[s3][:, kc * 128:(kc + 1) * 128], wd[q % 2][:, kc, :],
                       kc == 0, kc == NFF - 1, [bGtt[s3], bWd[q % 2][kc]], bPS[bi], kc == NFF - 1, kc > 0)
                fw.op("dve", lambda e, s_=s_, s3=s3, bi=bi: e.tensor_tensor(out=x3q[s_][:], in0=PS[bi][:, :], in1=x2q[s3][:],
                                                                            op=ALU.add),
                      reads=[bPS[bi], bX2q[s3]], writes=[bX3q[s_]])
                xb_ = b_X3[(2 if q == 3 else 0) + s_]
                fw.dma("sp", X3[t * 128:(t + 1) * 128, q * 512:(q + 1) * 512], x3q[s_][:], reads=[bX3q[s_]],
                       writes=[xb_], partial=True)
                if q == 3:
                    if t > 0:
                        final8b(t - 1)
                    final8a(t)
            if seq7:
                final8b(NT - 1)
                fw.wait_buf("sp", b_out[0])
                fw.wait_buf("sp", b_out[1])
            fw.barrier()
        LASTFW[0] = fw

    return nc, dram, dbg


def _shared_inputs(inputs):
    consts, _ = _get_consts()
    f = lambda a: np.ascontiguousarray(np.asarray(a, dtype=np.float32))
    m = {
        "w_in": f(inputs["w_in"][0]),
        "w_ret_up": f(inputs["w_ret_up"][0]),
        "w_moba_up": f(inputs["w_moba_up"][0]),
        "w_out": f(inputs["w_out"][0]),
        "w_ffn_up": f(inputs["w_ffn_up"][0]),
        "w_ffn_down": f(inputs["w_ffn_down"][0]),
        "attn_norm_w": f(inputs["attn_norm_w"][0]).reshape(1, D),
        "ffn_norm_w": f(inputs["ffn_norm_w"][0]).reshape(1, D),
        "final_norm_w": f(inputs["final_norm_w"]).reshape(1, D),
        "retw_t": f(np.asarray(inputs["ret_norm_w"][0]).reshape(8, 128).T),
        "convw_t": f(np.asarray(inputs["conv_w"][0]).reshape(3, 88, 128).transpose(2, 0, 1).reshape(128, 264)),
        "convb_t": f(np.asarray(inputs["conv_b"][0]).reshape(88, 128).T),
    }
    m.update(consts)
    return m


def kernel(**inputs):
    nc, dram, dbg = build_nc()
    shared = _shared_inputs(inputs)
    xs = np.asarray(inputs["x"], dtype=np.float32)
    in_maps = []
    for b in range(8):
        mm = dict(shared)
        mm["x"] = np.ascontiguousarray(xs[b])
        in_maps.append({k: mm[k] for k in dram})
    res = run_bass_kernel_spmd(nc, in_maps, core_ids=list(range(8)))
    return np.stack([np.asarray(r["out"], dtype=np.float32) for r in res.results], axis=0)
```

```python
import math
from contextlib import ExitStack

import numpy as np
import ml_dtypes
import concourse.bass as bass
import concourse.mybir as mybir
from concourse.bass_utils import run_bass_kernel_spmd

F32 = mybir.dt.float32
BF16 = mybir.dt.bfloat16
AF = mybir.ActivationFunctionType
ALU = mybir.AluOpType
AX = mybir.AxisListType

S = 2048
D = 2048
NT = 16
KC = 16
DFF = 5632
NFF = 44
IN_TOTAL = 11264
OFF_RQ, OFF_RK, OFF_RV, OFF_RG, OFF_MQ, OFF_MK, OFF_MV, OFF_GR, OFF_GM = (
    0, 1024, 2048, 3072, 4096, 5120, 6144, 7168, 9216)
EPS = 1e-6
NEG = -30000.0


class Buf:
    __slots__ = ("name", "w", "r", "dsem", "dcount", "excl")

    def __init__(self, name):
        self.name = name
        self.excl = False
        self.w = {}
        self.r = {}
        self.dsem = None
        self.dcount = 0


class Eng:
    def __init__(self, name, eng, sem):
        self.name = name
        self.eng = eng
        self.sem = sem
        self.seq = 0
        self.waited = {}
        self.nwaits = 0
        self.ninst = 0


class FW:
    def __init__(self, nc, stack):
        self.nc = nc
        self.stack = stack
        self.E = {}
        for name, eng in (("pe", nc.tensor), ("act", nc.scalar), ("dve", nc.vector),
                          ("pool", nc.gpsimd), ("sp", nc.sync)):
            sem = stack.enter_context(nc.semaphore("sem_" + name))
            self.E[name] = Eng(name, eng, sem)
        self.bufs = []
        self.free_dsems = []
        self.nsem = 5

    def buf(self, name):
        b = Buf(name)
        self.bufs.append(b)
        return b

    def bufs_n(self, name, n):
        return [self.buf(f"{name}{i}") for i in range(n)]

    def sb(self, st, name, shape, dtype):
        return st.enter_context(self.nc.sbuf_tensor(name, list(shape), dtype))

    def _wait(self, e, tok):
        sem, val = tok
        k = id(sem)
        if e.waited.get(k, 0) >= val:
            return
        e.eng.wait_ge(sem, val)
        e.waited[k] = val
        e.nwaits += 1

    def _deps(self, e, reads, writes, partial):
        toks = []
        for b in reads:
            toks.extend(b.w.values())
            if b.excl:
                toks.extend(t for t in b.r.values() if t[0] is not e.sem)
        for b in writes:
            if not partial:
                toks.extend(b.w.values())
            toks.extend(b.r.values())
        for tok in toks:
            if e.name == "pe" and tok[0] is e.sem:
                continue
            self._wait(e, tok)

    def _record(self, tok, reads, writes, partial):
        k = id(tok[0])
        for b in reads:
            old = b.r.get(k)
            if old is None or old[1] < tok[1]:
                b.r[k] = tok
        for b in writes:
            if not partial:
                b.w = {}
                b.r = {}
            b.w[k] = tok

    def op(self, en, fn, reads=(), writes=(), signal=True, partial=False):
        e = self.E[en]
        self._deps(e, reads, writes, partial)
        inst = fn(e.eng)
        e.ninst += 1
        if signal:
            e.seq += 1
            inst.then_inc(e.sem, 1)
            tok = (e.sem, e.seq)
        else:
            tok = (e.sem, e.seq + 1)
        self._record(tok, reads, writes, partial)
        return inst

    def dma(self, qn, out, in_, reads=(), writes=(), partial=False, **kw):
        e = self.E[qn]
        (dst,) = writes
        self._deps(e, reads, writes, partial)
        if dst.dsem is None:
            dst.dsem = self.stack.enter_context(self.nc.semaphore("dsem_" + dst.name))
            self.nsem += 1
        inst = e.eng.dma_start(out=out, in_=in_, **kw)
        dst.dcount += 1
        inst.then_inc(dst.dsem, 16)
        tok = (dst.dsem, 16 * dst.dcount)
        e.ninst += 1
        self._record(tok, reads, writes, partial)
        return inst

    def wait_buf(self, en, b):
        for tok in list(b.w.values()):
            self._wait(self.E[en], tok)

    def barrier(self):
        toks = {}
        for e in self.E.values():
            if e.seq > 0:
                toks[id(e.sem)] = (e.sem, e.seq)
        for b in self.bufs:
            for d in (b.w, b.r):
                for k, t in d.items():
                    if k not in toks or toks[k][1] < t[1]:
                        toks[k] = t
        for e in self.E.values():
            for t in toks.values():
                if t[0] is e.sem:
                    continue
                self._wait(e, t)


def _consts():
    c = {}
    bf = ml_dtypes.bfloat16
    c["c_ident"] = np.eye(128, dtype=np.float32).astype(bf)
    c["c_ones"] = np.ones((128, 128), np.float32).astype(bf)
    ind = np.zeros((128, 8, 128), np.float32)
    for n in range(8):
        ind[n, n, :] = 1.0
    c["c_ind"] = ind.astype(bf)
    cm = np.zeros((128, 4, 512), np.float32)
    kk = np.arange(128)[:, None]
    qq = np.arange(512)[None, :]
    for r in range(4):
        cm[:, r, :] = np.where(qq >= r * 128 + kk, 0.0, NEG)
    c["c_cm"] = cm.astype(bf)
    pos = np.arange(S, dtype=np.float64)
    inv = 500000.0 ** (-np.arange(0, 32, 2, dtype=np.float64) / 32.0)
    ang = (pos[None, :].astype(np.float32) * inv[:, None].astype(np.float32)).astype(np.float32).astype(np.float64)
    c["c_mcos"] = np.concatenate([np.cos(ang), np.cos(ang)], 0).astype(np.float32)
    c["c_msin"] = np.concatenate([-np.sin(ang), np.sin(ang)], 0).astype(np.float32)
    pastneg = np.zeros((128, 16, 8), np.float32)
    past01 = np.zeros((128, 16, 8), np.float32)
    own01 = np.zeros((128, 16, 8), np.float32)
    for t in range(16):
        cb = t // 2
        for n in range(8):
            if n < cb:
                past01[:, t, n] = 1.0
            else:
                pastneg[:, t, n] = -1e30
            if n == cb:
                own01[:, t, n] = 1.0
    c["c_pastneg"] = pastneg
    c["c_past01"] = past01
    c["c_own01"] = own01
    inv_r0 = 10000.0 ** (-np.arange(0, 256, 2, dtype=np.float64) / 256.0)
    ang_r0 = (pos[None, :].astype(np.float32) * inv_r0[:, None].astype(np.float32)).astype(np.float32).astype(np.float64)
    c["c_rcos"] = np.cos(ang_r0).astype(np.float32)
    c["c_rsin"] = np.sin(ang_r0).astype(np.float32)
    rdt = np.zeros((128, 4, 128), np.float64)
    rzeta = np.zeros((128, 4), np.float64)
    repsq = np.zeros((128, 4, 512), np.float64)
    kk_ = np.arange(128, dtype=np.float64)
    for h in range(4):
        lg = np.log1p(-np.exp2(-5.0 - h))
        causal = (kk_[:, None] <= kk_[None, :])
        rdt[:, h, :] = np.where(causal, np.exp(-lg * (kk_[:, None] + 1.0)) / 16.0, 0.0)
        rzeta[:, h] = np.exp(lg * (127.0 - kk_)) / 16.0
        eq = EPS / np.exp(2.0 * lg * (kk_ + 1.0))
        repsq[:, h, :] = np.tile(eq, 4)[None, :]
    c["c_rdt"] = rdt.astype(np.float32)
    c["c_rzeta"] = rzeta.astype(np.float32)
    c["c_repsq"] = repsq.astype(np.float32)
    inv_r = 10000.0 ** (-np.arange(0, 256, 2, dtype=np.float64) / 256.0)
    ang_r = (pos[None, :].astype(np.float32) * inv_r[:, None].astype(np.float32)).astype(np.float32).astype(np.float64)
    cosr, sinr = np.cos(ang_r), np.sin(ang_r)
    tl = (np.arange(S) % 128).astype(np.float64)
    rt = np.zeros((4, 4, 128, S), np.float32)
    gch = np.zeros((4,), np.float64)
    for h in range(4):
        lg = np.log1p(-np.exp2(-5.0 - h))
        xi = np.exp(lg * (tl + 1.0))
        kz = np.exp(-lg * (tl + 1.0)) / 16.0
        rt[h, 0] = cosr * xi[None, :]
        rt[h, 1] = sinr * xi[None, :]
        rt[h, 2] = cosr * kz[None, :]
        rt[h, 3] = sinr * kz[None, :]
        gch[h] = np.exp(lg * 128.0)
    return c, gch


_CONST_CACHE = {}
LASTFW = [None]


def _get_consts():
    if "c" not in _CONST_CACHE:
        _CONST_CACHE["c"] = _consts()
    return _CONST_CACHE["c"]


def build_nc(stage=99, debug=False):
    consts, gch = _get_consts()
    nc = bass.Bass("TRN2", target_bir_lowering=False)
    dram = {}

    def din(name, shape, dt=F32):
        dram[name] = nc.dram_tensor(name, list(shape), dt, kind="ExternalInput").ap()
        return dram[name]

    x = din("x", [S, D])
    w_in = din("w_in", [D, IN_TOTAL])
    small = debug and stage <= 3
    w_ret_up = None if small else din("w_ret_up", [1024, D])
    w_moba_up = None if small else din("w_moba_up", [1024, D])
    w_out = None if small else din("w_out", [D, D])
    w_ffn_up = None if small else din("w_ffn_up", [D, 2 * DFF])
    w_ffn_down = None if small else din("w_ffn_down", [DFF, D])
    attn_nw = din("attn_norm_w", [1, D])
    ffn_nw = din("ffn_norm_w", [1, D])
    fin_nw = din("final_norm_w", [1, D])
    retw_t = din("retw_t", [128, 8])
    convw_t = din("convw_t", [128, 3 * 88])
    convb_t = din("convb_t", [128, 88])
    cd = {}
    for k, v in consts.items():
        cd[k] = din(k, v.shape, BF16 if v.dtype == ml_dtypes.bfloat16 else F32)
    out = nc.dram_tensor("out", [S, D], F32, kind="ExternalOutput").ap()
    dbg = {}

    def dout(name, shape, dt=F32):
        dbg[name] = nc.dram_tensor(name, list(shape), dt, kind="ExternalOutput").ap()
        return dbg[name]

    def dscr(name, shape, dt):
        return nc.dram_tensor(name, list(shape), dt, kind="Internal").ap()

    OM = dscr("scr_om", [1024, S], BF16)
    OR = dscr("scr_or", [1024, S], BF16)
    MG = dscr("scr_mg", [NT, 128, KC * 128], BF16)
    X2 = dscr("scr_x2", [S, D], F32)
    GT = dscr("scr_gt", [NT, 128, NFF * 128], BF16)
    X3 = dscr("scr_x3", [S, D], F32)

    with ExitStack() as top:
        fw = FW(nc, top)
        PS = [top.enter_context(nc.psum_tensor(f"ps{i}", [128, 512], F32)) for i in range(8)]
        bPS = fw.bufs_n("ps", 8)
        for b_ in bPS:
            b_.excl = True
        ident = fw.sb(top, "ident", [128, 128], BF16)
        ones = fw.sb(top, "ones", [128, 128], BF16)
        epst = fw.sb(top, "epst", [128, 1], F32)
        b_const = fw.buf("const")
        fw.dma("sp", ident[:], cd["c_ident"], writes=[b_const], partial=True)
        fw.dma("sp", ones[:], cd["c_ones"], writes=[b_const], partial=True)
        b_eps = fw.buf("eps")
        fw.op("dve", lambda e: e.memset(epst[:], EPS), writes=[b_eps])

        NW = 8
        wring = [fw.sb(top, f"wring{i}", [128, KC, 128], BF16) for i in range(NW)]
        bW = fw.bufs_n("wring", NW)
        wstate = {"n": 0}

        def wload(src_ap, col0, nk=KC):
            i = wstate["n"] % NW
            wstate["n"] += 1
            srcv = src_ap[0:nk * 128, col0:col0 + 128].rearrange("(kc p) n -> p kc n", p=128)
            fw.dma("pool", wring[i][:, 0:nk, :], srcv, writes=[bW[i]])
            return wring[i], bW[i]

        class WStream:
            def __init__(self, specs, group=1):
                self.specs = specs
                self.group = group
                self.issued = 0
                self.tiles = []

            def get(self, i, group=None, base=None):
                g = self.group if group is None else group
                b = i if base is None else base
                while self.issued < len(self.specs) and self.issued <= b + NW - g:
                    self.tiles.append(wload(*self.specs[self.issued]))
                    self.issued += 1
                return self.tiles[i]

        mspecs = []
        for h in range(8):
            mspecs += [(w_in, OFF_MQ + h * 128, KC), (w_in, OFF_MK + h * 128, KC), (w_in, OFF_MV + h * 128, KC)]
        rspecs = []
        for h in range(4):
            for off in (OFF_RQ, OFF_RK, OFF_RV, OFF_RG):
                for c in range(2):
                    rspecs.append((w_in, off + h * 256 + c * 128, KC))
        gspecs = []
        if w_ret_up is not None:
            for c in range(KC):
                gspecs += [(w_in, OFF_GR + c * 128, KC), (w_in, OFF_GM + c * 128, KC),
                           (w_ret_up, c * 128, 8), (w_moba_up, c * 128, 8)]
        nsp = [8 * 3 if stage >= 2 else 0, 32 if stage >= 3 else 0, 64 if stage >= 4 else 0]
        aspecs = mspecs[:nsp[0]] + rspecs[:nsp[1]] + gspecs[:nsp[2]]
        AWS = WStream(aspecs)
        RBASE = nsp[0]
        GBASE = nsp[0] + nsp[1]

        sh = ExitStack()
        top.callback(sh.close)
        hT = fw.sb(sh, "hT", [128, KC, S], BF16)
        bH = fw.bufs_n("hT", 4)

        psrr = {"i": 0}

        def next_bank(lo=0, n=8):
            i = lo + psrr["i"] % n
            psrr["i"] += 1
            return i

        def proj_chunk(wt, bw, evac, act_T=None, b_act=None, nk=KC, banks=(0, 4)):
            aT = hT if act_T is None else act_T
            bA = bH if b_act is None else b_act
            for j in range(4):
                bi = next_bank(*banks)
                for kc in range(nk):
                    fw.op("pe", lambda e, kc=kc, bi=bi, j=j: e.matmul(
                        PS[bi][:, :], wt[:, kc, :], aT[:, kc, j * 512:(j + 1) * 512],
                        start=(kc == 0), stop=(kc == nk - 1)),
                        reads=[bw, bA[j]], writes=[bPS[bi]], signal=(kc == nk - 1), partial=(kc > 0))
                evac(j, PS[bi], bPS[bi])

        sc2 = ExitStack()
        top.callback(sc2.close)
        ind = fw.sb(sc2, "m_ind", [128, 8, 128], BF16)
        cm = fw.sb(sc2, "m_cm", [128, 4, 512], BF16)
        mcos = fw.sb(sc2, "m_cos", [32, S], F32)
        msin = fw.sb(sc2, "m_sin", [32, S], F32)
        pastneg = fw.sb(sc2, "m_pastneg", [128, 128], F32)
        past01 = fw.sb(sc2, "m_past01", [128, 128], F32)
        own01 = fw.sb(sc2, "m_own01", [128, 128], F32)
        b_c2 = fw.buf("m_consts")
        for tile_, src in ((ind, cd["c_ind"]), (cm, cd["c_cm"]), (mcos, cd["c_mcos"]), (msin, cd["c_msin"]),
                           (pastneg, cd["c_pastneg"].rearrange("p t n -> p (t n)")),
                           (past01, cd["c_past01"].rearrange("p t n -> p (t n)")),
                           (own01, cd["c_own01"].rearrange("p t n -> p (t n)"))):
            fw.dma("sp", tile_[:], src, writes=[b_c2], partial=True)
        if stage >= 2:
            AWS.get(0)

        def rms_tile(st_name, xt_ap, b_xt, ss, sd, rstd, col, b_stat, junk, b_junk, do_recip=True):
            fw.op("act", lambda e: e.activation(out=junk[:], in_=xt_ap, func=AF.Square,
                                                accum_out=ss[:, col:col + 1]),
                  reads=[b_xt], writes=[b_junk, b_stat])
            fw.op("act", lambda e: e.activation(out=sd[:, col:col + 1], in_=ss[:, col:col + 1], func=AF.Sqrt,
                                                scale=1.0 / D, bias=epst[:, 0:1]),
                  reads=[b_stat, b_eps], writes=[b_stat])
            if do_recip:
                fw.op("dve", lambda e: e.reciprocal(rstd[:, col:col + 1], sd[:, col:col + 1]),
                      reads=[b_stat], writes=[b_stat])

        with ExitStack() as p1:
            xt = [fw.sb(p1, f"p1_xt{i}", [128, D], F32) for i in range(3)]
            bXt = fw.bufs_n("p1_xt", 3)
            xn = [fw.sb(p1, f"p1_xn{i}", [128, D], BF16) for i in range(2)]
            bXn = fw.bufs_n("p1_xn", 2)
            junk = fw.sb(p1, "p1_junk", [128, D], BF16)
            b_junk = fw.buf("p1_junk")
            wbc = fw.sb(p1, "p1_wbc", [128, D], F32)
            b_wbc = fw.buf("p1_wbc")
            ss = fw.sb(p1, "p1_ss", [128, NT], F32)
            sd = fw.sb(p1, "p1_sd", [128, NT], F32)
            rstd = fw.sb(p1, "p1_rstd", [128, NT], F32)
            bSt = fw.bufs_n("p1_st", NT)
            fw.dma("sp", wbc[:], attn_nw.partition_broadcast(128), writes=[b_wbc])
            def stats1(t):
                s = t % 3
                fw.dma("sp", xt[s][:], x[t * 128:(t + 1) * 128, :], writes=[bXt[s]])
                rms_tile("p1", xt[s][:], bXt[s], ss, sd, rstd, t, bSt[t], junk, b_junk, do_recip=False)

            def norm1(t):
                s = t % 2
                s3 = t % 3
                fw.op("dve", lambda e: e.reciprocal(rstd[:, t:t + 1], sd[:, t:t + 1]),
                      reads=[bSt[t]], writes=[bSt[t]])
                fw.op("dve", lambda e: e.scalar_tensor_tensor(
                    out=xn[s][:], in0=xt[s3][:], scalar=rstd[:, t:t + 1], in1=wbc[:],
                    op0=ALU.mult, op1=ALU.mult),
                    reads=[bXt[s3], bSt[t], b_wbc], writes=[bXn[s]])
                ba, bb = (0, 1) if s == 0 else (2, 3)
                for half, bi in ((0, ba), (1, bb)):
                    pv = PS[bi][:].bitcast(BF16)
                    for c8 in range(8):
                        c = half * 8 + c8
                        fw.op("pe", lambda e, c=c, c8=c8, pv=pv: e.transpose(
                            pv[:, c8 * 128:(c8 + 1) * 128], xn[s][:, c * 128:(c + 1) * 128], ident[:]),
                            reads=[bXn[s], b_const], writes=[bPS[bi]], signal=(c8 == 7), partial=(c8 > 0))
                    if half == 0:
                        fw.op("act", lambda e, half=half, pv=pv: e.activation(
                            out=hT[:, half * 8:(half + 1) * 8, t * 128:(t + 1) * 128],
                            in_=pv.rearrange("p (c n) -> p c n", n=128), func=AF.Copy),
                            reads=[bPS[bi]], writes=[bH[t // 4]], partial=True)
                    else:
                        fw.op("dve", lambda e, half=half, pv=pv: e.tensor_copy(
                            out=hT[:, half * 8:(half + 1) * 8, t * 128:(t + 1) * 128],
                            in_=pv.rearrange("p (c n) -> p c n", n=128)),
                            reads=[bPS[bi]], writes=[bH[t // 4]], partial=True)
            stats1(0)
            for t in range(NT):
                if t + 1 < NT:
                    stats1(t + 1)
                norm1(t)
            fw.barrier()

        if debug and stage == 1:
            d_hT = dout("d_hT", [D, S], BF16)
            b_d = fw.buf("d_hT")
            fw.dma("sp", d_hT.rearrange("(c p) s -> p c s", p=128), hT[:], reads=bH, writes=[b_d])
            fw.wait_buf("sp", b_d)
            return nc, dram, dbg


        def mm(out_ap, lhsT, rhs, start, stop, reads, wbuf, signal, partial):
            fw.op("pe", lambda e: e.matmul(out_ap, lhsT, rhs, start=start, stop=stop),
                  reads=reads, writes=[wbuf], signal=signal, partial=partial)

        if debug:
            d_om = dout("d_om", [1024, S], BF16)
            b_dom = fw.bufs_n("d_om", 2)
        with ExitStack() as p2:
            SCALE = 128.0 ** -0.5
            qT = [fw.sb(p2, f"m_qT{i}", [128, S], BF16) for i in range(2)]
            kT = [fw.sb(p2, f"m_kT{i}", [128, S], BF16) for i in range(2)]
            bQ = fw.bufs_n("m_qT", 2)
            bK = fw.bufs_n("m_kT", 2)
            vT = fw.sb(p2, "m_vT", [128, S], BF16)
            b_vT = fw.buf("m_vT")
            vtok = [fw.sb(p2, f"m_vtok{i}", [128, NT, 128], BF16) for i in range(2)]
            bV = fw.bufs_n("m_vtok", 2)
            rr = fw.sb(p2, "m_rr", [32, S], F32)
            rp = fw.sb(p2, "m_rp", [32, S], F32)
            b_rr = fw.buf("m_rr")
            b_rp = fw.buf("m_rp")
            biasfull = fw.sb(p2, "m_biasfull", [128, NT, 128], BF16)
            b_bf = fw.buf("m_biasfull")
            biasT = fw.sb(p2, "m_biasT", [128, S], BF16)
            b_bT = fw.buf("m_biasT")
            Es = [fw.sb(p2, f"m_E{i}", [128, 512], BF16) for i in range(3)]
            bE = fw.bufs_n("m_E", 3)
            rec = fw.sb(p2, "m_rec", [128, 512], F32)
            b_rec = fw.buf("m_rec")
            dsb = fw.sb(p2, "m_dsb", [128, 512], F32)
            b_dsb = fw.buf("m_dsb")
            osb = fw.sb(p2, "m_osb", [128, 512], F32)
            b_osb = fw.buf("m_osb")
            oout = [fw.sb(p2, f"m_oout{i}", [128, S], BF16) for i in range(2)]
            bO = fw.bufs_n("m_oout", 2)
            km32 = fw.sb(p2, "m_km32", [128, 8], F32)
            kmb = [fw.sb(p2, f"m_kmb{i}", [128, 8], BF16) for i in range(2)]
            b_km = fw.buf("m_km32")
            bKm = fw.bufs_n("m_kmb", 2)
            s1 = fw.sb(p2, "m_s1", [128, 128], F32)
            cnt = fw.sb(p2, "m_cnt", [128, 128], F32)
            cmpt = fw.sb(p2, "m_cmp", [128, 128], F32)
            b_sel = fw.buf("m_sel")
            fw.op("dve", lambda e: e.memset(biasfull[:], 0.0), writes=[b_bf])
            fw.op("dve", lambda e: e.memset(biasT[:], 0.0), writes=[b_bT])

            class _M:
                def get(self, i):
                    return AWS.get(i, group=1)
            mws = _M()

            def v3(ap2d):
                return ap2d.rearrange("p (t n) -> p t n", n=8)

            def prepA(h):
                s = h % 2
                for which, dstT, bD in ((0, qT[s], bQ[s]), (1, kT[s], bK[s])):
                    wt, bw = mws.get(3 * h + which)

                    def evac(j, ps, bps, dstT=dstT, bD=bD):
                        fw.op("act", lambda e: e.activation(out=dstT[:, j * 512:(j + 1) * 512],
                                                            in_=ps[:, :], func=AF.Copy),
                              reads=[bps], writes=[bD], partial=True)
                        fw.op("dve", lambda e: e.tensor_copy(out=rr[:, j * 512:(j + 1) * 512], in_=ps[0:32, :]),
                              reads=[bps], writes=[b_rr], partial=(j > 0))
                    proj_chunk(wt, bw, evac)
                    fw.dma("sp", rp[0:16, :], rr[16:32, :], reads=[b_rr], writes=[b_rp])
                    fw.dma("sp", rp[16:32, :], rr[0:16, :], reads=[b_rr], writes=[b_rp], partial=True)
                    fw.op("dve", lambda e: e.tensor_tensor(out=rr[:], in0=rr[:], in1=mcos[:], op=ALU.mult),
                          reads=[b_rr, b_rp, b_c2], writes=[b_rr])
                    fw.op("dve", lambda e: e.tensor_tensor(out=rp[:], in0=rp[:], in1=msin[:], op=ALU.mult),
                          reads=[b_rp, b_c2], writes=[b_rp])
                    fw.op("dve", lambda e, dstT=dstT: e.tensor_tensor(out=dstT[0:32, :], in0=rr[:], in1=rp[:], op=ALU.add),
                          reads=[b_rr, b_rp], writes=[bD], partial=False)
                fw.op("dve", lambda e: e.tensor_reduce(out=km32[:, 0:8],
                                                       in_=kT[s][:, :].rearrange("p (n s) -> p n s", s=256),
                                                       axis=AX.X, op=ALU.add),
                      reads=[bK[s]], writes=[b_km])
                fw.op("act", lambda e: e.activation(out=kmb[s][:], in_=km32[:], func=AF.Copy, scale=1.0 / 256.0),
                      reads=[b_km], writes=[bKm[s]])
                wt, bw = mws.get(3 * h + 2)

                def evac_v(j, ps, bps):
                    fw.op("act", lambda e: e.activation(out=vT[:, j * 512:(j + 1) * 512], in_=ps[:, :], func=AF.Copy),
                          reads=[bps], writes=[b_vT], partial=(j > 0))
                proj_chunk(wt, bw, evac_v)
                for half in range(2):
                    bi = next_bank(0, 4)
                    pv = PS[bi][:].bitcast(BF16)
                    for t8 in range(8):
                        t = half * 8 + t8
                        fw.op("pe", lambda e, t=t, t8=t8, pv=pv: e.transpose(
                            pv[:, t8 * 128:(t8 + 1) * 128], vT[:, t * 128:(t + 1) * 128], ident[:]),
                            reads=[b_vT, b_const], writes=[bPS[bi]], signal=(t8 == 7), partial=(t8 > 0))
                    fw.op("dve", lambda e, half=half, pv=pv: e.tensor_copy(
                        out=vtok[s][:, half * 8:(half + 1) * 8, :], in_=pv.rearrange("p (c n) -> p c n", n=128)),
                        reads=[bPS[bi]], writes=[bV[s]], partial=(half > 0))

            def bscore(h):
                s = h % 2
                bi = next_bank(0, 4)
                for t in range(NT):
                    mm(PS[bi][:, t * 8:(t + 1) * 8], qT[s][:, t * 128:(t + 1) * 128], kmb[s][:, 0:8], True, True,
                       [bQ[s], bKm[s]], bPS[bi], t == NT - 1, t > 0)
                D_ = "dve"
                fw.op(D_, lambda e: e.tensor_tensor(out=s1[:], in0=PS[bi][:, 0:128], in1=pastneg[:], op=ALU.add),
                      reads=[bPS[bi], b_c2], writes=[b_sel])
                for m in range(8):
                    dst = cnt if m == 0 else cmpt
                    fw.op(D_, lambda e, m=m, dst=dst: e.tensor_tensor(
                        out=v3(dst[:]), in0=v3(s1[:])[:, :, m:m + 1].to_broadcast([128, NT, 8]), in1=v3(s1[:]),
                        op=ALU.is_gt), reads=[b_sel], writes=[b_sel])
                    if m > 0:
                        fw.op(D_, lambda e: e.tensor_tensor(out=cnt[:], in0=cnt[:], in1=cmpt[:], op=ALU.add),
                              reads=[b_sel], writes=[b_sel])
                fw.op(D_, lambda e: e.tensor_scalar(out=cnt[:], in0=cnt[:], scalar1=3.0, scalar2=None, op0=ALU.is_lt),
                      reads=[b_sel], writes=[b_sel])
                fw.op(D_, lambda e: e.tensor_tensor(out=cnt[:], in0=cnt[:], in1=past01[:], op=ALU.mult),
                      reads=[b_sel, b_c2], writes=[b_sel])
                fw.op(D_, lambda e: e.tensor_tensor(out=cnt[:], in0=cnt[:], in1=own01[:], op=ALU.add),
                      reads=[b_sel, b_c2], writes=[b_sel])
                fw.op(D_, lambda e: e.tensor_scalar(out=biasfull[:, :, 0:8], in0=v3(cnt[:]), scalar1=-1.0, scalar2=-NEG,
                                                    op0=ALU.add, op1=ALU.mult),
                      reads=[b_sel], writes=[b_bf])

            def biasTr(h):
                for g in range(4):
                    bi = next_bank(0, 4)
                    for t4 in range(4):
                        t = g * 4 + t4
                        mm(PS[bi][:, t4 * 128:(t4 + 1) * 128], biasfull[:, t, :], ident[:], True, True,
                           [b_bf, b_const], bPS[bi], t4 == 3, t4 > 0)
                    fw.op("act", lambda e, g=g, bi=bi: e.activation(out=biasT[0:8, g * 512:(g + 1) * 512],
                                                                    in_=PS[bi][0:8, :], func=AF.Copy),
                          reads=[bPS[bi]], writes=[b_bT], partial=(g > 0))

            def att(h):
                s = h % 2
                pairs = [(j, i) for j in range(4) for i in range(4 * (j + 1))]

                def qcols(p):
                    j, i = pairs[p]
                    return (256, 512) if i - 4 * j >= 2 else (0, 512)

                def emitS(p):
                    j, i = pairs[p]
                    sbk = 4 + (p % 2)
                    diag = i >= 4 * j
                    c0, c1 = qcols(p)
                    mm(PS[sbk][:, c0:c1], kT[s][:, i * 128:(i + 1) * 128], qT[s][:, j * 512 + c0:j * 512 + c1], True, False,
                       [bK[s], bQ[s]], bPS[sbk], False, False)
                    mm(PS[sbk][:, c0:c1], ind[:, i // 2, :], biasT[:, j * 512 + c0:j * 512 + c1], False, not diag,
                       [b_c2, b_bT], bPS[sbk], not diag, True)
                    if diag:
                        mm(PS[sbk][:, c0:c1], ident[:], cm[:, i - 4 * j, c0:c1], False, True,
                           [b_c2, b_const], bPS[sbk], True, True)
                    fw.op("act", lambda e: e.activation(out=Es[p % 3][:, c0:c1], in_=PS[sbk][:, c0:c1], func=AF.Exp,
                                                        scale=SCALE),
                          reads=[bPS[sbk]], writes=[bE[p % 3]])

                def emitOD(p):
                    j, i = pairs[p]
                    ni = 4 * (j + 1)
                    bo, bd = 6, 7
                    c0, c1 = qcols(p)
                    mm(PS[bo][:, c0:c1], vtok[s][:, i, :], Es[p % 3][:, c0:c1], i == 0, i == ni - 1,
                       [bV[s], bE[p % 3]], bPS[bo], False, i > 0)
                    mm(PS[bd][:, c0:c1], ones[:], Es[p % 3][:, c0:c1], i == 0, i == ni - 1,
                       [b_const, bE[p % 3]], bPS[bd], True, i > 0)
                    if i == ni - 1:
                        fw.op("act", lambda e: e.activation(out=dsb[:], in_=PS[bd][:, :], func=AF.Copy),
                              reads=[bPS[bd]], writes=[b_dsb])
                        fw.op("dve", lambda e: e.tensor_copy(out=osb[:], in_=PS[bo][:, :]),
                              reads=[bPS[bo]], writes=[b_osb])
                        fw.op("dve", lambda e: e.reciprocal(rec[:], dsb[:]), reads=[b_dsb], writes=[b_rec])
                        fw.op("dve", lambda e: e.tensor_tensor(out=oout[s][:, j * 512:(j + 1) * 512], in0=osb[:],
                                                               in1=rec[:], op=ALU.mult),
                              reads=[b_osb, b_rec], writes=[bO[s]], partial=(j > 0))
                emitS(0)
                for p in range(len(pairs)):
                    if p + 1 < len(pairs):
                        emitS(p + 1)
                    emitOD(p)
                fw.dma("sp", OM[h * 128:(h + 1) * 128, :], oout[s][:], reads=[bO[s]], writes=[b_OM[s]], partial=True)
                if debug:
                    fw.dma("sp", d_om[h * 128:(h + 1) * 128, :], oout[s][:], reads=[bO[s]], writes=[b_dom[s]], partial=True)

            b_OM = fw.bufs_n("scr_om", 2)
            NH = 8 if stage >= 2 else 0
            import os as _os
            sub = int(_os.environ.get("K_SUB", "0")) if debug else 0
            if sub:
                NH = 0
                d_q = dout("d_q", [128, S], BF16)
                d_k = dout("d_k", [128, S], BF16)
                d_v = dout("d_v", [128, NT * 128], BF16)
                d_b = dout("d_b", [128, S], BF16)
                b_dq = fw.bufs_n("d_sub", 4)
                prepA(0)
                if sub >= 2:
                    bscore(0)
                if sub >= 3:
                    biasTr(0)
                if sub >= 4:
                    att(0)
                fw.dma("sp", d_q, qT[0][:], reads=[bQ[0]], writes=[b_dq[0]])
                fw.dma("sp", d_k, kT[0][:], reads=[bK[0]], writes=[b_dq[1]])
                fw.dma("sp", d_v, vtok[0][:].rearrange("p a b -> p (a b)"), reads=[bV[0]], writes=[b_dq[2]])
                fw.dma("sp", d_b, biasT[:], reads=[b_bT, b_bf], writes=[b_dq[3]])
                for b_ in b_dq:
                    fw.wait_buf("sp", b_)
                if sub >= 4:
                    fw.wait_buf("sp", b_dom[0])
                return nc, dram, dbg
            if NH:
                prepA(0)
                bscore(0)
                if NH > 1:
                    prepA(1)
                biasTr(0)
                for h in range(NH):
                    att(h)
                    if h + 1 < NH:
                        bscore(h + 1)
                        if h + 2 < NH:
                            prepA(h + 2)
                        biasTr(h + 1)
            fw.barrier()
        sc2.close()

        if debug and stage == 2:
            fw.wait_buf("sp", b_dom[0])
            fw.wait_buf("sp", b_dom[1])
            return nc, dram, dbg


        if debug:
            d_or = dout("d_or", [1024, S], BF16)
            b_dor = fw.buf("d_or")
        b_OR = fw.buf("scr_or")
        with ExitStack() as p3:
            rcos = fw.sb(p3, "r_cos", [128, S], F32)
            rsin = fw.sb(p3, "r_sin", [128, S], F32)
            rdt = fw.sb(p3, "r_dt", [128, 4, 128], F32)
            rzeta = fw.sb(p3, "r_zeta", [128, 4], F32)
            repsq = fw.sb(p3, "r_epsq", [128, 4, 512], F32)
            retw = fw.sb(p3, "r_retw", [128, 8], F32)
            b_c3 = fw.buf("r_consts")
            for tile_, src in ((rcos, cd["c_rcos"]), (rsin, cd["c_rsin"]), (rdt, cd["c_rdt"]), (rzeta, cd["c_rzeta"]),
                               (repsq, cd["c_repsq"]), (retw, dram["retw_t"])):
                fw.dma("sp", tile_[:], src, writes=[b_c3], partial=True)
            rqT = fw.sb(p3, "r_qT", [128, 2, S], BF16)
            rkT = fw.sb(p3, "r_kT", [128, 2, S], BF16)
            rvT = fw.sb(p3, "r_vT", [128, 2, S], BF16)
            b_rq, b_rk, b_rv = fw.buf("r_qT"), fw.buf("r_kT"), fw.buf("r_vT")
            kTok = fw.sb(p3, "r_kTok", [128, NT, 256], BF16)
            rvtok = fw.sb(p3, "r_vtok", [128, NT, 256], BF16)
            b_kTok, b_rvtok = fw.buf("r_kTok"), fw.buf("r_vtok")
            rgs = fw.sb(p3, "r_gs", [128, 2, S], BF16)
            b_rgs = fw.buf("r_gs")
            oraw = fw.sb(p3, "r_oraw", [128, 2, S], F32)
            b_oraw = fw.buf("r_oraw")
            sq, b_sq = rvT, b_rv
            orT, b_orT = rkT, b_rk
            tmps = [fw.sb(p3, f"r_tmp{i}", [128, 512], F32) for i in range(4)]
            bT = fw.bufs_n("r_tmp", 4)
            Wst = fw.sb(p3, "r_W", [128, 512], F32)
            b_W = fw.buf("r_W")
            Wb = [fw.sb(p3, f"r_Wb{i}", [128, 512], BF16) for i in range(2)]
            bWb = fw.bufs_n("r_Wb", 2)
            PT = [fw.sb(p3, f"r_PT{i}", [128, 128], BF16) for i in range(2)]
            bPT = fw.bufs_n("r_PT", 2)
            nrm = fw.sb(p3, "r_nrm", [128, 512], F32)
            b_nrm = fw.buf("r_nrm")

            class _R:
                def get(self, i):
                    return AWS.get(RBASE + i, group=2, base=RBASE + (i // 2) * 2)
            rws = _R()

            def proj_pair(i0, evac2):
                (wt0, bw0), (wt1, bw1) = rws.get(i0), rws.get(i0 + 1)
                for j in range(4):
                    ba = next_bank(0, 4)
                    bb = next_bank(0, 4)
                    for wt, bw, bi in ((wt0, bw0, ba), (wt1, bw1, bb)):
                        for kc in range(KC):
                            fw.op("pe", lambda e, kc=kc, bi=bi, j=j, wt=wt: e.matmul(
                                PS[bi][:, :], wt[:, kc, :], hT[:, kc, j * 512:(j + 1) * 512],
                                start=(kc == 0), stop=(kc == KC - 1)),
                                reads=[bw, bH[j]], writes=[bPS[bi]], signal=(kc == KC - 1), partial=(kc > 0))
                    evac2(j, ba, bb)

            def rot_evac(dstT, bD):
                def f(j, ba, bb):
                    sl = slice(j * 512, (j + 1) * 512)
                    TT = lambda o, a, b_, op_, rd, wr, part=False: fw.op(
                        "dve", lambda e: e.tensor_tensor(out=o, in0=a, in1=b_, op=op_), reads=rd, writes=wr, partial=part)
                    TT(tmps[0][:], PS[ba][:, :], rcos[:, sl], ALU.mult, [bPS[ba], b_c3], [bT[0]])
                    TT(tmps[1][:], PS[bb][:, :], rsin[:, sl], ALU.mult, [bPS[bb], b_c3], [bT[1]])
                    TT(tmps[2][:], PS[bb][:, :], rcos[:, sl], ALU.mult, [bPS[bb], b_c3], [bT[2]])
                    TT(tmps[3][:], PS[ba][:, :], rsin[:, sl], ALU.mult, [bPS[ba], b_c3], [bT[3]])
                    TT(dstT[:, 0, sl], tmps[0][:], tmps[1][:], ALU.subtract, [bT[0], bT[1]], [bD], part=True)
                    TT(dstT[:, 1, sl], tmps[2][:], tmps[3][:], ALU.add, [bT[2], bT[3]], [bD], part=True)
                return f

            def copy_evac(dstT, bD, func):
                def f(j, ba, bb):
                    sl = slice(j * 512, (j + 1) * 512)
                    for c, bi in ((0, ba), (1, bb)):
                        fw.op("act", lambda e, c=c, bi=bi: e.activation(out=dstT[:, c, sl], in_=PS[bi][:, :], func=func),
                              reads=[bPS[bi]], writes=[bD], partial=True)
                return f

            def to_tok(srcT, bS, dst, bDst, h, scaled):
                for g in range(4):
                    bi = next_bank(0, 4)
                    pv = PS[bi][:].bitcast(BF16)
                    for t4 in range(4):
                        t = g * 4 + t4
                        for c in range(2):
                            k8 = t4 * 2 + c
                            fw.op("pe", lambda e, t=t, c=c, k8=k8, pv=pv: e.transpose(
                                pv[:, k8 * 128:(k8 + 1) * 128], srcT[:, c, t * 128:(t + 1) * 128], ident[:]),
                                reads=[bS, b_const], writes=[bPS[bi]], signal=(k8 == 7), partial=(k8 > 0))
                    if scaled:
                        fw.op("act", lambda e, g=g, pv=pv: e.activation(
                            out=dst[:, g * 4:(g + 1) * 4, :], in_=pv.rearrange("p (t n) -> p t n", n=256),
                            func=AF.Copy, scale=rzeta[:, h:h + 1]),
                            reads=[bPS[bi], b_c3], writes=[bDst], partial=(g > 0))
                    else:
                        fw.op("dve", lambda e, g=g, pv=pv: e.tensor_copy(
                            out=dst[:, g * 4:(g + 1) * 4, :], in_=pv.rearrange("p (t n) -> p t n", n=256)),
                            reads=[bPS[bi]], writes=[bDst], partial=(g > 0))

            for h in range(4 if stage >= 3 else 0):
                gC = float(gch[h])
                proj_pair(8 * h + 0, rot_evac(rqT, b_rq))
                proj_pair(8 * h + 2, rot_evac(rkT, b_rk))
                proj_pair(8 * h + 4, copy_evac(rvT, b_rv, AF.Copy))
                proj_pair(8 * h + 6, copy_evac(rgs, b_rgs, AF.Silu))
                to_tok(rkT, b_rk, kTok, b_kTok, h, True)
                to_tok(rvT, b_rv, rvtok, b_rvtok, h, False)

                def emitA(n):
                    st = n % 2
                    cs = slice(n * 128, (n + 1) * 128)
                    for c in range(2):
                        mm(PS[st][:, 0:128], rkT[:, c, cs], rqT[:, c, cs], c == 0, c == 1,
                           [b_rk, b_rq], bPS[st], c == 1, c > 0)
                    fw.op("dve", lambda e: e.tensor_tensor(out=PT[st][:], in0=PS[st][:, 0:128], in1=rdt[:, h, :],
                                                           op=ALU.mult),
                          reads=[bPS[st], b_c3], writes=[bPT[st]])
                    if n < NT - 1:
                        for c in range(2):
                            mm(PS[2 + st][:, c * 256:(c + 1) * 256], kTok[:, n, c * 128:(c + 1) * 128], rvtok[:, n, :],
                               True, True, [b_kTok, b_rvtok], bPS[2 + st], c == 1, c > 0)

                def emitB(n):
                    st = n % 2
                    ob = 4 + st
                    cs = slice(n * 128, (n + 1) * 128)
                    if n < NT - 1:
                        if n == 0:
                            fw.op("dve", lambda e: e.tensor_copy(out=Wst[:], in_=PS[2 + st][:, :]),
                                  reads=[bPS[2 + st]], writes=[b_W])
                        else:
                            fw.op("dve", lambda e: e.scalar_tensor_tensor(
                                out=Wst[:], in0=Wst[:], scalar=gC, in1=PS[2 + st][:, :], op0=ALU.mult, op1=ALU.add),
                                reads=[b_W, bPS[2 + st]], writes=[b_W])
                        fw.op("act", lambda e: e.activation(out=Wb[(n + 1) % 2][:], in_=Wst[:], func=AF.Copy),
                              reads=[b_W], writes=[bWb[(n + 1) % 2]])
                    for ec in range(2):
                        reg = PS[ob][:, ec * 128:(ec + 1) * 128]
                        last_inner = (n == 0)
                        mm(reg, rvtok[:, n, ec * 128:(ec + 1) * 128], PT[st][:], True, last_inner,
                           [b_rvtok, bPT[st]], bPS[ob], last_inner and ec == 1, ec > 0)
                        if n > 0:
                            for c in range(2):
                                mm(reg, Wb[st][:, c * 256 + ec * 128: c * 256 + (ec + 1) * 128], rqT[:, c, cs],
                                   False, c == 1, [bWb[st], b_rq], bPS[ob], c == 1 and ec == 1, True)
                    fw.op("act", lambda e: e.activation(out=oraw[:, :, cs],
                                                        in_=PS[ob][:, 0:256].rearrange("p (c n) -> p c n", n=128),
                                                        func=AF.Copy),
                          reads=[bPS[ob]], writes=[b_oraw], partial=(n > 0))
                emitA(0)
                for n in range(NT):
                    if n + 1 < NT:
                        emitA(n + 1)
                    emitB(n)
                fw.op("act", lambda e: e.activation(out=sq[:], in_=oraw[:], func=AF.Square),
                      reads=[b_oraw], writes=[b_sq])
                for j in range(4):
                    sl = slice(j * 512, (j + 1) * 512)
                    bi = next_bank(0, 4)
                    for c in range(2):
                        mm(PS[bi][:, :], ones[:], sq[:, c, sl], c == 0, c == 1, [b_const, b_sq], bPS[bi], c == 1, c > 0)
                    fw.op("dve", lambda e: e.scalar_tensor_tensor(out=nrm[:], in0=PS[bi][:, :], scalar=1.0 / 256.0,
                                                                  in1=repsq[:, h, :], op0=ALU.mult, op1=ALU.add),
                          reads=[bPS[bi], b_c3], writes=[b_nrm])
                    fw.op("act", lambda e: e.activation(out=nrm[:], in_=nrm[:], func=AF.Sqrt),
                          reads=[b_nrm], writes=[b_nrm])
                    fw.op("dve", lambda e: e.reciprocal(nrm[:], nrm[:]), reads=[b_nrm], writes=[b_nrm])
                    for ec in range(2):
                        ch = h * 2 + ec
                        fw.op("dve", lambda e, ec=ec, ch=ch: e.scalar_tensor_tensor(
                            out=tmps[ec][:], in0=oraw[:, ec, sl], scalar=retw[:, ch:ch + 1], in1=nrm[:],
                            op0=ALU.mult, op1=ALU.mult),
                            reads=[b_oraw, b_c3, b_nrm], writes=[bT[ec]])
                        fw.op("dve", lambda e, ec=ec: e.tensor_tensor(out=orT[:, ec, sl], in0=tmps[ec][:],
                                                                      in1=rgs[:, ec, sl], op=ALU.mult),
                              reads=[bT[ec], b_rgs], writes=[b_orT], partial=(j > 0 or ec > 0))
                for ec in range(2):
                    ch = h * 2 + ec
                    fw.dma("sp", OR[ch * 128:(ch + 1) * 128, :], orT[:, ec, :], reads=[b_orT], writes=[b_OR], partial=True)
                    if debug:
                        fw.dma("sp", d_or[ch * 128:(ch + 1) * 128, :], orT[:, ec, :], reads=[b_orT], writes=[b_dor],
                               partial=True)
            fw.barrier()

        if debug and stage == 3:
            fw.wait_buf("sp", b_dor)
            return nc, dram, dbg


        if debug and stage == 4:
            b_dmg = fw.buf("d_mg")
        b_MG = fw.bufs_n("scr_mg", 2)
        with ExitStack() as p4:
            orA = fw.sb(p4, "g_orA", [128, 8, S], BF16)
            omA = fw.sb(p4, "g_omA", [128, 8, S], BF16)
            b_orA, b_omA = fw.buf("g_orA"), fw.buf("g_omA")
            fw.dma("sp", omA[:], OM.rearrange("(c p) s -> p c s", p=128), reads=b_OM, writes=[b_omA])
            fw.dma("sp", orA[:], OR.rearrange("(c p) s -> p c s", p=128), reads=[b_OR], writes=[b_orA])
            sg = [fw.sb(p4, f"g_sg{i}", [128, 512], F32) for i in range(4)]
            bSg = fw.bufs_n("g_sg", 4)
            tt_ = [fw.sb(p4, f"g_tt{i}", [128, 512], F32) for i in range(4)]
            bTt = fw.bufs_n("g_tt", 4)
            mgc = [fw.sb(p4, f"g_mgc{i}", [128, S], BF16) for i in range(2)]
            bMgc = fw.bufs_n("g_mgc", 2)
            class _G:
                def get(self, i):
                    return AWS.get(GBASE + i, group=4, base=GBASE + (i // 4) * 4)
            gws = _G()
            it = 0
            srcs = ((hT, None, KC), (hT, None, KC), (orA, b_orA, 8), (omA, b_omA, 8))

            def grp(ws_, i, j, bi):
                aT, bA, nk = srcs[i]
                bA = bH[j] if bA is None else bA
                wt, bw = ws_[i]
                sl = slice(j * 512, (j + 1) * 512)
                for kc in range(nk):
                    mm(PS[bi][:, :], wt[:, kc, :], aT[:, kc, sl], kc == 0, kc == nk - 1,
                       [bw, bA], bPS[bi], kc == nk - 1, kc > 0)

            def combine(mc, bmc, j, bg0, bg1, bu0, bu1, k2):
                sl = slice(j * 512, (j + 1) * 512)
                for i, bg in enumerate((bg0, bg1)):
                    fw.op("act", lambda e, i=i, bg=bg: e.activation(out=sg[k2 + i][:], in_=PS[bg][:, :], func=AF.Sigmoid),
                          reads=[bPS[bg]], writes=[bSg[k2 + i]])
                for i, bu in enumerate((bu0, bu1)):
                    fw.op("dve", lambda e, i=i, bu=bu: e.tensor_tensor(out=tt_[k2 + i][:], in0=PS[bu][:, :],
                                                                       in1=sg[k2 + i][:], op=ALU.mult),
                          reads=[bPS[bu], bSg[k2 + i]], writes=[bTt[k2 + i]])
                fw.op("dve", lambda e: e.tensor_tensor(out=mc[:, sl], in0=tt_[k2][:], in1=tt_[k2 + 1][:], op=ALU.add),
                      reads=[bTt[k2], bTt[k2 + 1]], writes=[bmc], partial=(j > 0))
            for c in range(KC if stage >= 4 else 0):
                ws_ = [gws.get(4 * c + i) for i in range(4)]
                mc = mgc[c % 2]
                if c == 0:
                    for j in range(4):
                        grp(ws_, 0, j, 2 * j)
                        grp(ws_, 1, j, 2 * j + 1)
                    sgx = [fw.sb(p4, f"g_sgx{i}", [128, 512], F32) for i in range(8)]
                    bSgx = fw.bufs_n("g_sgx", 8)
                    for b_ in range(8):
                        fw.op("act", lambda e, b_=b_: e.activation(out=sgx[b_][:], in_=PS[b_][:, :], func=AF.Sigmoid),
                              reads=[bPS[b_]], writes=[bSgx[b_]])
                    for j in range(4):
                        sl = slice(j * 512, (j + 1) * 512)
                        bu0, bu1 = 2 * (j % 2), 2 * (j % 2) + 1
                        grp(ws_, 2, j, bu0)
                        grp(ws_, 3, j, bu1)
                        k2 = 2 * (j % 2)
                        for i, bu in enumerate((bu0, bu1)):
                            fw.op("dve", lambda e, i=i, bu=bu: e.tensor_tensor(
                                out=tt_[k2 + i][:], in0=PS[bu][:, :], in1=sgx[2 * j + i][:], op=ALU.mult),
                                reads=[bPS[bu], bSgx[2 * j + i]], writes=[bTt[k2 + i]])
                        fw.op("dve", lambda e: e.tensor_tensor(out=mc[:, sl], in0=tt_[k2][:], in1=tt_[k2 + 1][:],
                                                               op=ALU.add),
                              reads=[bTt[k2], bTt[k2 + 1]], writes=[bMgc[c % 2]], partial=(j > 0))
                else:
                    for j in range(4):
                        base = 4 * (it % 2)
                        it += 1
                        for i in range(4):
                            grp(ws_, i, j, base + i)
                        combine(mc, bMgc[c % 2], j, base, base + 1, base + 2, base + 3, 2 * (it % 2))
                fw.dma("sp", MG[:, :, c * 128:(c + 1) * 128].rearrange("t p n -> p t n"),
                       mc[:, :].rearrange("p (t n) -> p t n", n=128), reads=[bMgc[c % 2]], writes=[b_MG[c % 2]], partial=True)
                if debug and stage == 4:
                    pass
            fw.barrier()
        sh.close()
        fspecs = []
        if w_ffn_up is not None and stage >= 6:
            for c in range(NFF):
                fspecs += [(w_ffn_up, c * 128, KC), (w_ffn_up, DFF + c * 128, KC)]
        fws = WStream(fspecs)

        if debug and stage == 4:
            with ExitStack() as pd:
                dt_ = fw.sb(pd, "dbg_t", [128, KC * 128], BF16)
                b_dt = fw.buf("dbg_t")
                d_mg2 = dout("d_mg2", [NT, 128, KC * 128], BF16)
                for t in range(NT):
                    fw.dma("sp", dt_[:], MG[t], reads=b_MG, writes=[b_dt])
                    fw.dma("sp", d_mg2[t], dt_[:], reads=[b_dt], writes=[b_dmg], partial=True)
                fw.wait_buf("sp", b_dmg)
            return nc, dram, dbg


        sh2 = ExitStack()
        top.callback(sh2.close)
        h2T = fw.sb(sh2, "h2T", [128, KC, S], BF16)
        bH2 = fw.bufs_n("h2T", 4)
        with ExitStack() as p5:
            wout = fw.sb(p5, "o_wout", [128, KC, D], BF16)
            bWo = fw.bufs_n("o_wout", 4)
            for cg in range(4):
                fw.dma("pool", wout[:, :, cg * 512:(cg + 1) * 512],
                       w_out[:, cg * 512:(cg + 1) * 512].rearrange("(kc p) n -> p kc n", p=128), writes=[bWo[cg]])
            if fspecs:
                fws.get(0)
            wbc2 = fw.sb(p5, "o_wbc", [128, D], F32)
            b_wbc2 = fw.buf("o_wbc")
            fw.dma("sp", wbc2[:], ffn_nw.partition_broadcast(128), writes=[b_wbc2])
            mgt = [fw.sb(p5, f"o_mgt{i}", [128, KC * 128], BF16) for i in range(2)]
            bMgt = fw.bufs_n("o_mgt", 2)
            xb5 = [fw.sb(p5, f"o_xb{i}", [128, D], F32) for i in range(2)]
            bXb5 = fw.bufs_n("o_xb", 2)
            xn5 = fw.sb(p5, "o_xn", [128, D], BF16)
            b_xn5 = fw.buf("o_xn")
            ss5 = fw.sb(p5, "o_ss", [128, NT], F32)
            sd5 = fw.sb(p5, "o_sd", [128, NT], F32)
            rstd5 = fw.sb(p5, "o_rstd", [128, NT], F32)
            bSt5 = fw.bufs_n("o_st", NT)
            b_X2p = fw.bufs_n("scr_x2_", 2)

            def loads5(t):
                fw.dma("sp", mgt[t % 2][:], MG[t], reads=b_MG, writes=[bMgt[t % 2]])
                fw.dma("sp", xb5[t % 2][:], x[t * 128:(t + 1) * 128, :], writes=[bXb5[t % 2]])
            NT5 = NT if stage >= 5 else 0

            def mm5(t):
                s_ = t % 2
                for cg in range(4):
                    bi = cg
                    for kc in range(KC):
                        mm(PS[bi][:, :], mgt[s_][:, kc * 128:(kc + 1) * 128], wout[:, kc, cg * 512:(cg + 1) * 512],
                           kc == 0, kc == KC - 1, [bMgt[s_], bWo[cg]], bPS[bi], kc == KC - 1, kc > 0)

            def add5(t):
                s_ = t % 2
                x2t, b_x2t = xb5[s_], bXb5[s_]
                for cg in range(4):
                    bi = cg
                    fw.op("dve", lambda e, cg=cg, bi=bi: e.tensor_tensor(
                        out=x2t[:, cg * 512:(cg + 1) * 512], in0=PS[bi][:, :], in1=x2t[:, cg * 512:(cg + 1) * 512],
                        op=ALU.add), reads=[bPS[bi], b_x2t], writes=[b_x2t])
                fw.dma("sp", X2[t * 128:(t + 1) * 128, :], x2t[:], reads=[b_x2t], writes=[b_X2p[s_]], partial=True)

            def norm5(t):
                s_ = t % 2
                x2t, b_x2t = xb5[s_], bXb5[s_]
                rms_tile("p5", x2t[:], b_x2t, ss5, sd5, rstd5, t, bSt5[t], xn5, b_xn5)
                fw.op("dve", lambda e: e.scalar_tensor_tensor(
                    out=xn5[:], in0=x2t[:], scalar=rstd5[:, t:t + 1], in1=wbc2[:], op0=ALU.mult, op1=ALU.mult),
                    reads=[b_x2t, bSt5[t], b_wbc2], writes=[b_xn5])
                ba, bb = (4, 5) if s_ == 0 else (6, 7)
                for half, bi in ((0, ba), (1, bb)):
                    pv = PS[bi][:].bitcast(BF16)
                    for c8 in range(8):
                        c = half * 8 + c8
                        fw.op("pe", lambda e, c=c, c8=c8, pv=pv: e.transpose(
                            pv[:, c8 * 128:(c8 + 1) * 128], xn5[:, c * 128:(c + 1) * 128], ident[:]),
                            reads=[b_xn5, b_const], writes=[bPS[bi]], signal=(c8 == 7), partial=(c8 > 0))
                    fw.op("act", lambda e, half=half, pv=pv: e.activation(
                        out=h2T[:, half * 8:(half + 1) * 8, t * 128:(t + 1) * 128],
                        in_=pv.rearrange("p (c n) -> p c n", n=128), func=AF.Copy),
                        reads=[bPS[bi]], writes=[bH2[t // 4]], partial=True)
            if NT5:
                loads5(0)
                loads5(1)
                mm5(0)
                add5(0)
            for t in range(NT5):
                if t + 1 < NT5:
                    mm5(t + 1)
                norm5(t)
                if t + 2 < NT5:
                    loads5(t + 2)
                if t + 1 < NT5:
                    add5(t + 1)
            fw.barrier()

        if debug and stage == 5:
            d_x2 = dout("d_x2", [S, D], F32)
            d_h2T = dout("d_h2T", [D, S], BF16)
            b_d5 = fw.buf("d_5")
            fw.dma("sp", d_h2T.rearrange("(c p) s -> p c s", p=128), h2T[:], reads=bH2, writes=[b_d5], partial=True)
            with ExitStack() as pd:
                dt_ = fw.sb(pd, "dbg_t5", [128, D], F32)
                b_dt = fw.buf("dbg_t5")
                for t in range(NT):
                    fw.dma("sp", dt_[:], X2[t * 128:(t + 1) * 128, :], reads=b_X2p, writes=[b_dt])
                    fw.dma("sp", d_x2[t * 128:(t + 1) * 128, :], dt_[:], reads=[b_dt], writes=[b_d5], partial=True)
            fw.wait_buf("sp", b_d5)
            return nc, dram, dbg

        b_GT = fw.bufs_n("scr_gt", 2)
        with ExitStack() as p6:
            convw = fw.sb(p6, "f_convw", [128, 3 * 88], F32)
            convb = fw.sb(p6, "f_convb", [128, 88], F32)
            b_c6 = fw.buf("f_consts")
            fw.dma("sp", convw[:], dram["convw_t"], writes=[b_c6], partial=True)
            fw.dma("sp", convb[:], dram["convb_t"], writes=[b_c6], partial=True)
            ua = [[fw.sb(p6, f"f_u{ab}{i}", [128, S + 2], F32) for i in range(2)] for ab in range(2)]
            ya = [[fw.sb(p6, f"f_y{ab}{i}", [128, S], F32) for i in range(2)] for ab in range(2)]
            bU = [fw.bufs_n(f"f_u{ab}", 2) for ab in range(2)]
            bY = [fw.bufs_n(f"f_y{ab}", 2) for ab in range(2)]
            gt = [fw.sb(p6, f"f_g{i}", [128, S], BF16) for i in range(2)]
            bG = fw.bufs_n("f_g", 2)
            for ab in range(2):
                for i in range(2):
                    fw.op("dve", lambda e, ab=ab, i=i: e.memset(ua[ab][i][:, 0:2], 0.0), writes=[bU[ab][i]])
            for c in range(NFF if stage >= 6 else 0):
                s_ = c % 2
                for ab in range(2):
                    wt, bw = fws.get(2 * c + ab)
                    chn = ab * NFF + c
                    u_, y_ = ua[ab][s_], ya[ab][s_]
                    bu_, by_ = bU[ab][s_], bY[ab][s_]

                    def evac(j, ps, bps, u_=u_, y_=y_, bu_=bu_, by_=by_, chn=chn):
                        fw.op("act", lambda e: e.activation(out=u_[:, 2 + j * 512: 2 + (j + 1) * 512], in_=ps[:, :],
                                                            func=AF.Copy),
                              reads=[bps], writes=[bu_], partial=True)
                        fw.op("act", lambda e: e.activation(out=y_[:, j * 512:(j + 1) * 512], in_=ps[:, :],
                                                            func=AF.Identity, scale=convw[:, 2 * 88 + chn: 2 * 88 + chn + 1],
                                                            bias=convb[:, chn:chn + 1]),
                              reads=[bps, b_c6], writes=[by_], partial=(j > 0))
                    proj_chunk(wt, bw, evac, act_T=h2T, b_act=bH2, banks=(0, 8))
                    for tap in (1, 0):
                        sh_ = 2 - tap
                        fw.op("dve", lambda e, u_=u_, y_=y_, tap=tap, sh_=sh_, chn=chn: e.scalar_tensor_tensor(
                            out=y_[:], in0=u_[:, 2 - sh_: 2 - sh_ + S], scalar=convw[:, tap * 88 + chn: tap * 88 + chn + 1],
                            in1=y_[:], op0=ALU.mult, op1=ALU.add),
                            reads=[bu_, by_, b_c6], writes=[by_])
                fw.op("act", lambda e: e.activation(out=ya[0][s_][:], in_=ya[0][s_][:], func=AF.Silu),
                      reads=[bY[0][s_]], writes=[bY[0][s_]])
                fw.op("dve", lambda e: e.tensor_tensor(out=gt[s_][:], in0=ya[0][s_][:], in1=ya[1][s_][:], op=ALU.mult),
                      reads=[bY[0][s_], bY[1][s_]], writes=[bG[s_]])
                fw.dma("sp", GT[:, :, c * 128:(c + 1) * 128].rearrange("t p n -> p t n"),
                       gt[s_][:, :].rearrange("p (t n) -> p t n", n=128), reads=[bG[s_]], writes=[b_GT[s_]], partial=True)
            fw.barrier()
        sh2.close()

        if debug and stage == 6:
            b_d6 = fw.buf("d_6")
            d_gt2 = dout("d_gt2", [NT, 128, NFF * 128], BF16)
            with ExitStack() as pd:
                dt_ = fw.sb(pd, "dbg_t6", [128, NFF * 128], BF16)
                b_dt = fw.buf("dbg_t6")
                for t in range(NT):
                    fw.dma("sp", dt_[:], GT[t], reads=b_GT, writes=[b_dt])
                    fw.dma("sp", d_gt2[t], dt_[:], reads=[b_dt], writes=[b_d6], partial=True)
            fw.wait_buf("sp", b_d6)
            return nc, dram, dbg

        b_X3 = fw.bufs_n("scr_x3", 4)
        b_out = fw.bufs_n("out", 2)
        with ExitStack() as p7:
            wd = [fw.sb(p7, f"d_wd{i}", [128, NFF, 512], BF16) for i in range(2)]
            bWd = [[fw.buf(f"d_wd{i}_")] * NFF for i in range(2)]
            gtt = [fw.sb(p7, f"d_gtt{i}", [128, NFF * 128], BF16) for i in range(3)]
            bGtt = fw.bufs_n("d_gtt", 3)
            x2q = [fw.sb(p7, f"d_x2q{i}", [128, 512], F32) for i in range(3)]
            bX2q = fw.bufs_n("d_x2q", 3)
            x3q = [fw.sb(p7, f"d_x3q{i}", [128, 512], F32) for i in range(2)]
            bX3q = fw.bufs_n("d_x3q", 2)
            wbc3 = fw.sb(p7, "e_wbc", [128, D], F32)
            b_wbc3 = fw.buf("e_wbc")
            x3t = [fw.sb(p7, f"e_x3t{i}", [128, D], F32) for i in range(2)]
            bX3t = fw.bufs_n("e_x3t", 2)
            junk8 = fw.sb(p7, "e_junk", [128, D], BF16)
            b_junk8 = fw.buf("e_junk")
            ss8 = fw.sb(p7, "e_ss", [128, NT], F32)
            sd8 = fw.sb(p7, "e_sd", [128, NT], F32)
            rstd8 = fw.sb(p7, "e_rstd", [128, NT], F32)
            bSt8 = fw.bufs_n("e_st", NT)

            def load_wd(q):
                for hk in range(2):
                    k0, k1 = hk * 22, (hk + 1) * 22
                    fw.dma("pool", wd[q % 2][:, k0:k1, :],
                           w_ffn_down[k0 * 128:k1 * 128, q * 512:(q + 1) * 512].rearrange("(kc p) n -> p kc n", p=128),
                           writes=[bWd[q % 2][k0]], partial=(hk > 0))
            seq7 = [(q, t) for q in range(4) for t in range(NT)] if stage >= 7 else []

            def loads7(k):
                q, t = seq7[k]
                fw.dma("sp", gtt[k % 3][:], GT[t], reads=b_GT, writes=[bGtt[k % 3]])
                fw.dma("sp", x2q[k % 3][:], X2[t * 128:(t + 1) * 128, q * 512:(q + 1) * 512], reads=b_X2p,
                       writes=[bX2q[k % 3]])

            def final8a(t):
                s_ = t % 2
                fw.dma("act", x3t[s_][:], X3[t * 128:(t + 1) * 128, :], reads=b_X3, writes=[bX3t[s_]])
                fw.op("act", lambda e: e.activation(out=junk8[:], in_=x3t[s_][:], func=AF.Square,
                                                    accum_out=ss8[:, t:t + 1]),
                      reads=[bX3t[s_]], writes=[b_junk8, bSt8[t]])
                fw.op("act", lambda e: e.activation(out=sd8[:, t:t + 1], in_=ss8[:, t:t + 1], func=AF.Sqrt,
                                                    scale=1.0 / D, bias=epst[:, 0:1]),
                      reads=[bSt8[t], b_eps], writes=[bSt8[t]])

            def final8b(t):
                s_ = t % 2
                fw.op("dve", lambda e: e.reciprocal(rstd8[:, t:t + 1], sd8[:, t:t + 1]),
                      reads=[bSt8[t]], writes=[bSt8[t]])
                fw.op("dve", lambda e: e.scalar_tensor_tensor(
                    out=x3t[s_][:], in0=x3t[s_][:], scalar=rstd8[:, t:t + 1], in1=wbc3[:], op0=ALU.mult, op1=ALU.mult),
                    reads=[bX3t[s_], bSt8[t], b_wbc3], writes=[bX3t[s_]])
                fw.dma("act", out[t * 128:(t + 1) * 128, :], x3t[s_][:], reads=[bX3t[s_]], writes=[b_out[s_]], partial=True)
            if seq7:
                fw.dma("sp", wbc3[:], fin_nw.partition_broadcast(128), writes=[b_wbc3])
                load_wd(0)
                loads7(0)
                loads7(1)
            for k, (q, t) in enumerate(seq7):
                if t == 0 and q + 1 < 4:
                    load_wd(q + 1)
                if k + 2 < len(seq7):
                    loads7(k + 2)
                s3, s_ = k % 3, k % 2
                bi = next_bank(0, 8)
                for kc in range(NFF):
                    mm(PS[bi][:, :], gtt[s3][:, kc * 128:(kc + 1) * 128], wd[q % 2][:, kc, :],
                       kc == 0, kc == NFF - 1, [bGtt[s3], bWd[q % 2][kc]], bPS[bi], kc == NFF - 1, kc > 0)
                fw.op("dve", lambda e, s_=s_, s3=s3, bi=bi: e.tensor_tensor(out=x3q[s_][:], in0=PS[bi][:, :], in1=x2q[s3][:],
                                                                            op=ALU.add),
                      reads=[bPS[bi], bX2q[s3]], writes=[bX3q[s_]])
                xb_ = b_X3[(2 if q == 3 else 0) + s_]
                fw.dma("sp", X3[t * 128:(t + 1) * 128, q * 512:(q + 1) * 512], x3q[s_][:], reads=[bX3q[s_]],
                       writes=[xb_], partial=True)
                if q == 3:
                    if t > 0:
                        final8b(t - 1)
                    final8a(t)
            if seq7:
                final8b(NT - 1)
                fw.wait_buf("sp", b_out[0])
                fw.wait_buf("sp", b_out[1])
            fw.barrier()
        LASTFW[0] = fw

    return nc, dram, dbg


def _shared_inputs(inputs):
    consts, _ = _get_consts()
    f = lambda a: np.ascontiguousarray(np.asarray(a, dtype=np.float32))
    m = {
        "w_in": f(inputs["w_in"][0]),
        "w_ret_up": f(inputs["w_ret_up"][0]),
        "w_moba_up": f(inputs["w_moba_up"][0]),
        "w_out": f(inputs["w_out"][0]),
        "w_ffn_up": f(inputs["w_ffn_up"][0]),
        "w_ffn_down": f(inputs["w_ffn_down"][0]),
        "attn_norm_w": f(inputs["attn_norm_w"][0]).reshape(1, D),
        "ffn_norm_w": f(inputs["ffn_norm_w"][0]).reshape(1, D),
        "final_norm_w": f(inputs["final_norm_w"]).reshape(1, D),
        "retw_t": f(np.asarray(inputs["ret_norm_w"][0]).reshape(8, 128).T),
        "convw_t": f(np.asarray(inputs["conv_w"][0]).reshape(3, 88, 128).transpose(2, 0, 1).reshape(128, 264)),
        "convb_t": f(np.asarray(inputs["conv_b"][0]).reshape(88, 128).T),
    }
    m.update(consts)
    return m


def kernel(**inputs):
    nc, dram, dbg = build_nc()
    shared = _shared_inputs(inputs)
    xs = np.asarray(inputs["x"], dtype=np.float32)
    in_maps = []
    for b in range(8):
        mm = dict(shared)
        mm["x"] = np.ascontiguousarray(xs[b])
        in_maps.append({k: mm[k] for k in dram})
    res = run_bass_kernel_spmd(nc, in_maps, core_ids=list(range(8)))
    return np.stack([np.asarray(r["out"], dtype=np.float32) for r in res.results], axis=0)
```

```python
import math
from contextlib import ExitStack

import numpy as np
import ml_dtypes
import concourse.bass as bass
import concourse.mybir as mybir
from concourse.bass_utils import run_bass_kernel_spmd

F32 = mybir.dt.float32
BF16 = mybir.dt.bfloat16
AF = mybir.ActivationFunctionType
ALU = mybir.AluOpType
AX = mybir.AxisListType

S = 2048
D = 2048
NT = 16
KC = 16
DFF = 5632
NFF = 44
IN_TOTAL = 11264
OFF_RQ, OFF_RK, OFF_RV, OFF_RG, OFF_MQ, OFF_MK, OFF_MV, OFF_GR, OFF_GM = (
    0, 1024, 2048, 3072, 4096, 5120, 6144, 7168, 9216)
EPS = 1e-6
NEG = -30000.0


class Buf:
    __slots__ = ("name", "w", "r", "dsem", "dcount", "excl")

    def __init__(self, name):
        self.name = name
        self.excl = False
        self.w = {}
        self.r = {}
        self.dsem = None
        self.dcount = 0


class Eng:
    def __init__(self, name, eng, sem):
        self.name = name
        self.eng = eng
        self.sem = sem
        self.seq = 0
        self.waited = {}
        self.nwaits = 0
        self.ninst = 0


class FW:
    def __init__(self, nc, stack):
        self.nc = nc
        self.stack = stack
        self.E = {}
        for name, eng in (("pe", nc.tensor), ("act", nc.scalar), ("dve", nc.vector),
                          ("pool", nc.gpsimd), ("sp", nc.sync)):
            sem = stack.enter_context(nc.semaphore("sem_" + name))
            self.E[name] = Eng(name, eng, sem)
        self.bufs = []
        self.free_dsems = []
        self.nsem = 5

    def buf(self, name):
        b = Buf(name)
        self.bufs.append(b)
        return b

    def bufs_n(self, name, n):
        return [self.buf(f"{name}{i}") for i in range(n)]

    def sb(self, st, name, shape, dtype):
        return st.enter_context(self.nc.sbuf_tensor(name, list(shape), dtype))

    def _wait(self, e, tok):
        sem, val = tok
        k = id(sem)
        if e.waited.get(k, 0) >= val:
            return
        e.eng.wait_ge(sem, val)
        e.waited[k] = val
        e.nwaits += 1

    def _deps(self, e, reads, writes, partial):
        toks = []
        for b in reads:
            toks.extend(b.w.values())
            if b.excl:
                toks.extend(t for t in b.r.values() if t[0] is not e.sem)
        for b in writes:
            if not partial:
                toks.extend(b.w.values())
            toks.extend(b.r.values())
        for tok in toks:
            if e.name == "pe" and tok[0] is e.sem:
                continue
            self._wait(e, tok)

    def _record(self, tok, reads, writes, partial):
        k = id(tok[0])
        for b in reads:
            old = b.r.get(k)
            if old is None or old[1] < tok[1]:
                b.r[k] = tok
        for b in writes:
            if not partial:
                b.w = {}
                b.r = {}
            b.w[k] = tok

    def op(self, en, fn, reads=(), writes=(), signal=True, partial=False):
        e = self.E[en]
        self._deps(e, reads, writes, partial)
        inst = fn(e.eng)
        e.ninst += 1
        if signal:
            e.seq += 1
            inst.then_inc(e.sem, 1)
            tok = (e.sem, e.seq)
        else:
            tok = (e.sem, e.seq + 1)
        self._record(tok, reads, writes, partial)
        return inst

    def dma(self, qn, out, in_, reads=(), writes=(), partial=False, **kw):
        e = self.E[qn]
        (dst,) = writes
        self._deps(e, reads, writes, partial)
        if dst.dsem is None:
            dst.dsem = self.stack.enter_context(self.nc.semaphore("dsem_" + dst.name))
            self.nsem += 1
        inst = e.eng.dma_start(out=out, in_=in_, **kw)
        dst.dcount += 1
        inst.then_inc(dst.dsem, 16)
        tok = (dst.dsem, 16 * dst.dcount)
        e.ninst += 1
        self._record(tok, reads, writes, partial)
        return inst

    def wait_buf(self, en, b):
        for tok in list(b.w.values()):
            self._wait(self.E[en], tok)

    def barrier(self):
        toks = {}
        for e in self.E.values():
            if e.seq > 0:
                toks[id(e.sem)] = (e.sem, e.seq)
        for b in self.bufs:
            for d in (b.w, b.r):
                for k, t in d.items():
                    if k not in toks or toks[k][1] < t[1]:
                        toks[k] = t
        for e in self.E.values():
            for t in toks.values():
                if t[0] is e.sem:
                    continue
                self._wait(e, t)


def _consts():
    c = {}
    bf = ml_dtypes.bfloat16
    c["c_ident"] = np.eye(128, dtype=np.float32).astype(bf)
    c["c_ones"] = np.ones((128, 128), np.float32).astype(bf)
    ind = np.zeros((128, 8, 128), np.float32)
    for n in range(8):
        ind[n, n, :] = 1.0
    c["c_ind"] = ind.astype(bf)
    cm = np.zeros((128, 4, 512), np.float32)
    kk = np.arange(128)[:, None]
    qq = np.arange(512)[None, :]
    for r in range(4):
        cm[:, r, :] = np.where(qq >= r * 128 + kk, 0.0, NEG)
    c["c_cm"] = cm.astype(bf)
    pos = np.arange(S, dtype=np.float64)
    inv = 500000.0 ** (-np.arange(0, 32, 2, dtype=np.float64) / 32.0)
    ang = (pos[None, :].astype(np.float32) * inv[:, None].astype(np.float32)).astype(np.float32).astype(np.float64)
    c["c_mcos"] = np.concatenate([np.cos(ang), np.cos(ang)], 0).astype(np.float32)
    c["c_msin"] = np.concatenate([-np.sin(ang), np.sin(ang)], 0).astype(np.float32)
    pastneg = np.zeros((128, 16, 8), np.float32)
    past01 = np.zeros((128, 16, 8), np.float32)
    own01 = np.zeros((128, 16, 8), np.float32)
    for t in range(16):
        cb = t // 2
        for n in range(8):
            if n < cb:
                past01[:, t, n] = 1.0
            else:
                pastneg[:, t, n] = -1e30
            if n == cb:
                own01[:, t, n] = 1.0
    c["c_pastneg"] = pastneg
    c["c_past01"] = past01
    c["c_own01"] = own01
    inv_r0 = 10000.0 ** (-np.arange(0, 256, 2, dtype=np.float64) / 256.0)
    ang_r0 = (pos[None, :].astype(np.float32) * inv_r0[:, None].astype(np.float32)).astype(np.float32).astype(np.float64)
    c["c_rcos"] = np.cos(ang_r0).astype(np.float32)
    c["c_rsin"] = np.sin(ang_r0).astype(np.float32)
    rdt = np.zeros((128, 4, 128), np.float64)
    rzeta = np.zeros((128, 4), np.float64)
    repsq = np.zeros((128, 4, 512), np.float64)
    kk_ = np.arange(128, dtype=np.float64)
    for h in range(4):
        lg = np.log1p(-np.exp2(-5.0 - h))
        causal = (kk_[:, None] <= kk_[None, :])
        rdt[:, h, :] = np.where(causal, np.exp(-lg * (kk_[:, None] + 1.0)) / 16.0, 0.0)
        rzeta[:, h] = np.exp(lg * (127.0 - kk_)) / 16.0
        eq = EPS / np.exp(2.0 * lg * (kk_ + 1.0))
        repsq[:, h, :] = np.tile(eq, 4)[None, :]
    c["c_rdt"] = rdt.astype(np.float32)
    c["c_rzeta"] = rzeta.astype(np.float32)
    c["c_repsq"] = repsq.astype(np.float32)
    inv_r = 10000.0 ** (-np.arange(0, 256, 2, dtype=np.float64) / 256.0)
    ang_r = (pos[None, :].astype(np.float32) * inv_r[:, None].astype(np.float32)).astype(np.float32).astype(np.float64)
    cosr, sinr = np.cos(ang_r), np.sin(ang_r)
    tl = (np.arange(S) % 128).astype(np.float64)
    rt = np.zeros((4, 4, 128, S), np.float32)
    gch = np.zeros((4,), np.float64)
    for h in range(4):
        lg = np.log1p(-np.exp2(-5.0 - h))
        xi = np.exp(lg * (tl + 1.0))
        kz = np.exp(-lg * (tl + 1.0)) / 16.0
        rt[h, 0] = cosr * xi[None, :]
        rt[h, 1] = sinr * xi[None, :]
        rt[h, 2] = cosr * kz[None, :]
        rt[h, 3] = sinr * kz[None, :]
        gch[h] = np.exp(lg * 128.0)
    return c, gch


_CONST_CACHE = {}
LASTFW = [None]


def _get_consts():
    if "c" not in _CONST_CACHE:
        _CONST_CACHE["c"] = _consts()
    return _CONST_CACHE["c"]


def build_nc(stage=99, debug=False):
    consts, gch = _get_consts()
    nc = bass.Bass("TRN2", target_bir_lowering=False)
    dram = {}

    def din(name, shape, dt=F32):
        dram[name] = nc.dram_tensor(name, list(shape), dt, kind="ExternalInput").ap()
        return dram[name]

    x = din("x", [S, D])
    w_in = din("w_in", [D, IN_TOTAL])
    small = debug and stage <= 3
    w_ret_up = None if small else din("w_ret_up", [1024, D])
    w_moba_up = None if small else din("w_moba_up", [1024, D])
    w_out = None if small else din("w_out", [D, D])
    w_ffn_up = None if small else din("w_ffn_up", [D, 2 * DFF])
    w_ffn_down = None if small else din("w_ffn_down", [DFF, D])
    attn_nw = din("attn_norm_w", [1, D])
    ffn_nw = din("ffn_norm_w", [1, D])
    fin_nw = din("final_norm_w", [1, D])
    retw_t = din("retw_t", [128, 8])
    convw_t = din("convw_t", [128, 3 * 88])
    convb_t = din("convb_t", [128, 88])
    cd = {}
    for k, v in consts.items():
        cd[k] = din(k, v.shape, BF16 if v.dtype == ml_dtypes.bfloat16 else F32)
    out = nc.dram_tensor("out", [S, D], F32, kind="ExternalOutput").ap()
    dbg = {}

    def dout(name, shape, dt=F32):
        dbg[name] = nc.dram_tensor(name, list(shape), dt, kind="ExternalOutput").ap()
        return dbg[name]

    def dscr(name, shape, dt):
        return nc.dram_tensor(name, list(shape), dt, kind="Internal").ap()

    OM = dscr("scr_om", [1024, S], BF16)
    OR = dscr("scr_or", [1024, S], BF16)
    MG = dscr("scr_mg", [NT, 128, KC * 128], BF16)
    X2 = dscr("scr_x2", [S, D], F32)
    GT = dscr("scr_gt", [NT, 128, NFF * 128], BF16)
    X3 = dscr("scr_x3", [S, D], F32)

    with ExitStack() as top:
        fw = FW(nc, top)
        PS = [top.enter_context(nc.psum_tensor(f"ps{i}", [128, 512], F32)) for i in range(8)]
        bPS = fw.bufs_n("ps", 8)
        for b_ in bPS:
            b_.excl = True
        ident = fw.sb(top, "ident", [128, 128], BF16)
        ones = fw.sb(top, "ones", [128, 128], BF16)
        epst = fw.sb(top, "epst", [128, 1], F32)
        b_const = fw.buf("const")
        fw.dma("sp", ident[:], cd["c_ident"], writes=[b_const], partial=True)
        fw.dma("sp", ones[:], cd["c_ones"], writes=[b_const], partial=True)
        b_eps = fw.buf("eps")
        fw.op("dve", lambda e: e.memset(epst[:], EPS), writes=[b_eps])

        NW = 8
        wring = [fw.sb(top, f"wring{i}", [128, KC, 128], BF16) for i in range(NW)]
        bW = fw.bufs_n("wring", NW)
        wstate = {"n": 0}

        def wload(src_ap, col0, nk=KC):
            i = wstate["n"] % NW
            wstate["n"] += 1
            srcv = src_ap[0:nk * 128, col0:col0 + 128].rearrange("(kc p) n -> p kc n", p=128)
            fw.dma("pool", wring[i][:, 0:nk, :], srcv, writes=[bW[i]])
            return wring[i], bW[i]

        class WStream:
            def __init__(self, specs, group=1):
                self.specs = specs
                self.group = group
                self.issued = 0
                self.tiles = []

            def get(self, i, group=None, base=None):
                g = self.group if group is None else group
                b = i if base is None else base
                while self.issued < len(self.specs) and self.issued <= b + NW - g:
                    self.tiles.append(wload(*self.specs[self.issued]))
                    self.issued += 1
                return self.tiles[i]

        mspecs = []
        for h in range(8):
            mspecs += [(w_in, OFF_MQ + h * 128, KC), (w_in, OFF_MK + h * 128, KC), (w_in, OFF_MV + h * 128, KC)]
        rspecs = []
        for h in range(4):
            for off in (OFF_RQ, OFF_RK, OFF_RV, OFF_RG):
                for c in range(2):
                    rspecs.append((w_in, off + h * 256 + c * 128, KC))
        gspecs = []
        if w_ret_up is not None:
            for c in range(KC):
                gspecs += [(w_in, OFF_GR + c * 128, KC), (w_in, OFF_GM + c * 128, KC),
                           (w_ret_up, c * 128, 8), (w_moba_up, c * 128, 8)]
        nsp = [8 * 3 if stage >= 2 else 0, 32 if stage >= 3 else 0, 64 if stage >= 4 else 0]
        aspecs = mspecs[:nsp[0]] + rspecs[:nsp[1]] + gspecs[:nsp[2]]
        AWS = WStream(aspecs)
        RBASE = nsp[0]
        GBASE = nsp[0] + nsp[1]

        sh = ExitStack()
        top.callback(sh.close)
        hT = fw.sb(sh, "hT", [128, KC, S], BF16)
        bH = fw.bufs_n("hT", 4)

        psrr = {"i": 0}

        def next_bank(lo=0, n=8):
            i = lo + psrr["i"] % n
            psrr["i"] += 1
            return i

        def proj_chunk(wt, bw, evac, act_T=None, b_act=None, nk=KC, banks=(0, 4)):
            aT = hT if act_T is None else act_T
            bA = bH if b_act is None else b_act
            for j in range(4):
                bi = next_bank(*banks)
                for kc in range(nk):
                    fw.op("pe", lambda e, kc=kc, bi=bi, j=j: e.matmul(
                        PS[bi][:, :], wt[:, kc, :], aT[:, kc, j * 512:(j + 1) * 512],
                        start=(kc == 0), stop=(kc == nk - 1)),
                        reads=[bw, bA[j]], writes=[bPS[bi]], signal=(kc == nk - 1), partial=(kc > 0))
                evac(j, PS[bi], bPS[bi])

        sc2 = ExitStack()
        top.callback(sc2.close)
        ind = fw.sb(sc2, "m_ind", [128, 8, 128], BF16)
        cm = fw.sb(sc2, "m_cm", [128, 4, 512], BF16)
        mcos = fw.sb(sc2, "m_cos", [32, S], F32)
        msin = fw.sb(sc2, "m_sin", [32, S], F32)
        pastneg = fw.sb(sc2, "m_pastneg", [128, 128], F32)
        past01 = fw.sb(sc2, "m_past01", [128, 128], F32)
        own01 = fw.sb(sc2, "m_own01", [128, 128], F32)
        b_c2 = fw.buf("m_consts")
        for tile_, src in ((ind, cd["c_ind"]), (cm, cd["c_cm"]), (mcos, cd["c_mcos"]), (msin, cd["c_msin"]),
                           (pastneg, cd["c_pastneg"].rearrange("p t n -> p (t n)")),
                           (past01, cd["c_past01"].rearrange("p t n -> p (t n)")),
                           (own01, cd["c_own01"].rearrange("p t n -> p (t n)"))):
            fw.dma("sp", tile_[:], src, writes=[b_c2], partial=True)
        if stage >= 2:
            AWS.get(0)

        def rms_tile(st_name, xt_ap, b_xt, ss, sd, rstd, col, b_stat, junk, b_junk, do_recip=True):
            fw.op("act", lambda e: e.activation(out=junk[:], in_=xt_ap, func=AF.Square,
                                                accum_out=ss[:, col:col + 1]),
                  reads=[b_xt], writes=[b_junk, b_stat])
            fw.op("act", lambda e: e.activation(out=sd[:, col:col + 1], in_=ss[:, col:col + 1], func=AF.Sqrt,
                                                scale=1.0 / D, bias=epst[:, 0:1]),
                  reads=[b_stat, b_eps], writes=[b_stat])
            if do_recip:
                fw.op("dve", lambda e: e.reciprocal(rstd[:, col:col + 1], sd[:, col:col + 1]),
                      reads=[b_stat], writes=[b_stat])

        with ExitStack() as p1:
            xt = [fw.sb(p1, f"p1_xt{i}", [128, D], F32) for i in range(3)]
            bXt = fw.bufs_n("p1_xt", 3)
            xn = [fw.sb(p1, f"p1_xn{i}", [128, D], BF16) for i in range(2)]
            bXn = fw.bufs_n("p1_xn", 2)
            junk = fw.sb(p1, "p1_junk", [128, D], BF16)
            b_junk = fw.buf("p1_junk")
            wbc = fw.sb(p1, "p1_wbc", [128, D], F32)
            b_wbc = fw.buf("p1_wbc")
            ss = fw.sb(p1, "p1_ss", [128, NT], F32)
            sd = fw.sb(p1, "p1_sd", [128, NT], F32)
            rstd = fw.sb(p1, "p1_rstd", [128, NT], F32)
            bSt = fw.bufs_n("p1_st", NT)
            fw.dma("sp", wbc[:], attn_nw.partition_broadcast(128), writes=[b_wbc])
            def stats1(t):
                s = t % 3
                fw.dma("sp", xt[s][:], x[t * 128:(t + 1) * 128, :], writes=[bXt[s]])
                rms_tile("p1", xt[s][:], bXt[s], ss, sd, rstd, t, bSt[t], junk, b_junk, do_recip=False)

            def norm1(t):
                s = t % 2
                s3 = t % 3
                fw.op("dve", lambda e: e.reciprocal(rstd[:, t:t + 1], sd[:, t:t + 1]),
                      reads=[bSt[t]], writes=[bSt[t]])
                fw.op("dve", lambda e: e.scalar_tensor_tensor(
                    out=xn[s][:], in0=xt[s3][:], scalar=rstd[:, t:t + 1], in1=wbc[:],
                    op0=ALU.mult, op1=ALU.mult),
                    reads=[bXt[s3], bSt[t], b_wbc], writes=[bXn[s]])
                ba, bb = (0, 1) if s == 0 else (2, 3)
                for half, bi in ((0, ba), (1, bb)):
                    pv = PS[bi][:].bitcast(BF16)
                    for c8 in range(8):
                        c = half * 8 + c8
                        fw.op("pe", lambda e, c=c, c8=c8, pv=pv: e.transpose(
                            pv[:, c8 * 128:(c8 + 1) * 128], xn[s][:, c * 128:(c + 1) * 128], ident[:]),
                            reads=[bXn[s], b_const], writes=[bPS[bi]], signal=(c8 == 7), partial=(c8 > 0))
                    if half == 0:
                        fw.op("act", lambda e, half=half, pv=pv: e.activation(
                            out=hT[:, half * 8:(half + 1) * 8, t * 128:(t + 1) * 128],
                            in_=pv.rearrange("p (c n) -> p c n", n=128), func=AF.Copy),
                            reads=[bPS[bi]], writes=[bH[t // 4]], partial=True)
                    else:
                        fw.op("dve", lambda e, half=half, pv=pv: e.tensor_copy(
                            out=hT[:, half * 8:(half + 1) * 8, t * 128:(t + 1) * 128],
                            in_=pv.rearrange("p (c n) -> p c n", n=128)),
                            reads=[bPS[bi]], writes=[bH[t // 4]], partial=True)
            stats1(0)
            for t in range(NT):
                if t + 1 < NT:
                    stats1(t + 1)
                norm1(t)
            fw.barrier()

        if debug and stage == 1:
            d_hT = dout("d_hT", [D, S], BF16)
            b_d = fw.buf("d_hT")
            fw.dma("sp", d_hT.rearrange("(c p) s -> p c s", p=128), hT[:], reads=bH, writes=[b_d])
            fw.wait_buf("sp", b_d)
            return nc, dram, dbg


        def mm(out_ap, lhsT, rhs, start, stop, reads, wbuf, signal, partial):
            fw.op("pe", lambda e: e.matmul(out_ap, lhsT, rhs, start=start, stop=stop),
                  reads=reads, writes=[wbuf], signal=signal, partial=partial)

        if debug:
            d_om = dout("d_om", [1024, S], BF16)
            b_dom = fw.bufs_n("d_om", 2)
        with ExitStack() as p2:
            SCALE = 128.0 ** -0.5
            qT = [fw.sb(p2, f"m_qT{i}", [128, S], BF16) for i in range(2)]
            kT = [fw.sb(p2, f"m_kT{i}", [128, S], BF16) for i in range(2)]
            bQ = fw.bufs_n("m_qT", 2)
            bK = fw.bufs_n("m_kT", 2)
            vT = fw.sb(p2, "m_vT", [128, S], BF16)
            b_vT = fw.buf("m_vT")
            vtok = [fw.sb(p2, f"m_vtok{i}", [128, NT, 128], BF16) for i in range(2)]
            bV = fw.bufs_n("m_vtok", 2)
            rr = fw.sb(p2, "m_rr", [32, S], F32)
            rp = fw.sb(p2, "m_rp", [32, S], F32)
            b_rr = fw.buf("m_rr")
            b_rp = fw.buf("m_rp")
            biasfull = fw.sb(p2, "m_biasfull", [128, NT, 128], BF16)
            b_bf = fw.buf("m_biasfull")
            biasT = fw.sb(p2, "m_biasT", [128, S], BF16)
            b_bT = fw.buf("m_biasT")
            Es = [fw.sb(p2, f"m_E{i}", [128, 512], BF16) for i in range(3)]
            bE = fw.bufs_n("m_E", 3)
            rec = fw.sb(p2, "m_rec", [128, 512], F32)
            b_rec = fw.buf("m_rec")
            dsb = fw.sb(p2, "m_dsb", [128, 512], F32)
            b_dsb = fw.buf("m_dsb")
            osb = fw.sb(p2, "m_osb", [128, 512], F32)
            b_osb = fw.buf("m_osb")
            oout = [fw.sb(p2, f"m_oout{i}", [128, S], BF16) for i in range(2)]
            bO = fw.bufs_n("m_oout", 2)
            km32 = fw.sb(p2, "m_km32", [128, 8], F32)
            kmb = [fw.sb(p2, f"m_kmb{i}", [128, 8], BF16) for i in range(2)]
            b_km = fw.buf("m_km32")
            bKm = fw.bufs_n("m_kmb", 2)
            s1 = fw.sb(p2, "m_s1", [128, 128], F32)
            cnt = fw.sb(p2, "m_cnt", [128, 128], F32)
            cmpt = fw.sb(p2, "m_cmp", [128, 128], F32)
            b_sel = fw.buf("m_sel")
            fw.op("dve", lambda e: e.memset(biasfull[:], 0.0), writes=[b_bf])
            fw.op("dve", lambda e: e.memset(biasT[:], 0.0), writes=[b_bT])

            class _M:
                def get(self, i):
                    return AWS.get(i, group=1)
            mws = _M()

            def v3(ap2d):
                return ap2d.rearrange("p (t n) -> p t n", n=8)

            def prepA(h):
                s = h % 2
                for which, dstT, bD in ((0, qT[s], bQ[s]), (1, kT[s], bK[s])):
                    wt, bw = mws.get(3 * h + which)

                    def evac(j, ps, bps, dstT=dstT, bD=bD):
                        fw.op("act", lambda e: e.activation(out=dstT[:, j * 512:(j + 1) * 512],
                                                            in_=ps[:, :], func=AF.Copy),
                              reads=[bps], writes=[bD], partial=True)
                        fw.op("dve", lambda e: e.tensor_copy(out=rr[:, j * 512:(j + 1) * 512], in_=ps[0:32, :]),
                              reads=[bps], writes=[b_rr], partial=(j > 0))
                    proj_chunk(wt, bw, evac)
                    fw.dma("sp", rp[0:16, :], rr[16:32, :], reads=[b_rr], writes=[b_rp])
                    fw.dma("sp", rp[16:32, :], rr[0:16, :], reads=[b_rr], writes=[b_rp], partial=True)
                    fw.op("dve", lambda e: e.tensor_tensor(out=rr[:], in0=rr[:], in1=mcos[:], op=ALU.mult),
                          reads=[b_rr, b_rp, b_c2], writes=[b_rr])
                    fw.op("dve", lambda e: e.tensor_tensor(out=rp[:], in0=rp[:], in1=msin[:], op=ALU.mult),
                          reads=[b_rp, b_c2], writes=[b_rp])
                    fw.op("dve", lambda e, dstT=dstT: e.tensor_tensor(out=dstT[0:32, :], in0=rr[:], in1=rp[:], op=ALU.add),
                          reads=[b_rr, b_rp], writes=[bD], partial=False)
                fw.op("dve", lambda e: e.tensor_reduce(out=km32[:, 0:8],
                                                       in_=kT[s][:, :].rearrange("p (n s) -> p n s", s=256),
                                                       axis=AX.X, op=ALU.add),
                      reads=[bK[s]], writes=[b_km])
                fw.op("act", lambda e: e.activation(out=kmb[s][:], in_=km32[:], func=AF.Copy, scale=1.0 / 256.0),
                      reads=[b_km], writes=[bKm[s]])
                wt, bw = mws.get(3 * h + 2)

                def evac_v(j, ps, bps):
                    fw.op("act", lambda e: e.activation(out=vT[:, j * 512:(j + 1) * 512], in_=ps[:, :], func=AF.Copy),
                          reads=[bps], writes=[b_vT], partial=(j > 0))
                proj_chunk(wt, bw, evac_v)
                for half in range(2):
                    bi = next_bank(0, 4)
                    pv = PS[bi][:].bitcast(BF16)
                    for t8 in range(8):
                        t = half * 8 + t8
                        fw.op("pe", lambda e, t=t, t8=t8, pv=pv: e.transpose(
                            pv[:, t8 * 128:(t8 + 1) * 128], vT[:, t * 128:(t + 1) * 128], ident[:]),
                            reads=[b_vT, b_const], writes=[bPS[bi]], signal=(t8 == 7), partial=(t8 > 0))
                    fw.op("dve", lambda e, half=half, pv=pv: e.tensor_copy(
                        out=vtok[s][:, half * 8:(half + 1) * 8, :], in_=pv.rearrange("p (c n) -> p c n", n=128)),
                        reads=[bPS[bi]], writes=[bV[s]], partial=(half > 0))

            def bscore(h):
                s = h % 2
                bi = next_bank(0, 4)
                for t in range(NT):
                    mm(PS[bi][:, t * 8:(t + 1) * 8], qT[s][:, t * 128:(t + 1) * 128], kmb[s][:, 0:8], True, True,
                       [bQ[s], bKm[s]], bPS[bi], t == NT - 1, t > 0)
                D_ = "dve"
                fw.op(D_, lambda e: e.tensor_tensor(out=s1[:], in0=PS[bi][:, 0:128], in1=pastneg[:], op=ALU.add),
                      reads=[bPS[bi], b_c2], writes=[b_sel])
                for m in range(8):
                    dst = cnt if m == 0 else cmpt
                    fw.op(D_, lambda e, m=m, dst=dst: e.tensor_tensor(
                        out=v3(dst[:]), in0=v3(s1[:])[:, :, m:m + 1].to_broadcast([128, NT, 8]), in1=v3(s1[:]),
                        op=ALU.is_gt), reads=[b_sel], writes=[b_sel])
                    if m > 0:
                        fw.op(D_, lambda e: e.tensor_tensor(out=cnt[:], in0=cnt[:], in1=cmpt[:], op=ALU.add),
                              reads=[b_sel], writes=[b_sel])
                fw.op(D_, lambda e: e.tensor_scalar(out=cnt[:], in0=cnt[:], scalar1=3.0, scalar2=None, op0=ALU.is_lt),
                      reads=[b_sel], writes=[b_sel])
                fw.op(D_, lambda e: e.tensor_tensor(out=cnt[:], in0=cnt[:], in1=past01[:], op=ALU.mult),
                      reads=[b_sel, b_c2], writes=[b_sel])
                fw.op(D_, lambda e: e.tensor_tensor(out=cnt[:], in0=cnt[:], in1=own01[:], op=ALU.add),
                      reads=[b_sel, b_c2], writes=[b_sel])
                fw.op(D_, lambda e: e.tensor_scalar(out=biasfull[:, :, 0:8], in0=v3(cnt[:]), scalar1=-1.0, scalar2=-NEG,
                                                    op0=ALU.add, op1=ALU.mult),
                      reads=[b_sel], writes=[b_bf])

            def biasTr(h):
                for g in range(4):
                    bi = next_bank(0, 4)
                    for t4 in range(4):
                        t = g * 4 + t4
                        mm(PS[bi][:, t4 * 128:(t4 + 1) * 128], biasfull[:, t, :], ident[:], True, True,
                           [b_bf, b_const], bPS[bi], t4 == 3, t4 > 0)
                    fw.op("act", lambda e, g=g, bi=bi: e.activation(out=biasT[0:8, g * 512:(g + 1) * 512],
                                                                    in_=PS[bi][0:8, :], func=AF.Copy),
                          reads=[bPS[bi]], writes=[b_bT], partial=(g > 0))

            def att(h):
                s = h % 2
                pairs = [(j, i) for j in range(4) for i in range(4 * (j + 1))]

                def qcols(p):
                    j, i = pairs[p]
                    return (256, 512) if i - 4 * j >= 2 else (0, 512)

                def emitS(p):
                    j, i = pairs[p]
                    sbk = 4 + (p % 2)
                    diag = i >= 4 * j
                    c0, c1 = qcols(p)
                    mm(PS[sbk][:, c0:c1], kT[s][:, i * 128:(i + 1) * 128], qT[s][:, j * 512 + c0:j * 512 + c1], True, False,
                       [bK[s], bQ[s]], bPS[sbk], False, False)
                    mm(PS[sbk][:, c0:c1], ind[:, i // 2, :], biasT[:, j * 512 + c0:j * 512 + c1], False, not diag,
                       [b_c2, b_bT], bPS[sbk], not diag, True)
                    if diag:
                        mm(PS[sbk][:, c0:c1], ident[:], cm[:, i - 4 * j, c0:c1], False, True,
                           [b_c2, b_const], bPS[sbk], True, True)
                    fw.op("act", lambda e: e.activation(out=Es[p % 3][:, c0:c1], in_=PS[sbk][:, c0:c1], func=AF.Exp,
                                                        scale=SCALE),
                          reads=[bPS[sbk]], writes=[bE[p % 3]])

                def emitOD(p):
                    j, i = pairs[p]
                    ni = 4 * (j + 1)
                    bo, bd = 6, 7
                    c0, c1 = qcols(p)
                    mm(PS[bo][:, c0:c1], vtok[s][:, i, :], Es[p % 3][:, c0:c1], i == 0, i == ni - 1,
                       [bV[s], bE[p % 3]], bPS[bo], False, i > 0)
                    mm(PS[bd][:, c0:c1], ones[:], Es[p % 3][:, c0:c1], i == 0, i == ni - 1,
                       [b_const, bE[p % 3]], bPS[bd], True, i > 0)
                    if i == ni - 1:
                        fw.op("act", lambda e: e.activation(out=dsb[:], in_=PS[bd][:, :], func=AF.Copy),
                              reads=[bPS[bd]], writes=[b_dsb])
                        fw.op("dve", lambda e: e.tensor_copy(out=osb[:], in_=PS[bo][:, :]),
                              reads=[bPS[bo]], writes=[b_osb])
                        fw.op("dve", lambda e: e.reciprocal(rec[:], dsb[:]), reads=[b_dsb], writes=[b_rec])
                        fw.op("dve", lambda e: e.tensor_tensor(out=oout[s][:, j * 512:(j + 1) * 512], in0=osb[:],
                                                               in1=rec[:], op=ALU.mult),
                              reads=[b_osb, b_rec], writes=[bO[s]], partial=(j > 0))
                emitS(0)
                for p in range(len(pairs)):
                    if p + 1 < len(pairs):
                        emitS(p + 1)
                    emitOD(p)
                fw.dma("sp", OM[h * 128:(h + 1) * 128, :], oout[s][:], reads=[bO[s]], writes=[b_OM[s]], partial=True)
                if debug:
                    fw.dma("sp", d_om[h * 128:(h + 1) * 128, :], oout[s][:], reads=[bO[s]], writes=[b_dom[s]], partial=True)

            b_OM = fw.bufs_n("scr_om", 2)
            NH = 8 if stage >= 2 else 0
            import os as _os
            sub = int(_os.environ.get("K_SUB", "0")) if debug else 0
            if sub:
                NH = 0
                d_q = dout("d_q", [128, S], BF16)
                d_k = dout("d_k", [128, S], BF16)
                d_v = dout("d_v", [128, NT * 128], BF16)
                d_b = dout("d_b", [128, S], BF16)
                b_dq = fw.bufs_n("d_sub", 4)
                prepA(0)
                if sub >= 2:
                    bscore(0)
                if sub >= 3:
                    biasTr(0)
                if sub >= 4:
                    att(0)
                fw.dma("sp", d_q, qT[0][:], reads=[bQ[0]], writes=[b_dq[0]])
                fw.dma("sp", d_k, kT[0][:], reads=[bK[0]], writes=[b_dq[1]])
                fw.dma("sp", d_v, vtok[0][:].rearrange("p a b -> p (a b)"), reads=[bV[0]], writes=[b_dq[2]])
                fw.dma("sp", d_b, biasT[:], reads=[b_bT, b_bf], writes=[b_dq[3]])
                for b_ in b_dq:
                    fw.wait_buf("sp", b_)
                if sub >= 4:
                    fw.wait_buf("sp", b_dom[0])
                return nc, dram, dbg
            if NH:
                prepA(0)
                bscore(0)
                if NH > 1:
                    prepA(1)
                biasTr(0)
                for h in range(NH):
                    att(h)
                    if h + 1 < NH:
                        bscore(h + 1)
                        if h + 2 < NH:
                            prepA(h + 2)
                        biasTr(h + 1)
            fw.barrier()
        sc2.close()

        if debug and stage == 2:
            fw.wait_buf("sp", b_dom[0])
            fw.wait_buf("sp", b_dom[1])
            return nc, dram, dbg


        if debug:
            d_or = dout("d_or", [1024, S], BF16)
            b_dor = fw.buf("d_or")
        b_OR = fw.buf("scr_or")
        with ExitStack() as p3:
            rcos = fw.sb(p3, "r_cos", [128, S], F32)
            rsin = fw.sb(p3, "r_sin", [128, S], F32)
            rdt = fw.sb(p3, "r_dt", [128, 4, 128], F32)
            rzeta = fw.sb(p3, "r_zeta", [128, 4], F32)
            repsq = fw.sb(p3, "r_epsq", [128, 4, 512], F32)
            retw = fw.sb(p3, "r_retw", [128, 8], F32)
            b_c3 = fw.buf("r_consts")
            for tile_, src in ((rcos, cd["c_rcos"]), (rsin, cd["c_rsin"]), (rdt, cd["c_rdt"]), (rzeta, cd["c_rzeta"]),
                               (repsq, cd["c_repsq"]), (retw, dram["retw_t"])):
                fw.dma("sp", tile_[:], src, writes=[b_c3], partial=True)
            rqT = fw.sb(p3, "r_qT", [128, 2, S], BF16)
            rkT = fw.sb(p3, "r_kT", [128, 2, S], BF16)
            rvT = fw.sb(p3, "r_vT", [128, 2, S], BF16)
            b_rq, b_rk, b_rv = fw.buf("r_qT"), fw.buf("r_kT"), fw.buf("r_vT")
            kTok = fw.sb(p3, "r_kTok", [128, NT, 256], BF16)
            rvtok = fw.sb(p3, "r_vtok", [128, NT, 256], BF16)
            b_kTok, b_rvtok = fw.buf("r_kTok"), fw.buf("r_vtok")
            rgs = fw.sb(p3, "r_gs", [128, 2, S], BF16)
            b_rgs = fw.buf("r_gs")
            oraw = fw.sb(p3, "r_oraw", [128, 2, S], F32)
            b_oraw = fw.buf("r_oraw")
            sq, b_sq = rvT, b_rv
            orT, b_orT = rkT, b_rk
            tmps = [fw.sb(p3, f"r_tmp{i}", [128, 512], F32) for i in range(4)]
            bT = fw.bufs_n("r_tmp", 4)
            Wst = fw.sb(p3, "r_W", [128, 512], F32)
            b_W = fw.buf("r_W")
            Wb = [fw.sb(p3, f"r_Wb{i}", [128, 512], BF16) for i in range(2)]
            bWb = fw.bufs_n("r_Wb", 2)
            PT = [fw.sb(p3, f"r_PT{i}", [128, 128], BF16) for i in range(2)]
            bPT = fw.bufs_n("r_PT", 2)
            nrm = fw.sb(p3, "r_nrm", [128, 512], F32)
            b_nrm = fw.buf("r_nrm")

            class _R:
                def get(self, i):
                    return AWS.get(RBASE + i, group=2, base=RBASE + (i // 2) * 2)
            rws = _R()

            def proj_pair(i0, evac2):
                (wt0, bw0), (wt1, bw1) = rws.get(i0), rws.get(i0 + 1)
                for j in range(4):
                    ba = next_bank(0, 4)
                    bb = next_bank(0, 4)
                    for wt, bw, bi in ((wt0, bw0, ba), (wt1, bw1, bb)):
                        for kc in range(KC):
                            fw.op("pe", lambda e, kc=kc, bi=bi, j=j, wt=wt: e.matmul(
                                PS[bi][:, :], wt[:, kc, :], hT[:, kc, j * 512:(j + 1) * 512],
                                start=(kc == 0), stop=(kc == KC - 1)),
                                reads=[bw, bH[j]], writes=[bPS[bi]], signal=(kc == KC - 1), partial=(kc > 0))
                    evac2(j, ba, bb)

            def rot_evac(dstT, bD):
                def f(j, ba, bb):
                    sl = slice(j * 512, (j + 1) * 512)
                    TT = lambda o, a, b_, op_, rd, wr, part=False: fw.op(
                        "dve", lambda e: e.tensor_tensor(out=o, in0=a, in1=b_, op=op_), reads=rd, writes=wr, partial=part)
                    TT(tmps[0][:], PS[ba][:, :], rcos[:, sl], ALU.mult, [bPS[ba], b_c3], [bT[0]])
                    TT(tmps[1][:], PS[bb][:, :], rsin[:, sl], ALU.mult, [bPS[bb], b_c3], [bT[1]])
                    TT(tmps[2][:], PS[bb][:, :], rcos[:, sl], ALU.mult, [bPS[bb], b_c3], [bT[2]])
                    TT(tmps[3][:], PS[ba][:, :], rsin[:, sl], ALU.mult, [bPS[ba], b_c3], [bT[3]])
                    TT(dstT[:, 0, sl], tmps[0][:], tmps[1][:], ALU.subtract, [bT[0], bT[1]], [bD], part=True)
                    TT(dstT[:, 1, sl], tmps[2][:], tmps[3][:], ALU.add, [bT[2], bT[3]], [bD], part=True)
                return f

            def copy_evac(dstT, bD, func):
                def f(j, ba, bb):
                    sl = slice(j * 512, (j + 1) * 512)
                    for c, bi in ((0, ba), (1, bb)):
                        fw.op("act", lambda e, c=c, bi=bi: e.activation(out=dstT[:, c, sl], in_=PS[bi][:, :], func=func),
                              reads=[bPS[bi]], writes=[bD], partial=True)
                return f

            def to_tok(srcT, bS, dst, bDst, h, scaled):
                for g in range(4):
                    bi = next_bank(0, 4)
                    pv = PS[bi][:].bitcast(BF16)
                    for t4 in range(4):
                        t = g * 4 + t4
                        for c in range(2):
                            k8 = t4 * 2 + c
                            fw.op("pe", lambda e, t=t, c=c, k8=k8, pv=pv: e.transpose(
                                pv[:, k8 * 128:(k8 + 1) * 128], srcT[:, c, t * 128:(t + 1) * 128], ident[:]),
                                reads=[bS, b_const], writes=[bPS[bi]], signal=(k8 == 7), partial=(k8 > 0))
                    if scaled:
                        fw.op("act", lambda e, g=g, pv=pv: e.activation(
                            out=dst[:, g * 4:(g + 1) * 4, :], in_=pv.rearrange("p (t n) -> p t n", n=256),
                            func=AF.Copy, scale=rzeta[:, h:h + 1]),
                            reads=[bPS[bi], b_c3], writes=[bDst], partial=(g > 0))
                    else:
                        fw.op("dve", lambda e, g=g, pv=pv: e.tensor_copy(
                            out=dst[:, g * 4:(g + 1) * 4, :], in_=pv.rearrange("p (t n) -> p t n", n=256)),
                            reads=[bPS[bi]], writes=[bDst], partial=(g > 0))

            for h in range(4 if stage >= 3 else 0):
                gC = float(gch[h])
                proj_pair(8 * h + 0, rot_evac(rqT, b_rq))
                proj_pair(8 * h + 2, rot_evac(rkT, b_rk))
                proj_pair(8 * h + 4, copy_evac(rvT, b_rv, AF.Copy))
                proj_pair(8 * h + 6, copy_evac(rgs, b_rgs, AF.Silu))
                to_tok(rkT, b_rk, kTok, b_kTok, h, True)
                to_tok(rvT, b_rv, rvtok, b_rvtok, h, False)

                def emitA(n):
                    st = n % 2
                    cs = slice(n * 128, (n + 1) * 128)
                    for c in range(2):
                        mm(PS[st][:, 0:128], rkT[:, c, cs], rqT[:, c, cs], c == 0, c == 1,
                           [b_rk, b_rq], bPS[st], c == 1, c > 0)
                    fw.op("dve", lambda e: e.tensor_tensor(out=PT[st][:], in0=PS[st][:, 0:128], in1=rdt[:, h, :],
                                                           op=ALU.mult),
                          reads=[bPS[st], b_c3], writes=[bPT[st]])
                    if n < NT - 1:
                        for c in range(2):
                            mm(PS[2 + st][:, c * 256:(c + 1) * 256], kTok[:, n, c * 128:(c + 1) * 128], rvtok[:, n, :],
                               True, True, [b_kTok, b_rvtok], bPS[2 + st], c == 1, c > 0)

                def emitB(n):
                    st = n % 2
                    ob = 4 + st
                    cs = slice(n * 128, (n + 1) * 128)
                    if n < NT - 1:
                        if n == 0:
                            fw.op("dve", lambda e: e.tensor_copy(out=Wst[:], in_=PS[2 + st][:, :]),
                                  reads=[bPS[2 + st]], writes=[b_W])
                        else:
                            fw.op("dve", lambda e: e.scalar_tensor_tensor(
                                out=Wst[:], in0=Wst[:], scalar=gC, in1=PS[2 + st][:, :], op0=ALU.mult, op1=ALU.add),
                                reads=[b_W, bPS[2 + st]], writes=[b_W])
                        fw.op("act", lambda e: e.activation(out=Wb[(n + 1) % 2][:], in_=Wst[:], func=AF.Copy),
                              reads=[b_W], writes=[bWb[(n + 1) % 2]])
                    for ec in range(2):
                        reg = PS[ob][:, ec * 128:(ec + 1) * 128]
                        last_inner = (n == 0)
                        mm(reg, rvtok[:, n, ec * 128:(ec + 1) * 128], PT[st][:], True, last_inner,
                           [b_rvtok, bPT[st]], bPS[ob], last_inner and ec == 1, ec > 0)
                        if n > 0:
                            for c in range(2):
                                mm(reg, Wb[st][:, c * 256 + ec * 128: c * 256 + (ec + 1) * 128], rqT[:, c, cs],
                                   False, c == 1, [bWb[st], b_rq], bPS[ob], c == 1 and ec == 1, True)
                    fw.op("act", lambda e: e.activation(out=oraw[:, :, cs],
                                                        in_=PS[ob][:, 0:256].rearrange("p (c n) -> p c n", n=128),
                                                        func=AF.Copy),
                          reads=[bPS[ob]], writes=[b_oraw], partial=(n > 0))
                emitA(0)
                for n in range(NT):
                    if n + 1 < NT:
                        emitA(n + 1)
                    emitB(n)
                fw.op("act", lambda e: e.activation(out=sq[:], in_=oraw[:], func=AF.Square),
                      reads=[b_oraw], writes=[b_sq])
                for j in range(4):
                    sl = slice(j * 512, (j + 1) * 512)
                    bi = next_bank(0, 4)
                    for c in range(2):
                        mm(PS[bi][:, :], ones[:], sq[:, c, sl], c == 0, c == 1, [b_const, b_sq], bPS[bi], c == 1, c > 0)
                    fw.op("dve", lambda e: e.scalar_tensor_tensor(out=nrm[:], in0=PS[bi][:, :], scalar=1.0 / 256.0,
                                                                  in1=repsq[:, h, :], op0=ALU.mult, op1=ALU.add),
                          reads=[bPS[bi], b_c3], writes=[b_nrm])
                    fw.op("act", lambda e: e.activation(out=nrm[:], in_=nrm[:], func=AF.Sqrt),
                          reads=[b_nrm], writes=[b_nrm])
                    fw.op("dve", lambda e: e.reciprocal(nrm[:], nrm[:]), reads=[b_nrm], writes=[b_nrm])
                    for ec in range(2):
                        ch = h * 2 + ec
                        fw.op("dve", lambda e, ec=ec, ch=ch: e.scalar_tensor_tensor(
                            out=tmps[ec][:], in0=oraw[:, ec, sl], scalar=retw[:, ch:ch + 1], in1=nrm[:],
                            op0=ALU.mult, op1=ALU.mult),
                            reads=[b_oraw, b_c3, b_nrm], writes=[bT[ec]])
                        fw.op("dve", lambda e, ec=ec: e.tensor_tensor(out=orT[:, ec, sl], in0=tmps[ec][:],
                                                                      in1=rgs[:, ec, sl], op=ALU.mult),
                              reads=[bT[ec], b_rgs], writes=[b_orT], partial=(j > 0 or ec > 0))
                for ec in range(2):
                    ch = h * 2 + ec
                    fw.dma("sp", OR[ch * 128:(ch + 1) * 128, :], orT[:, ec, :], reads=[b_orT], writes=[b_OR], partial=True)
                    if debug:
                        fw.dma("sp", d_or[ch * 128:(ch + 1) * 128, :], orT[:, ec, :], reads=[b_orT], writes=[b_dor],
                               partial=True)
            fw.barrier()

        if debug and stage == 3:
            fw.wait_buf("sp", b_dor)
            return nc, dram, dbg


        if debug and stage == 4:
            b_dmg = fw.buf("d_mg")
        b_MG = fw.bufs_n("scr_mg", 2)
        with ExitStack() as p4:
            orA = fw.sb(p4, "g_orA", [128, 8, S], BF16)
            omA = fw.sb(p4, "g_omA", [128, 8, S], BF16)
            b_orA, b_omA = fw.buf("g_orA"), fw.buf("g_omA")
            fw.dma("sp", omA[:], OM.rearrange("(c p) s -> p c s", p=128), reads=b_OM, writes=[b_omA])
            fw.dma("sp", orA[:], OR.rearrange("(c p) s -> p c s", p=128), reads=[b_OR], writes=[b_orA])
            sg = [fw.sb(p4, f"g_sg{i}", [128, 512], F32) for i in range(4)]
            bSg = fw.bufs_n("g_sg", 4)
            tt_ = [fw.sb(p4, f"g_tt{i}", [128, 512], F32) for i in range(4)]
            bTt = fw.bufs_n("g_tt", 4)
            mgc = [fw.sb(p4, f"g_mgc{i}", [128, S], BF16) for i in range(2)]
            bMgc = fw.bufs_n("g_mgc", 2)
            class _G:
                def get(self, i):
                    return AWS.get(GBASE + i, group=4, base=GBASE + (i // 4) * 4)
            gws = _G()
            it = 0
            srcs = ((hT, None, KC), (hT, None, KC), (orA, b_orA, 8), (omA, b_omA, 8))

            def grp(ws_, i, j, bi):
                aT, bA, nk = srcs[i]
                bA = bH[j] if bA is None else bA
                wt, bw = ws_[i]
                sl = slice(j * 512, (j + 1) * 512)
                for kc in range(nk):
                    mm(PS[bi][:, :], wt[:, kc, :], aT[:, kc, sl], kc == 0, kc == nk - 1,
                       [bw, bA], bPS[bi], kc == nk - 1, kc > 0)

            def combine(mc, bmc, j, bg0, bg1, bu0, bu1, k2):
                sl = slice(j * 512, (j + 1) * 512)
                for i, bg in enumerate((bg0, bg1)):
                    fw.op("act", lambda e, i=i, bg=bg: e.activation(out=sg[k2 + i][:], in_=PS[bg][:, :], func=AF.Sigmoid),
                          reads=[bPS[bg]], writes=[bSg[k2 + i]])
                for i, bu in enumerate((bu0, bu1)):
                    fw.op("dve", lambda e, i=i, bu=bu: e.tensor_tensor(out=tt_[k2 + i][:], in0=PS[bu][:, :],
                                                                       in1=sg[k2 + i][:], op=ALU.mult),
                          reads=[bPS[bu], bSg[k2 + i]], writes=[bTt[k2 + i]])
                fw.op("dve", lambda e: e.tensor_tensor(out=mc[:, sl], in0=tt_[k2][:], in1=tt_[k2 + 1][:], op=ALU.add),
                      reads=[bTt[k2], bTt[k2 + 1]], writes=[bmc], partial=(j > 0))
            for c in range(KC if stage >= 4 else 0):
                ws_ = [gws.get(4 * c + i) for i in range(4)]
                mc = mgc[c % 2]
                if c == 0:
                    for j in range(4):
                        grp(ws_, 0, j, 2 * j)
                        grp(ws_, 1, j, 2 * j + 1)
                    sgx = [fw.sb(p4, f"g_sgx{i}", [128, 512], F32) for i in range(8)]
                    bSgx = fw.bufs_n("g_sgx", 8)
                    for b_ in range(8):
                        fw.op("act", lambda e, b_=b_: e.activation(out=sgx[b_][:], in_=PS[b_][:, :], func=AF.Sigmoid),
                              reads=[bPS[b_]], writes=[bSgx[b_]])
                    for j in range(4):
                        sl = slice(j * 512, (j + 1) * 512)
                        bu0, bu1 = 2 * (j % 2), 2 * (j % 2) + 1
                        grp(ws_, 2, j, bu0)
                        grp(ws_, 3, j, bu1)
                        k2 = 2 * (j % 2)
                        for i, bu in enumerate((bu0, bu1)):
                            fw.op("dve", lambda e, i=i, bu=bu: e.tensor_tensor(
                                out=tt_[k2 + i][:], in0=PS[bu][:, :], in1=sgx[2 * j + i][:], op=ALU.mult),
                                reads=[bPS[bu], bSgx[2 * j + i]], writes=[bTt[k2 + i]])
                        fw.op("dve", lambda e: e.tensor_tensor(out=mc[:, sl], in0=tt_[k2][:], in1=tt_[k2 + 1][:],
                                                               op=ALU.add),
                              reads=[bTt[k2], bTt[k2 + 1]], writes=[bMgc[c % 2]], partial=(j > 0))
                else:
                    for j in range(4):
                        base = 4 * (it % 2)
                        it += 1
                        for i in range(4):
                            grp(ws_, i, j, base + i)
                        combine(mc, bMgc[c % 2], j, base, base + 1, base + 2, base + 3, 2 * (it % 2))
                fw.dma("sp", MG[:, :, c * 128:(c + 1) * 128].rearrange("t p n -> p t n"),
                       mc[:, :].rearrange("p (t n) -> p t n", n=128), reads=[bMgc[c % 2]], writes=[b_MG[c % 2]], partial=True)
                if debug and stage == 4:
                    pass
            fw.barrier()
        sh.close()
        fspecs = []
        if w_ffn_up is not None and stage >= 6:
            for c in range(NFF):
                fspecs += [(w_ffn_up, c * 128, KC), (w_ffn_up, DFF + c * 128, KC)]
        fws = WStream(fspecs)

        if debug and stage == 4:
            with ExitStack() as pd:
                dt_ = fw.sb(pd, "dbg_t", [128, KC * 128], BF16)
                b_dt = fw.buf("dbg_t")
                d_mg2 = dout("d_mg2", [NT, 128, KC * 128], BF16)
                for t in range(NT):
                    fw.dma("sp", dt_[:], MG[t], reads=b_MG, writes=[b_dt])
                    fw.dma("sp", d_mg2[t], dt_[:], reads=[b_dt], writes=[b_dmg], partial=True)
                fw.wait_buf("sp", b_dmg)
            return nc, dram, dbg


        sh2 = ExitStack()
        top.callback(sh2.close)
        h2T = fw.sb(sh2, "h2T", [128, KC, S], BF16)
        bH2 = fw.bufs_n("h2T", 4)
        with ExitStack() as p5:
            wout = fw.sb(p5, "o_wout", [128, KC, D], BF16)
            bWo = fw.bufs_n("o_wout", 4)
            for cg in range(4):
                fw.dma("pool", wout[:, :, cg * 512:(cg + 1) * 512],
                       w_out[:, cg * 512:(cg + 1) * 512].rearrange("(kc p) n -> p kc n", p=128), writes=[bWo[cg]])
            if fspecs:
                fws.get(0)
            wbc2 = fw.sb(p5, "o_wbc", [128, D], F32)
            b_wbc2 = fw.buf("o_wbc")
            fw.dma("sp", wbc2[:], ffn_nw.partition_broadcast(128), writes=[b_wbc2])
            mgt = [fw.sb(p5, f"o_mgt{i}", [128, KC * 128], BF16) for i in range(2)]
            bMgt = fw.bufs_n("o_mgt", 2)
            xb5 = [fw.sb(p5, f"o_xb{i}", [128, D], F32) for i in range(2)]
            bXb5 = fw.bufs_n("o_xb", 2)
            xn5 = fw.sb(p5, "o_xn", [128, D], BF16)
            b_xn5 = fw.buf("o_xn")
            ss5 = fw.sb(p5, "o_ss", [128, NT], F32)
            sd5 = fw.sb(p5, "o_sd", [128, NT], F32)
            rstd5 = fw.sb(p5, "o_rstd", [128, NT], F32)
            bSt5 = fw.bufs_n("o_st", NT)
            b_X2p = fw.bufs_n("scr_x2_", 2)

            def loads5(t):
                fw.dma("sp", mgt[t % 2][:], MG[t], reads=b_MG, writes=[bMgt[t % 2]])
                fw.dma("sp", xb5[t % 2][:], x[t * 128:(t + 1) * 128, :], writes=[bXb5[t % 2]])
            NT5 = NT if stage >= 5 else 0

            def mm5(t):
                s_ = t % 2
                for cg in range(4):
                    bi = cg
                    for kc in range(KC):
                        mm(PS[bi][:, :], mgt[s_][:, kc * 128:(kc + 1) * 128], wout[:, kc, cg * 512:(cg + 1) * 512],
                           kc == 0, kc == KC - 1, [bMgt[s_], bWo[cg]], bPS[bi], kc == KC - 1, kc > 0)

            def add5(t):
                s_ = t % 2
                x2t, b_x2t = xb5[s_], bXb5[s_]
                for cg in range(4):
                    bi = cg
                    fw.op("dve", lambda e, cg=cg, bi=bi: e.tensor_tensor(
                        out=x2t[:, cg * 512:(cg + 1) * 512], in0=PS[bi][:, :], in1=x2t[:, cg * 512:(cg + 1) * 512],
                        op=ALU.add), reads=[bPS[bi], b_x2t], writes=[b_x2t])
                fw.dma("sp", X2[t * 128:(t + 1) * 128, :], x2t[:], reads=[b_x2t], writes=[b_X2p[s_]], partial=True)

            def norm5(t):
                s_ = t % 2
                x2t, b_x2t = xb5[s_], bXb5[s_]
                rms_tile("p5", x2t[:], b_x2t, ss5, sd5, rstd5, t, bSt5[t], xn5, b_xn5)
                fw.op("dve", lambda e: e.scalar_tensor_tensor(
                    out=xn5[:], in0=x2t[:], scalar=rstd5[:, t:t + 1], in1=wbc2[:], op0=ALU.mult, op1=ALU.mult),
                    reads=[b_x2t, bSt5[t], b_wbc2], writes=[b_xn5])
                ba, bb = (4, 5) if s_ == 0 else (6, 7)
                for half, bi in ((0, ba), (1, bb)):
                    pv = PS[bi][:].bitcast(BF16)
                    for c8 in range(8):
                        c = half * 8 + c8
                        fw.op("pe", lambda e, c=c, c8=c8, pv=pv: e.transpose(
                            pv[:, c8 * 128:(c8 + 1) * 128], xn5[:, c * 128:(c + 1) * 128], ident[:]),
                            reads=[b_xn5, b_const], writes=[bPS[bi]], signal=(c8 == 7), partial=(c8 > 0))
                    fw.op("act", lambda e, half=half, pv=pv: e.activation(
                        out=h2T[:, half * 8:(half + 1) * 8, t * 128:(t + 1) * 128],
                        in_=pv.rearrange("p (c n) -> p c n", n=128), func=AF.Copy),
                        reads=[bPS[bi]], writes=[bH2[t // 4]], partial=True)
            if NT5:
                loads5(0)
                loads5(1)
                mm5(0)
                add5(0)
            for t in range(NT5):
                if t + 1 < NT5:
                    mm5(t + 1)
                norm5(t)
                if t + 2 < NT5:
                    loads5(t + 2)
                if t + 1 < NT5:
                    add5(t + 1)
            fw.barrier()

        if debug and stage == 5:
            d_x2 = dout("d_x2", [S, D], F32)
            d_h2T = dout("d_h2T", [D, S], BF16)
            b_d5 = fw.buf("d_5")
            fw.dma("sp", d_h2T.rearrange("(c p) s -> p c s", p=128), h2T[:], reads=bH2, writes=[b_d5], partial=True)
            with ExitStack() as pd:
                dt_ = fw.sb(pd, "dbg_t5", [128, D], F32)
                b_dt = fw.buf("dbg_t5")
                for t in range(NT):
                    fw.dma("sp", dt_[:], X2[t * 128:(t + 1) * 128, :], reads=b_X2p, writes=[b_dt])
                    fw.dma("sp", d_x2[t * 128:(t + 1) * 128, :], dt_[:], reads=[b_dt], writes=[b_d5], partial=True)
            fw.wait_buf("sp", b_d5)
            return nc, dram, dbg

        b_GT = fw.bufs_n("scr_gt", 2)
        with ExitStack() as p6:
            convw = fw.sb(p6, "f_convw", [128, 3 * 88], F32)
            convb = fw.sb(p6, "f_convb", [128, 88], F32)
            b_c6 = fw.buf("f_consts")
            fw.dma("sp", convw[:], dram["convw_t"], writes=[b_c6], partial=True)
            fw.dma("sp", convb[:], dram["convb_t"], writes=[b_c6], partial=True)
            ua = [[fw.sb(p6, f"f_u{ab}{i}", [128, S + 2], F32) for i in range(2)] for ab in range(2)]
            ya = [[fw.sb(p6, f"f_y{ab}{i}", [128, S], F32) for i in range(2)] for ab in range(2)]
            bU = [fw.bufs_n(f"f_u{ab}", 2) for ab in range(2)]
            bY = [fw.bufs_n(f"f_y{ab}", 2) for ab in range(2)]
            gt = [fw.sb(p6, f"f_g{i}", [128, S], BF16) for i in range(2)]
            bG = fw.bufs_n("f_g", 2)
            for ab in range(2):
                for i in range(2):
                    fw.op("dve", lambda e, ab=ab, i=i: e.memset(ua[ab][i][:, 0:2], 0.0), writes=[bU[ab][i]])
            for c in range(NFF if stage >= 6 else 0):
                s_ = c % 2
                for ab in range(2):
                    wt, bw = fws.get(2 * c + ab)
                    chn = ab * NFF + c
                    u_, y_ = ua[ab][s_], ya[ab][s_]
                    bu_, by_ = bU[ab][s_], bY[ab][s_]

                    def evac(j, ps, bps, u_=u_, y_=y_, bu_=bu_, by_=by_, chn=chn):
                        fw.op("act", lambda e: e.activation(out=u_[:, 2 + j * 512: 2 + (j + 1) * 512], in_=ps[:, :],
                                                            func=AF.Copy),
                              reads=[bps], writes=[bu_], partial=True)
                        fw.op("act", lambda e: e.activation(out=y_[:, j * 512:(j + 1) * 512], in_=ps[:, :],
                                                            func=AF.Identity, scale=convw[:, 2 * 88 + chn: 2 * 88 + chn + 1],
                                                            bias=convb[:, chn:chn + 1]),
                              reads=[bps, b_c6], writes=[by_], partial=(j > 0))
                    proj_chunk(wt, bw, evac, act_T=h2T, b_act=bH2, banks=(0, 8))
                    for tap in (1, 0):
                        sh_ = 2 - tap
                        fw.op("dve", lambda e, u_=u_, y_=y_, tap=tap, sh_=sh_, chn=chn: e.scalar_tensor_tensor(
                            out=y_[:], in0=u_[:, 2 - sh_: 2 - sh_ + S], scalar=convw[:, tap * 88 + chn: tap * 88 + chn + 1],
                            in1=y_[:], op0=ALU.mult, op1=ALU.add),
                            reads=[bu_, by_, b_c6], writes=[by_])
                fw.op("act", lambda e: e.activation(out=ya[0][s_][:], in_=ya[0][s_][:], func=AF.Silu),
                      reads=[bY[0][s_]], writes=[bY[0][s_]])
                fw.op("dve", lambda e: e.tensor_tensor(out=gt[s_][:], in0=ya[0][s_][:], in1=ya[1][s_][:], op=ALU.mult),
                      reads=[bY[0][s_], bY[1][s_]], writes=[bG[s_]])
                fw.dma("sp", GT[:, :, c * 128:(c + 1) * 128].rearrange("t p n -> p t n"),
                       gt[s_][:, :].rearrange("p (t n) -> p t n", n=128), reads=[bG[s_]], writes=[b_GT[s_]], partial=True)
            fw.barrier()
        sh2.close()

        if debug and stage == 6:
            b_d6 = fw.buf("d_6")
            d_gt2 = dout("d_gt2", [NT, 128, NFF * 128], BF16)
            with ExitStack() as pd:
                dt_ = fw.sb(pd, "dbg_t6", [128, NFF * 128], BF16)
                b_dt = fw.buf("dbg_t6")
                for t in range(NT):
                    fw.dma("sp", dt_[:], GT[t], reads=b_GT, writes=[b_dt])
                    fw.dma("sp", d_gt2[t], dt_[:], reads=[b_dt], writes=[b_d6], partial=True)
            fw.wait_buf("sp", b_d6)
            return nc, dram, dbg

        b_X3 = fw.bufs_n("scr_x3", 2)
        b_out = fw.bufs_n("out", 2)
        with ExitStack() as p7:
            wd = [fw.sb(p7, f"d_wd{i}", [128, NFF, 512], BF16) for i in range(2)]
            bWd = [[fw.buf(f"d_wd{i}_")] * NFF for i in range(2)]
            gtt = [fw.sb(p7, f"d_gtt{i}", [128, NFF * 128], BF16) for i in range(3)]
            bGtt = fw.bufs_n("d_gtt", 3)
            x2q = [fw.sb(p7, f"d_x2q{i}", [128, 512], F32) for i in range(3)]
            bX2q = fw.bufs_n("d_x2q", 3)
            x3q = [fw.sb(p7, f"d_x3q{i}", [128, 512], F32) for i in range(2)]
            bX3q = fw.bufs_n("d_x3q", 2)
            wbc3 = fw.sb(p7, "e_wbc", [128, D], F32)
            b_wbc3 = fw.buf("e_wbc")
            x3t = [fw.sb(p7, f"e_x3t{i}", [128, D], F32) for i in range(2)]
            bX3t = fw.bufs_n("e_x3t", 2)
            junk8 = fw.sb(p7, "e_junk", [128, D], BF16)
            b_junk8 = fw.buf("e_junk")
            ss8 = fw.sb(p7, "e_ss", [128, NT], F32)
            sd8 = fw.sb(p7, "e_sd", [128, NT], F32)
            rstd8 = fw.sb(p7, "e_rstd", [128, NT], F32)
            bSt8 = fw.bufs_n("e_st", NT)

            def load_wd(q):
                for hk in range(2):
                    k0, k1 = hk * 22, (hk + 1) * 22
                    fw.dma("pool", wd[q % 2][:, k0:k1, :],
                           w_ffn_down[k0 * 128:k1 * 128, q * 512:(q + 1) * 512].rearrange("(kc p) n -> p kc n", p=128),
                           writes=[bWd[q % 2][k0]], partial=(hk > 0))
            seq7 = [(q, t) for q in range(4) for t in range(NT)] if stage >= 7 else []

            def loads7(k):
                q, t = seq7[k]
                fw.dma("sp", gtt[k % 3][:], GT[t], reads=b_GT, writes=[bGtt[k % 3]])
                fw.dma("sp", x2q[k % 3][:], X2[t * 128:(t + 1) * 128, q * 512:(q + 1) * 512], reads=b_X2p,
                       writes=[bX2q[k % 3]])

            def load3q(t):
                s_ = t % 2
                fw.dma("act", x3t[s_][:, 0:1536], X3[t * 128:(t + 1) * 128, 0:1536], reads=b_X3, writes=[bX3t[s_]])

            def final8a(t):
                s_ = t % 2
                fw.op("act", lambda e: e.activation(out=junk8[:], in_=x3t[s_][:], func=AF.Square,
                                                    accum_out=ss8[:, t:t + 1]),
                      reads=[bX3t[s_]], writes=[b_junk8, bSt8[t]])
                fw.op("act", lambda e: e.activation(out=sd8[:, t:t + 1], in_=ss8[:, t:t + 1], func=AF.Sqrt,
                                                    scale=1.0 / D, bias=epst[:, 0:1]),
                      reads=[bSt8[t], b_eps], writes=[bSt8[t]])

            def final8b(t):
                s_ = t % 2
                fw.op("dve", lambda e: e.reciprocal(rstd8[:, t:t + 1], sd8[:, t:t + 1]),
                      reads=[bSt8[t]], writes=[bSt8[t]])
                fw.op("dve", lambda e: e.scalar_tensor_tensor(
                    out=x3t[s_][:], in0=x3t[s_][:], scalar=rstd8[:, t:t + 1], in1=wbc3[:], op0=ALU.mult, op1=ALU.mult),
                    reads=[bX3t[s_], bSt8[t], b_wbc3], writes=[bX3t[s_]])
                fw.dma("act", out[t * 128:(t + 1) * 128, :], x3t[s_][:], reads=[bX3t[s_]], writes=[b_out[s_]], partial=True)
            if seq7:
                fw.dma("sp", wbc3[:], fin_nw.partition_broadcast(128), writes=[b_wbc3])
                load_wd(0)
                loads7(0)
                loads7(1)
            for k, (q, t) in enumerate(seq7):
                if t == 0 and q + 1 < 4:
                    load_wd(q + 1)
                if k + 2 < len(seq7):
                    loads7(k + 2)
                s3, s_ = k % 3, k % 2
                bi = next_bank(0, 8)
                for kc in range(NFF):
                    mm(PS[bi][:, :], gtt[s3][:, kc * 128:(kc + 1) * 128], wd[q % 2][:, kc, :],
                       kc == 0, kc == NFF - 1, [bGtt[s3], bWd[q % 2][kc]], bPS[bi], kc == NFF - 1, kc > 0)
                if q < 3:
                    fw.op("dve", lambda e, s_=s_, s3=s3, bi=bi: e.tensor_tensor(out=x3q[s_][:], in0=PS[bi][:, :],
                                                                                in1=x2q[s3][:], op=ALU.add),
                          reads=[bPS[bi], bX2q[s3]], writes=[bX3q[s_]])
                    fw.dma("sp", X3[t * 128:(t + 1) * 128, q * 512:(q + 1) * 512], x3q[s_][:], reads=[bX3q[s_]],
                           writes=[b_X3[s_]], partial=True)
                else:
                    if t == 0:
                        load3q(0)
                    if t > 0:
                        final8b(t - 1)
                    if t + 1 < NT:
                        load3q(t + 1)
                    ts_ = t % 2
                    fw.op("dve", lambda e, ts_=ts_, s3=s3, bi=bi: e.tensor_tensor(
                        out=x3t[ts_][:, 1536:2048], in0=PS[bi][:, :], in1=x2q[s3][:], op=ALU.add),
                        reads=[bPS[bi], bX2q[s3], bX3t[ts_]], writes=[bX3t[ts_]], partial=True)
                    final8a(t)
            if seq7:
                final8b(NT - 1)
                fw.wait_buf("sp", b_out[0])
                fw.wait_buf("sp", b_out[1])
            fw.barrier()
        LASTFW[0] = fw

    return nc, dram, dbg


def _shared_inputs(inputs):
    consts, _ = _get_consts()
    f = lambda a: np.ascontiguousarray(np.asarray(a, dtype=np.float32))
    m = {
        "w_in": f(inputs["w_in"][0]),
        "w_ret_up": f(inputs["w_ret_up"][0]),
        "w_moba_up": f(inputs["w_moba_up"][0]),
        "w_out": f(inputs["w_out"][0]),
        "w_ffn_up": f(inputs["w_ffn_up"][0]),
        "w_ffn_down": f(inputs["w_ffn_down"][0]),
        "attn_norm_w": f(inputs["attn_norm_w"][0]).reshape(1, D),
        "ffn_norm_w": f(inputs["ffn_norm_w"][0]).reshape(1, D),
        "final_norm_w": f(inputs["final_norm_w"]).reshape(1, D),
        "retw_t": f(np.asarray(inputs["ret_norm_w"][0]).reshape(8, 128).T),
        "convw_t": f(np.asarray(inputs["conv_w"][0]).reshape(3, 88, 128).transpose(2, 0, 1).reshape(128, 264)),
        "convb_t": f(np.asarray(inputs["conv_b"][0]).reshape(88, 128).T),
    }
    m.update(consts)
    return m


def kernel(**inputs):
    nc, dram, dbg = build_nc()
    shared = _shared_inputs(inputs)
    xs = np.asarray(inputs["x"], dtype=np.float32)
    in_maps = []
    for b in range(8):
        mm = dict(shared)
        mm["x"] = np.ascontiguousarray(xs[b])
        in_maps.append({k: mm[k] for k in dram})
    res = run_bass_kernel_spmd(nc, in_maps, core_ids=list(range(8)))
    return np.stack([np.asarray(r["out"], dtype=np.float32) for r in res.results], axis=0)
```
